# Optimizing a Trainium2 kernel written in Bass

```python
import math
import jax, jax.numpy as jnp
from jax import lax
import numpy as np

D_MODEL = 1024
BATCH = 2
SEQ = 16384
DEPTH = 1
DEC_BATCH = 16
DEC_SEQ = 4096
PAST_LEN = 128

A_HEADS = 8
A_HEAD_DIM = 64
A_V_DIM = 2 * A_HEAD_DIM
A_QK = 2 * A_HEADS * A_HEAD_DIM
A_WIDTH = A_HEADS * A_V_DIM
Q_BLOCK = 128
B_GROUPS = 8
B_WIDTH = 1024
B_GROUP_DIM = B_WIDTH // B_GROUPS
CHUNK = 128
REL_BUCKETS = 32
REL_MAX_DIST = 128
EPS = 1e-6

IN_SIZES = (A_QK, A_QK, A_WIDTH, A_WIDTH, B_WIDTH, B_WIDTH, B_WIDTH, D_MODEL, D_MODEL)
IN_COLS = sum(IN_SIZES)
IN_SPLITS = tuple(int(s) for s in np.cumsum(IN_SIZES)[:-1])

kernel_name = "hybrid_diffattn_gmlp_gated_encoder"


def rms_norm(x, g):
    xf = x.astype(jnp.float32)
    y = xf * lax.rsqrt(jnp.mean(xf * xf, axis=-1, keepdims=True) + EPS) * g.astype(jnp.float32)
    return y.astype(x.dtype)


def layer_norm(x, g, b):
    xf = x.astype(jnp.float32)
    mu = jnp.mean(xf, axis=-1, keepdims=True)
    xc = xf - mu
    y = xc * lax.rsqrt(jnp.mean(xc * xc, axis=-1, keepdims=True) + EPS)
    return (y * g.astype(jnp.float32) + b.astype(jnp.float32)).astype(x.dtype)


def rel_bucket(rel):
    half = REL_BUCKETS // 2
    max_exact = half // 2
    n = jnp.abs(rel)
    nf = jnp.maximum(n, 1).astype(jnp.float32)
    large = max_exact + (jnp.log(nf / max_exact) / math.log(REL_MAX_DIST / max_exact)
                         * (half - max_exact)).astype(jnp.int32)
    large = jnp.minimum(large, half - 1)
    return jnp.where(rel > 0, half, 0) + jnp.where(n < max_exact, n, large)


def diff_attention(q1, q2, k1, k2, v, lam, rel_bias):
    B, S, H, d = q1.shape
    nblk = S // Q_BLOCK
    scale = d ** -0.5
    kpos = jnp.arange(S)
    table = rel_bias.astype(jnp.float32)

    def to_blocks(t):
        return t.reshape(B, nblk, Q_BLOCK, H, d).swapaxes(0, 1)

    def one_block(args):
        i, a1, a2 = args
        qpos = i * Q_BLOCK + jnp.arange(Q_BLOCK)
        bias = table[rel_bucket(kpos[None, :] - qpos[:, None])].transpose(2, 0, 1)
        s1 = jnp.einsum('bqhd,bkhd->bhqk', a1, k1, preferred_element_type=jnp.float32) * scale + bias
        s2 = jnp.einsum('bqhd,bkhd->bhqk', a2, k2, preferred_element_type=jnp.float32) * scale + bias
        p = jax.nn.softmax(s1, axis=-1) - lam * jax.nn.softmax(s2, axis=-1)
        return jnp.einsum('bhqk,bkhe->bqhe', p.astype(v.dtype), v)

    o = lax.map(one_block, (jnp.arange(nblk), to_blocks(q1), to_blocks(q2)))
    return o.swapaxes(0, 1).reshape(B, S, H, v.shape[-1])


def spatial_gate(u, v, ln_g, ln_b, w_s, b_s):
    B, S, _ = v.shape
    vn = layer_norm(v, ln_g, ln_b).reshape(B, S // CHUNK, CHUNK, B_GROUPS, B_GROUP_DIM)
    s = jnp.einsum('gpq,bcqgk->bcpgk', w_s, vn) + b_s.T[None, None, :, :, None]
    return u * s.reshape(B, S, B_WIDTH)


def encoder_layer(x, layer_idx, g_pre, w_in, lambda_q1, lambda_k1, lambda_q2, lambda_k2, subln_g,
                  w_pa, ln_g, ln_b, w_s, b_s, w_pb, w_o, g_post, rel_bias):
    B, S, _ = x.shape
    lambda_init = 0.8 - 0.6 * math.exp(-0.3 * layer_idx)
    h = rms_norm(x, g_pre)
    z = h @ w_in
    q, k, v_a, gate_a, u_b, v_b, gate_b, m_a, m_b = jnp.split(z, IN_SPLITS, axis=-1)
    q = q.reshape(B, S, 2, A_HEADS, A_HEAD_DIM)
    k = k.reshape(B, S, 2, A_HEADS, A_HEAD_DIM)
    v_a = v_a.reshape(B, S, A_HEADS, A_V_DIM)
    lam = (jnp.exp(jnp.sum(lambda_q1.astype(jnp.float32) * lambda_k1.astype(jnp.float32)))
           - jnp.exp(jnp.sum(lambda_q2.astype(jnp.float32) * lambda_k2.astype(jnp.float32)))
           + lambda_init)
    o = diff_attention(q[:, :, 0], q[:, :, 1], k[:, :, 0], k[:, :, 1], v_a, lam, rel_bias)
    o = rms_norm(o, subln_g) * (1.0 - lambda_init)
    y_a = (o.reshape(B, S, A_WIDTH) * jax.nn.silu(gate_a)) @ w_pa
    y_b = (spatial_gate(u_b, v_b, ln_g, ln_b, w_s, b_s) * jax.nn.silu(gate_b)) @ w_pb
    merged = jax.nn.sigmoid(m_a) * y_a + jax.nn.sigmoid(m_b) * y_b
    out = merged @ w_o
    return x + rms_norm(out, g_post)


def setup_inputs(seed: int = 0) -> dict:
    key = jax.random.key(seed)
    ks = jax.random.split(key, 20)
    f32 = jnp.float32
    nrm = lambda k, shape, s: jax.random.normal(k, shape, f32) * s
    return {
        "x_prompt": nrm(ks[0], (BATCH, SEQ, D_MODEL), 1.0),
        "x_sample": nrm(ks[1], (DEC_BATCH, DEC_SEQ, D_MODEL), 1.0),
        "g_pre": 1.0 + nrm(ks[2], (DEPTH, D_MODEL), 0.02),
        "w_in": nrm(ks[3], (DEPTH, D_MODEL, IN_COLS), D_MODEL ** -0.5),
        "lambda_q1": nrm(ks[4], (DEPTH, A_HEAD_DIM), 0.1),
        "lambda_k1": nrm(ks[5], (DEPTH, A_HEAD_DIM), 0.1),
        "lambda_q2": nrm(ks[6], (DEPTH, A_HEAD_DIM), 0.1),
        "lambda_k2": nrm(ks[7], (DEPTH, A_HEAD_DIM), 0.1),
        "subln_g": 1.0 + nrm(ks[8], (DEPTH, A_V_DIM), 0.02),
        "w_pa": nrm(ks[9], (DEPTH, A_WIDTH, D_MODEL), A_WIDTH ** -0.5),
        "ln_g": 1.0 + nrm(ks[10], (DEPTH, B_WIDTH), 0.02),
        "ln_b": nrm(ks[11], (DEPTH, B_WIDTH), 0.02),
        "w_s": nrm(ks[12], (DEPTH, B_GROUPS, CHUNK, CHUNK), CHUNK ** -0.5),
        "b_s": 1.0 + nrm(ks[13], (DEPTH, B_GROUPS, CHUNK), 0.1),
        "w_pb": nrm(ks[14], (DEPTH, B_WIDTH, D_MODEL), B_WIDTH ** -0.5),
        "w_o": nrm(ks[15], (DEPTH, D_MODEL, D_MODEL), D_MODEL ** -0.5),
        "g_post": 1.0 + nrm(ks[16], (DEPTH, D_MODEL), 0.02),
        "rel_bias": nrm(ks[17], (REL_BUCKETS, A_HEADS), 0.2),
    }


def reference(x_prompt, x_sample, g_pre, w_in, lambda_q1, lambda_k1, lambda_q2, lambda_k2, subln_g,
              w_pa, ln_g, ln_b, w_s, b_s, w_pb, w_o, g_post, rel_bias):
    y_prompt = x_prompt
    y_sample = x_sample
    for l in range(DEPTH):
        args = (g_pre[l], w_in[l], lambda_q1[l], lambda_k1[l], lambda_q2[l], lambda_k2[l], subln_g[l],
                w_pa[l], ln_g[l], ln_b[l], w_s[l], b_s[l], w_pb[l], w_o[l], g_post[l], rel_bias)
        y_prompt = encoder_layer(y_prompt, l, *args)
        y_sample = encoder_layer(y_sample, l, *args)
    return (y_prompt, y_sample)
```

```python
import contextlib
import math
import numpy as np
import concourse.bass as bass
import concourse.mybir as mybir
from concourse.bass_utils import run_bass_kernel_spmd

F32 = mybir.dt.float32
BF16 = mybir.dt.bfloat16
AF = mybir.ActivationFunctionType
ALU = mybir.AluOpType

D = 1024
NH = 8
S_S = 4096
S_P = 16384
EPS = 1e-6
LAMBDA_INIT = 0.8 - 0.6 * math.exp(-0.3 * 0)
SCALE = 0.125
GL = 1280
EPOCH = 20000


class Tok:
    __slots__ = ("eng", "sem", "val", "group")

    def __init__(self, eng, sem, val, group=None):
        self.eng, self.sem, self.val, self.group = eng, sem, val, group


class Op:
    __slots__ = ("fn", "deps", "tok", "is_dma")

    def __init__(self, fn, deps, tok, is_dma):
        self.fn, self.deps, self.tok, self.is_dma = fn, deps, tok, is_dma


class Sched:
    COMPUTE = ("pe", "act", "dve", "pool")
    ALL = ("pe", "act", "dve", "pool", "sp")
    ATTR = {"pe": "tensor", "act": "scalar", "dve": "vector", "pool": "gpsimd", "sp": "sync"}

    def __init__(self, nc, stack):
        self.nc, self.stack = nc, stack
        self.ops = {e: [] for e in self.ALL}
        self.count = {e: 0 for e in self.COMPUTE}
        self.esems = {e: [] for e in self.COMPUTE}
        self.res = {}
        self.dsem = {}
        self.groups = {}
        self.nsem = 0
        self.waited = {e: {} for e in self.ALL}
        self.group_final = set()

    def _newsem(self, name):
        self.nsem += 1
        return self.stack.enter_context(self.nc.semaphore(f"s{self.nsem}_{name}"))

    def _deps(self, reads, writes):
        deps = []
        for r in reads:
            e = self.res.get(r)
            if e and e[0] is not None:
                deps.append(e[0])
        for w in writes:
            e = self.res.get(w)
            if e:
                if e[0] is not None:
                    deps.append(e[0])
                deps.extend(e[1])
        return deps

    def _commit(self, tok, reads, writes):
        for r in reads:
            e = self.res.setdefault(r, [None, []])
            e[1].append(tok)
            if len(e[1]) > 64:
                del e[1][:32]
        for w in writes:
            self.res[w] = [tok, []]

    def op(self, eng, fn, reads=(), writes=()):
        deps = self._deps(reads, writes)
        n = self.count[eng]
        ep = n // EPOCH
        if ep >= len(self.esems[eng]):
            self.esems[eng].append(self._newsem(f"{eng}{ep}"))
        tok = Tok(eng, self.esems[eng][ep], n - ep * EPOCH + 1)
        self.count[eng] = n + 1
        if eng == "pe":
            deps = [d for d in deps if d.eng != "pe"]
        self.ops[eng].append(Op(fn, deps, tok, False))
        self._commit(tok, reads, writes)
        return tok

    def dma(self, queue, fn, reads=(), writes=(), group=None):
        deps = self._deps(reads, writes)
        key = writes[0]
        d = self.dsem.get(key)
        if d is None:
            d = self.dsem[key] = [self._newsem("d"), 0]
        d[1] += 16
        tok = Tok("dma", d[0], d[1])
        if group is not None:
            self.groups.setdefault(group, set()).add(key)
        self.ops[queue].append(Op(fn, deps, tok, True))
        self._commit(tok, reads, writes)
        return tok

    def _resolve(self, d):
        return d.sem, d.val

    def flush(self, final_groups=()):
        nc = self.nc
        bar = []
        for e in self.COMPUTE:
            n = self.count[e]
            if n:
                ep = (n - 1) // EPOCH
                bar.append((self.esems[e][ep], n - ep * EPOCH))
        for d in self.dsem.values():
            bar.append((d[0], d[1]))

        def run(engname, e):
            waited = self.waited[engname]
            for op in self.ops[engname]:
                for d in op.deps:
                    sem, val = self._resolve(d)
                    k = id(sem)
                    if waited.get(k, 0) >= val:
                        continue
                    waited[k] = val
                    e.wait_ge(sem, val)
                ins = op.fn(e)
                ins.then_inc(op.tok.sem, 16 if op.is_dma else 1)
            for sem, val in bar:
                k = id(sem)
                if waited.get(k, 0) >= val:
                    continue
                waited[k] = val
                e.wait_ge(sem, val)
            self.ops[engname] = []

        with nc.Block() as block:
            for engname in self.ALL:
                def mk(engname=engname):
                    def f(e):
                        run(engname, e)
                    return f
                getattr(block, self.ATTR[engname])(mk())
        self.res = {}


def build_program():
    nc = bass.Bass("TRN2", target_bir_lowering=False)
    dt_in = lambda name, shape: nc.dram_tensor(name, shape, F32, kind="ExternalInput")
    xs_d = dt_in("xs", [2, S_S, D])
    xp_d = dt_in("xp", [S_P, D])
    wA_d = dt_in("wA", [NH, D, 512])
    wC_d = dt_in("wC", [D, 5120])
    wP_d = dt_in("wP", [3, D, D])
    gpre_d = dt_in("gpre_b", [128, D])
    gpost_d = dt_in("gpost_b", [128, D])
    lamv_d = dt_in("lamv", [128, 4 * 64])
    subg_d = dt_in("subg", [128, 1])
    lng_d = dt_in("lng", [128, 8])
    lnbrow_d = dt_in("lnb_rows", [128, D])
    wsT_d = dt_in("wsT", [128, 8 * 128])
    bsrow_d = dt_in("bsrow", [1, D])
    relb_d = dt_in("relb", [32, 8])
    cbf_d = dt_in("cbf", [128, 16])
    sel_d = dt_in("sel", [128, 8])
    oh_d = dt_in("oh", [32, GL])
    ident_d = dt_in("ident", [128, 128])
    ys_d = nc.dram_tensor("ys", [2, S_S, D], F32, kind="ExternalOutput")
    yp_d = nc.dram_tensor("yp", [S_S, D], F32, kind="ExternalOutput")

    scr = lambda name, shape, dt: nc.dram_tensor(name, shape, dt, kind="Internal")
    wAb_d = scr("wAb", [NH, 128, 8 * 512], BF16)
    wCb_d = scr("wCb", [128, 8 * 5120], BF16)
    wPb_d = scr("wPb", [3, 128, 8 * 1024], BF16)
    hT_d = [scr("hT0", [D, S_S], BF16), scr("hT1", [D, S_S], BF16), scr("hT2", [D, S_P], BF16)]
    ga_d = [scr(f"ga{i}", [D, S_S], BF16) for i in range(3)]
    gp_d = scr("gp", [NH * 128 * GL], F32)

    with contextlib.ExitStack() as st:
        S = Sched(nc, st)
        _nm = [0]

        def sb(stack, name, shape, dt):
            _nm[0] += 1
            return stack.enter_context(nc.sbuf_tensor(f"sb{_nm[0]}_{name}", shape, dt))
        ps = st.enter_context(nc.psum_tensor("ps", [128, 8, 512], F32))
        psb = ps[:, 0:8, :].bitcast(BF16)

        ident_f = sb(st, "ident_f", [128, 128], F32)
        ident_b = sb(st, "ident_b", [128, 128], BF16)
        ones_b = sb(st, "ones_b", [128, 128], BF16)
        onesdiv = sb(st, "onesdiv", [128, 128], F32)
        onesrow = sb(st, "onesrow", [1, 128], F32)
        gpre_b = sb(st, "gpre", [128, D], F32)
        gpost_b = sb(st, "gpost", [128, D], F32)
        small = sb(st, "small", [128, 64], F32)
        subgs = sb(st, "subgs", [128, 1], F32)
        lng = sb(st, "lng", [128, 8], F32)
        wsT_b = sb(st, "wsT_b", [128, 8, 128], BF16)
        Cg = sb(st, "Cg", [128, 8, 128], F32)
        cbf = sb(st, "cbf", [128, 2, 8], F32)
        cbf8 = sb(st, "cbf8", [128, 2, 8], F32)
        CBt = sb(st, "CBt", [128, 5, 8], F32)
        CBt8 = sb(st, "CBt8", [128, 5, 8], F32)
        sel = sb(st, "sel", [128, 8], F32)
        xin = [sb(st, f"xin{i}", [128, D], F32) for i in range(2)]
        junk = sb(st, "junk", [128, D], BF16)
        hTt = [sb(st, f"hTt{i}", [128, 8, 512], BF16) for i in range(2)]
        TT = sb(st, "TT", [128, 6, 512], F32)
        stat = sb(st, "stat", [128, 16], F32)
        neglam = small[:, 0:1]
        mhalf = sb(st, "mhalf", [128, 8], F32)
        _mh = mhalf[:, 0:1]
        mhalf1 = _mh
        mhalf512 = bass.AP(_mh.tensor, _mh.offset, [list(_mh.ap[0]), [0, 512]])

        cnt = {"xin": 0, "hTt": 0, "bank": 0, "T": 0, "st": 0}

        def nxt(k, n):
            v = cnt[k] % n
            cnt[k] += 1
            return v

        def bank():
            return nxt("bank", 8)

        def bank2():
            if cnt["bank"] % 2:
                cnt["bank"] += 1
            b = cnt["bank"] % 8
            cnt["bank"] += 2
            return b

        def tmp():
            return nxt("T", 6)

        def stcol():
            return nxt("st", 16)

        def setup():
            lp = lambda dst, src, key: S.dma("sp", lambda e: e.dma_start(out=dst, in_=src), writes=[key])
            lp(ident_f[:], ident_d.ap(), "ident_f")
            lp(gpre_b[:], gpre_d.ap(), "gpre")
            lp(gpost_b[:], gpost_d.ap(), "gpost")
            lp(subgs[:], subg_d.ap(), "subg_raw")
            lp(lng[:], lng_d.ap(), "lng")
            lp(cbf[:].rearrange("p a h -> p (a h)"), cbf_d.ap(), "cbf")
            lp(sel[:], sel_d.ap(), "sel")
            S.op("dve", lambda e: e.tensor_copy(out=ident_b[:], in_=ident_f[:]), reads=["ident_f"], writes=["ident_b"])
            S.op("dve", lambda e: e.memset(ones_b[:], 1.0), writes=["ones_b"])
            S.op("dve", lambda e: e.memset(onesdiv[:], 1.0 / 128.0), writes=["onesdiv"])
            S.op("dve", lambda e: e.memset(onesrow[:], 1.0), writes=["onesrow"])
            S.op("dve", lambda e: e.memset(mhalf[:], -0.5), writes=["mhalf"])
            S.op("dve", lambda e: e.tensor_scalar(out=subgs[:], in0=subgs[:], scalar1=1.0 - LAMBDA_INIT, scalar2=None,
                                                  op0=ALU.mult), reads=["subg_raw"], writes=["subg_raw"])
            with contextlib.ExitStack() as ls:
                lamv = sb(ls, "lamv", [128, 4, 64], F32)
                lnbrows = sb(ls, "lnbrows", [128, D], F32)
                wsT_f = sb(ls, "wsT_f", [128, 8, 128], F32)
                bsrow = sb(ls, "bsrow", [1, D], F32)
                relb = sb(ls, "relb", [32, 8], F32)
                oh = sb(ls, "oh", [32, GL], F32)
                G = sb(ls, "G", [8, GL], F32)
                stg = [sb(ls, f"stg{i}", [128, 2048], F32) for i in range(2)]
                stb = [sb(ls, f"stb{i}", [128, 2048], BF16) for i in range(2)]
                lp(lamv[:].rearrange("p a j -> p (a j)"), lamv_d.ap(), "lamv")
                lp(lnbrows[:], lnbrow_d.ap(), "lnbrows")
                lp(wsT_f[:].rearrange("p g q -> p (g q)"), wsT_d.ap(), "wsT_f")
                lp(bsrow[:], bsrow_d.ap(), "bsrow")
                lp(relb[:], relb_d.ap(), "relb")
                lp(oh[:], oh_d.ap(), "oh")
                for j in range(2):
                    S.op("dve", lambda e, j=j: e.tensor_tensor(out=TT[:, j, 0:64], in0=lamv[:, 2 * j, :],
                                                               in1=lamv[:, 2 * j + 1, :], op=ALU.mult),
                         reads=["lamv"], writes=[("T", j)])
                    S.op("dve", lambda e, j=j: e.reduce_sum(out=small[:, 1 + j:2 + j], in_=TT[:, j, 0:64],
                                                            axis=mybir.AxisListType.X),
                         reads=[("T", j)], writes=[("sm", 1 + j)])
                    S.op("act", lambda e, j=j: e.activation(out=small[:, 3 + j:4 + j], in_=small[:, 1 + j:2 + j],
                                                            func=AF.Exp), reads=[("sm", 1 + j)], writes=[("sm", 3 + j)])
                S.op("dve", lambda e: e.tensor_tensor(out=small[:, 5:6], in0=small[:, 3:4], in1=small[:, 4:5],
                                                      op=ALU.subtract), reads=[("sm", 3), ("sm", 4)], writes=[("sm", 5)])
                S.op("dve", lambda e: e.tensor_scalar(out=small[:, 0:1], in0=small[:, 5:6], scalar1=LAMBDA_INIT,
                                                      scalar2=-1.0, op0=ALU.add, op1=ALU.mult),
                     reads=[("sm", 5)], writes=["neglam"])
                S.op("dve", lambda e: e.tensor_copy(out=wsT_b[:], in_=wsT_f[:]), reads=["wsT_f"], writes=["wsT_b"])
                for g in range(8):
                    bk, off = g // 4, (g % 4) * 128
                    S.op("pe", lambda e, g=g, bk=bk, off=off: e.matmul(
                        ps[:, bk, off:off + 128], lhsT=lnbrows[:, g * 128:(g + 1) * 128], rhs=wsT_f[:, g, :],
                        start=True, stop=False), reads=["lnbrows", "wsT_f"], writes=[("ps", bk)])
                    S.op("pe", lambda e, g=g, bk=bk, off=off: e.matmul(
                        ps[:, bk, off:off + 128], lhsT=onesrow[0:1, :], rhs=bsrow[0:1, g * 128:(g + 1) * 128],
                        start=False, stop=True), reads=["onesrow", "bsrow"], writes=[("ps", bk)])
                S.op("dve", lambda e: e.tensor_copy(out=Cg[:].rearrange("p (a b) q -> p a (b q)", a=2),
                                                    in_=ps[:, 0:2, :]), reads=[("ps", 0), ("ps", 1)], writes=["Cg"])
                S.op("dve", lambda e: e.tensor_scalar(out=cbf8[:], in0=cbf[:], scalar1=8.0, scalar2=None, op0=ALU.mult),
                     reads=["cbf"], writes=["cbf8"])
                S.op("dve", lambda e: e.tensor_copy(out=CBt[:, 0:2, :], in_=cbf[:]), reads=["cbf"], writes=["CBt01"])
                for i in range(3):
                    S.op("dve", lambda e, i=i: e.tensor_scalar(out=TT[:, 2, 0:8], in0=cbf[:, 0, :],
                                                               scalar1=sel[:, i:i + 1], scalar2=None, op0=ALU.mult),
                         reads=["cbf", "sel"], writes=[("T", 2)])
                    S.op("dve", lambda e, i=i: e.scalar_tensor_tensor(out=CBt[:, 2 + i, :], in0=cbf[:, 1, :],
                                                                      scalar=sel[:, 3 + i:4 + i], in1=TT[:, 2, 0:8],
                                                                      op0=ALU.mult, op1=ALU.add),
                         reads=["cbf", "sel", ("T", 2)], writes=[("CBt", i)])
                S.op("dve", lambda e: e.tensor_scalar(out=CBt8[:], in0=CBt[:], scalar1=8.0, scalar2=None, op0=ALU.mult),
                     reads=["CBt01", ("CBt", 0), ("CBt", 1), ("CBt", 2)], writes=["CBt8"])
                for c0 in range(0, GL, 512):
                    w = min(512, GL - c0)
                    bk = 2 + c0 // 512
                    S.op("pe", lambda e, c0=c0, w=w, bk=bk: e.matmul(ps[0:8, bk, 0:w], lhsT=relb[:, :], rhs=oh[:, c0:c0 + w],
                                                                     start=True, stop=True),
                         reads=["relb", "oh"], writes=[("ps", bk)])
                    S.op("dve", lambda e, c0=c0, w=w, bk=bk: e.tensor_scalar(out=G[:, c0:c0 + w], in0=ps[0:8, bk, 0:w],
                                                                             scalar1=8.0, scalar2=None, op0=ALU.mult),
                         reads=[("ps", bk)], writes=[("G", c0)])
                gsrc = bass.AP(G[:].tensor, G[:].offset, [list(G[:].ap[0]), [0, 128], [1, GL]])
                S.dma("pool", lambda e: e.dma_start(out=bass.AP(gp_d, 0, [[128 * GL, 8], [GL, 128], [1, GL]]), in_=gsrc),
                      reads=[("G", 0), ("G", 512), ("G", 1024)], writes=["st_gp"], group="gp")
                jobs = []
                for h in range(NH):
                    for k0 in (0, 4):
                        src = wA_d.ap()[h, k0 * 128:(k0 + 4) * 128, :].rearrange("(kc p) n -> p kc n", p=128)
                        dst = wAb_d.ap()[h].rearrange("p (kc n) -> p kc n", kc=8)[:, k0:k0 + 4, :]
                        jobs.append((src, dst, [4, 512]))
                for kc in range(8):
                    for c0, w in ((0, 2048), (2048, 2048), (4096, 1024)):
                        src = wC_d.ap()[kc * 128:(kc + 1) * 128, c0:c0 + w]
                        dst = wCb_d.ap().rearrange("p (kc n) -> p kc n", kc=8)[:, kc, c0:c0 + w]
                        jobs.append((src, dst, [w]))
                for m in range(3):
                    for k0 in (0, 2, 4, 6):
                        src = wP_d.ap()[m, k0 * 128:(k0 + 2) * 128, :].rearrange("(kc p) n -> p kc n", p=128)
                        dst = wPb_d.ap()[m].rearrange("p (kc n) -> p kc n", kc=8)[:, k0:k0 + 2, :]
                        jobs.append((src, dst, [2, 1024]))
                for i, (src, dst, shp) in enumerate(jobs):
                    sl = i % 2
                    n = int(np.prod(shp))
                    if len(shp) == 2:
                        vf = stg[sl][:, 0:n].rearrange("p (a b) -> p a b", a=shp[0])
                        vb = stb[sl][:, 0:n].rearrange("p (a b) -> p a b", a=shp[0])
                    else:
                        vf, vb = stg[sl][:, 0:n], stb[sl][:, 0:n]
                    S.dma("sp", lambda e, vf=vf, src=src: e.dma_start(out=vf, in_=src), writes=[("stg", sl)])
                    eng = ("dve", "pool", "act")[i % 3]
                    if eng == "act":
                        S.op("act", lambda e, sl=sl, n=n: e.activation(out=stb[sl][:, 0:n], in_=stg[sl][:, 0:n], func=AF.Copy),
                             reads=[("stg", sl)], writes=[("stb", sl)])
                    else:
                        S.op(eng, lambda e, sl=sl, n=n: e.tensor_copy(out=stb[sl][:, 0:n], in_=stg[sl][:, 0:n]),
                             reads=[("stg", sl)], writes=[("stb", sl)])
                    S.dma("pool", lambda e, vb=vb, dst=dst: e.dma_start(out=dst, in_=vb), reads=[("stb", sl)], writes=[("st_wbf", sl)], group="wbf")
                S.flush()

        def phase_a(seg, x_ap, ntok):
            with contextlib.ExitStack() as as_:
                hbf = sb(as_, "hbf", [128, D], BF16)
                phase_a_body(seg, x_ap, ntok, hbf)

        def phase_a_body(seg, x_ap, ntok, hbf):
            hTv = hT_d[seg].ap().rearrange("(kc p) t -> p kc t", p=128)
            for g in range(ntok // 512):
                slot = nxt("hTt", 2)
                for sub in range(4):
                    xs_ = nxt("xin", 2)
                    r0 = g * 512 + sub * 128
                    S.dma("sp", lambda e, xs_=xs_, r0=r0: e.dma_start(out=xin[xs_][:], in_=x_ap[r0:r0 + 128, :]),
                          writes=[("xin", xs_)])
                    c = stcol()
                    S.op("act", lambda e, xs_=xs_, c=c: e.activation(out=junk[:], in_=xin[xs_][:], func=AF.Square,
                                                                     accum_out=stat[:, c:c + 1]),
                         reads=[("xin", xs_)], writes=["junk", ("st", c)])
                    c2 = stcol()
                    S.op("dve", lambda e, c=c, c2=c2: e.tensor_scalar(out=stat[:, c2:c2 + 1], in0=stat[:, c:c + 1],
                                                                      scalar1=1.0 / D, scalar2=EPS, op0=ALU.mult, op1=ALU.add),
                         reads=[("st", c)], writes=[("st", c2)])
                    c3 = stcol()
                    S.op("pool", lambda e, c2=c2, c3=c3: e.tensor_tensor(out=stat[:, c3:c3 + 1], in0=stat[:, c2:c2 + 1],
                                                                         in1=mhalf1, op=ALU.pow),
                         reads=[("st", c2)], writes=[("st", c3)])
                    S.op("dve", lambda e, xs_=xs_, c3=c3: e.scalar_tensor_tensor(
                        out=hbf[:], in0=xin[xs_][:], scalar=stat[:, c3:c3 + 1], in1=gpre_b[:], op0=ALU.mult, op1=ALU.mult),
                        reads=[("xin", xs_), ("st", c3), "gpre"], writes=["hbf"])
                    for kc in range(8):
                        S.op("pe", lambda e, kc=kc, sub=sub: e.transpose(
                            out=psb[:, kc, sub * 128:(sub + 1) * 128], in_=hbf[:, kc * 128:(kc + 1) * 128], identity=ident_b[:]),
                            reads=["hbf", "ident_b"], writes=[("ps", kc)])
                S.op("act", lambda e, slot=slot: e.activation(out=hTt[slot][:, 0:4, :], in_=psb[:, 0:4, 0:512], func=AF.Copy),
                     reads=[("ps", k) for k in range(4)], writes=[("hTt", slot, 0)])
                S.op("dve", lambda e, slot=slot: e.tensor_copy(out=hTt[slot][:, 4:8, :], in_=psb[:, 4:8, 0:512]),
                     reads=[("ps", k) for k in range(4, 8)], writes=[("hTt", slot, 1)])
                S.dma("pool", lambda e, slot=slot, g=g: e.dma_start(out=hTv[:, :, g * 512:(g + 1) * 512], in_=hTt[slot][:]),
                      reads=[("hTt", slot, 0), ("hTt", slot, 1)], writes=[("st_hT", slot)], group=f"hT{seg}")
            S.flush()

        def units(seg, skv, prompt):
            hTv = hT_d[seg].ap().rearrange("(kc p) t -> p kc t", p=128)
            gav = ga_d[seg].ap()
            nkb = skv // 128
            with contextlib.ExitStack() as us:
                KT = sb(us, "KT", [128, skv], BF16)
                V = sb(us, "V", [128, nkb, 128], BF16)
                QT = sb(us, "QT", [128, S_S], BF16)
                SG = sb(us, "SG", [128, S_S], BF16)
                Wh = sb(us, "Wh", [128, 8, 512], BF16)
                E = [sb(us, f"E{i}", [128, 2, 512], BF16) for i in range(3)]
                U0 = sb(us, "U0", [128, 1152], F32)
                BB = sb(us, "BB", [128, 2, 512], F32)
                o_all = sb(us, "o_all", [128, S_S], F32)
                GA = [sb(us, f"GA{i}", [128, 512], BF16) for i in range(2)]
                u0p = list(U0[:].ap[0])

                def uwin(w):
                    return bass.AP(U0[:].tensor, U0[:, w:w + 512].offset, [u0p, [0, 2], [1, 512]])

                def bwin(j):
                    a = BB[:, j, :]
                    return bass.AP(a.tensor, a.offset, [list(a.ap[0]), [0, 2], [1, 512]])

                for h in range(NH):
                    S.dma("sp", lambda e, h=h: e.dma_start(out=Wh[:].rearrange("p k n -> p (k n)"), in_=wAb_d.ap()[h]),
                          reads=["@wbf"], writes=["Wh"])
                    S.dma("sp", lambda e, h=h: e.dma_start(out=U0[:], in_=bass.AP(gp_d, h * 128 * GL + 127,
                                                                                  [[GL - 1, 128], [1, 1152]])),
                          reads=["@gp"], writes=["U0"])
                    if prompt:
                        S.op("dve", lambda e, h=h: e.tensor_scalar(out=BB[:, 0, :], in0=U0[:, 0:512], scalar1=cbf8[:, 1, h:h + 1],
                                                                   scalar2=sel[:, 6:7], op0=ALU.subtract, op1=ALU.mult),
                             reads=["U0", "cbf8", "sel"], writes=["BB0"])
                        S.op("dve", lambda e, h=h: e.tensor_scalar(out=BB[:, 0, :], in0=BB[:, 0, :], scalar1=CBt8[:, 2, h:h + 1],
                                                                   scalar2=None, op0=ALU.add),
                             reads=["BB0", "CBt8"], writes=["BB0"])
                        S.op("dve", lambda e, h=h: e.tensor_scalar(out=BB[:, 1, :], in0=U0[:, 640:1152], scalar1=cbf8[:, 0, h:h + 1],
                                                                   scalar2=sel[:, 7:8], op0=ALU.subtract, op1=ALU.mult),
                             reads=["U0", "cbf8", "sel"], writes=["BB1"])
                        S.op("dve", lambda e, h=h: e.tensor_scalar(out=BB[:, 1, :], in0=BB[:, 1, :], scalar1=CBt8[:, 4, h:h + 1],
                                                                   scalar2=None, op0=ALU.add),
                             reads=["BB1", "CBt8"], writes=["BB1"])
                    for t in range(skv // 512):
                        slot = nxt("hTt", 2)
                        S.dma("sp", lambda e, slot=slot, t=t: e.dma_start(out=hTt[slot][:], in_=hTv[:, :, t * 512:(t + 1) * 512]),
                              reads=[f"@hT{seg}"], writes=[("hTt", slot, 0), ("hTt", slot, 1)])
                        hk = [("hTt", slot, 0), ("hTt", slot, 1)]
                        b = bank()
                        for kc in range(8):
                            S.op("pe", lambda e, b=b, kc=kc, slot=slot: e.matmul(
                                ps[:, b, :], lhsT=Wh[:, kc, 128:256], rhs=hTt[slot][:, kc, :], start=(kc == 0), stop=(kc == 7)),
                                reads=["Wh"] + hk, writes=[("ps", b)])
                        S.op("dve", lambda e, b=b, t=t: e.tensor_copy(out=KT[:, t * 512:(t + 1) * 512], in_=ps[:, b, :]),
                             reads=[("ps", b)], writes=[("KT", t)])
                        b = bank()
                        for sub in range(4):
                            for kc in range(8):
                                S.op("pe", lambda e, b=b, kc=kc, sub=sub, slot=slot: e.matmul(
                                    ps[:, b, sub * 128:(sub + 1) * 128], lhsT=hTt[slot][:, kc, sub * 128:(sub + 1) * 128],
                                    rhs=Wh[:, kc, 256:384], start=(kc == 0), stop=(kc == 7)),
                                    reads=["Wh"] + hk, writes=[("ps", b)])
                        S.op("act", lambda e, b=b, t=t: e.activation(
                            out=V[:, 4 * t:4 * t + 4, :].rearrange("p a b -> p (a b)"), in_=ps[:, b, :], func=AF.Copy),
                            reads=[("ps", b)], writes=[("V", t)])
                        if t < 8:
                            b = bank()
                            for kc in range(8):
                                S.op("pe", lambda e, b=b, kc=kc, slot=slot: e.matmul(
                                    ps[:, b, :], lhsT=Wh[:, kc, 0:128], rhs=hTt[slot][:, kc, :], start=(kc == 0), stop=(kc == 7)),
                                    reads=["Wh"] + hk, writes=[("ps", b)])
                            S.op("dve", lambda e, b=b, t=t: e.tensor_copy(out=QT[:, t * 512:(t + 1) * 512], in_=ps[:, b, :]),
                                 reads=[("ps", b)], writes=[("QT", t)])
                            b = bank()
                            for kc in range(8):
                                S.op("pe", lambda e, b=b, kc=kc, slot=slot: e.matmul(
                                    ps[:, b, :], lhsT=Wh[:, kc, 384:512], rhs=hTt[slot][:, kc, :], start=(kc == 0), stop=(kc == 7)),
                                    reads=["Wh"] + hk, writes=[("ps", b)])
                            ti = tmp()
                            S.op("act", lambda e, b=b, ti=ti: e.activation(out=TT[:, ti, :], in_=ps[:, b, :], func=AF.Exp, scale=-1.0),
                                 reads=[("ps", b)], writes=[("T", ti)])
                            S.op("dve", lambda e, ti=ti: e.tensor_scalar(out=TT[:, ti, :], in0=TT[:, ti, :], scalar1=1.0, scalar2=None,
                                                                         op0=ALU.add), reads=[("T", ti)], writes=[("T", ti)])
                            S.op("dve", lambda e, ti=ti: e.reciprocal(out=TT[:, ti, :], in_=TT[:, ti, :]),
                                 reads=[("T", ti)], writes=[("T", ti)])
                            S.op("dve", lambda e, b=b, ti=ti, t=t: e.tensor_tensor(out=SG[:, t * 512:(t + 1) * 512], in0=ps[:, b, :],
                                                                                   in1=TT[:, ti, :], op=ALU.mult),
                                 reads=[("ps", b), ("T", ti)], writes=[("SG", t)])
                    steps = [(qt, kb) for qt in range(8) for kb in range(nkb)]
                    nst = len(steps)

                    def emit_scores(i):
                        qt, kb = steps[i]
                        b0 = 2 * (i % 2)
                        rk = [("KT", kb // 4), ("QT", qt)]
                        S.op("pe", lambda e: e.matmul(ps[:, b0, :], lhsT=KT[0:64, kb * 128:(kb + 1) * 128],
                                                      rhs=QT[0:64, qt * 512:(qt + 1) * 512], start=True, stop=True),
                             reads=rk, writes=[("ps", b0)])
                        S.op("pe", lambda e: e.matmul(ps[:, b0 + 1, :], lhsT=KT[64:128, kb * 128:(kb + 1) * 128],
                                                      rhs=QT[64:128, qt * 512:(qt + 1) * 512], start=True, stop=True),
                             reads=rk, writes=[("ps", b0 + 1)])
                        chunk, kbl = kb // 32, kb % 32
                        dl = kbl - 4 * qt
                        win = None
                        if chunk == 0 and -1 <= dl <= 4:
                            win = uwin(512 - 128 * dl)
                            rd = ["U0"]
                        elif prompt and chunk == 1 and qt == 7 and kbl == 0:
                            win, rd = bwin(0), ["BB0"]
                        elif prompt and chunk == 3 and qt == 0 and kbl == 31:
                            win, rd = bwin(1), ["BB1"]
                        if win is not None:
                            S.op("dve", lambda e: e.tensor_tensor(out=ps[:, b0:b0 + 2, :], in0=ps[:, b0:b0 + 2, :], in1=win, op=ALU.add),
                                 reads=rd + [("ps", b0), ("ps", b0 + 1)], writes=[("ps", b0), ("ps", b0 + 1)])
                            bias = 0.0
                            rb = []
                        else:
                            if chunk == 0:
                                ci = 0 if dl < -1 else 1
                            else:
                                ci = 1 + chunk
                            bias = CBt[:, ci, h:h + 1]
                            rb = []
                        ei = i % 3
                        S.op("act", lambda e: e.activation(out=E[ei][:], in_=ps[:, b0:b0 + 2, :], func=AF.Exp, bias=bias, scale=SCALE),
                             reads=[("ps", b0), ("ps", b0 + 1)] + rb, writes=[("E", ei)])

                    def emit_pv(i):
                        qt, kb = steps[i]
                        ei = i % 3
                        first, last = kb == 0, kb == nkb - 1
                        for m in range(2):
                            S.op("pe", lambda e, m=m: e.matmul(ps[:, 4 + m, :], lhsT=V[:, kb, :], rhs=E[ei][:, m, :],
                                                               start=first, stop=last),
                                 reads=[("V", kb // 4), ("E", ei)], writes=[("ps", 4 + m)])
                        for m in range(2):
                            S.op("pe", lambda e, m=m: e.matmul(ps[:, 6 + m, :], lhsT=ones_b[:], rhs=E[ei][:, m, :],
                                                               start=first, stop=last),
                                 reads=[("E", ei)], writes=[("ps", 6 + m)])
                        if last:
                            t0, t1, t2, t3 = tmp(), tmp(), tmp(), tmp()
                            S.op("dve", lambda e: e.reciprocal(out=TT[:, t0, :], in_=ps[:, 6, :]), reads=[("ps", 6)], writes=[("T", t0)])
                            S.op("dve", lambda e: e.reciprocal(out=TT[:, t1, :], in_=ps[:, 7, :]), reads=[("ps", 7)], writes=[("T", t1)])
                            S.op("dve", lambda e: e.tensor_tensor(out=TT[:, t2, :], in0=ps[:, 4, :], in1=TT[:, t0, :], op=ALU.mult),
                                 reads=[("ps", 4), ("T", t0)], writes=[("T", t2)])
                            S.op("dve", lambda e: e.tensor_tensor(out=TT[:, t3, :], in0=ps[:, 5, :], in1=TT[:, t1, :], op=ALU.mult),
                                 reads=[("ps", 5), ("T", t1)], writes=[("T", t3)])
                            S.op("dve", lambda e: e.scalar_tensor_tensor(out=o_all[:, qt * 512:(qt + 1) * 512], in0=TT[:, t3, :],
                                                                         scalar=neglam, in1=TT[:, t2, :], op0=ALU.mult, op1=ALU.add),
                                 reads=[("T", t2), ("T", t3)], writes=[("o", qt)])

                    emit_scores(0)
                    for i in range(nst):
                        if i + 1 < nst:
                            emit_scores(i + 1)
                        emit_pv(i)
                    cnt["bank"] = 0
                    for qt in range(8):
                        osl = o_all[:, qt * 512:(qt + 1) * 512]
                        t0, t1, t2 = tmp(), tmp(), tmp()
                        S.op("dve", lambda e, osl=osl, t0=t0: e.tensor_tensor(out=TT[:, t0, :], in0=osl, in1=osl, op=ALU.mult),
                             reads=[("o", qt)], writes=[("T", t0)])
                        b = nxt("bank", 4)
                        S.op("pe", lambda e, b=b, t0=t0: e.matmul(ps[:, b, :], lhsT=onesdiv[:], rhs=TT[:, t0, :], start=True, stop=True),
                             reads=[("T", t0)], writes=[("ps", b)])
                        S.op("dve", lambda e, b=b, t1=t1: e.tensor_scalar(out=TT[:, t1, :], in0=ps[:, b, :], scalar1=EPS, scalar2=None,
                                                                          op0=ALU.add),
                             reads=[("ps", b)], writes=[("T", t1)])
                        S.op("pool", lambda e, t1=t1: e.tensor_tensor(out=TT[:, t1, :], in0=TT[:, t1, :], in1=mhalf512, op=ALU.pow),
                             reads=[("T", t1)], writes=[("T", t1)])
                        S.op("dve", lambda e, osl=osl, t1=t1, t2=t2: e.scalar_tensor_tensor(
                            out=TT[:, t2, :], in0=osl, scalar=subgs[:, 0:1], in1=TT[:, t1, :], op0=ALU.mult, op1=ALU.mult),
                            reads=[("o", qt), ("T", t1)], writes=[("T", t2)])
                        gs = qt % 2
                        S.op("dve", lambda e, t2=t2, gs=gs, qt=qt: e.tensor_tensor(out=GA[gs][:], in0=TT[:, t2, :],
                                                                                   in1=SG[:, qt * 512:(qt + 1) * 512], op=ALU.mult),
                             reads=[("T", t2), ("SG", qt)], writes=[("GA", gs)])
                        S.dma("pool", lambda e, gs=gs, qt=qt, h=h: e.dma_start(
                            out=gav[h * 128:(h + 1) * 128, qt * 512:(qt + 1) * 512], in_=GA[gs][:]),
                            reads=[("GA", gs)], writes=[("st_ga", gs)], group=f"ga{seg}")
                    S.flush()

        def phase_c(seg, x_ap, y_ap):
            hTv = hT_d[seg].ap().rearrange("(kc p) t -> p kc t", p=128)
            gav = ga_d[seg].ap().rearrange("(kc p) t -> p kc t", p=128)
            with contextlib.ExitStack() as cs:
                WC = sb(cs, "WC", [128, 8, 5120], BF16)
                WP = [sb(cs, f"WP{i}", [128, 8, 1024], BF16) for i in range(3)]
                GAin = sb(cs, "GAin", [128, 8, 512], BF16)
                gbT = sb(cs, "gbT", [128, 8, 512], BF16)
                vm = sb(cs, "vm", [128, 4, 1024], BF16)
                mT = vm[:].rearrange("p a (b c) -> p (a b) c", b=2)
                for kc in range(8):
                    S.dma("sp", lambda e, kc=kc: e.dma_start(
                        out=WC[:, kc, :], in_=wCb_d.ap().rearrange("p (kc n) -> p kc n", kc=8)[:, kc, :]),
                        reads=["@wbf"], writes=[("WC", kc)])
                for m in range(3):
                    S.dma("sp", lambda e, m=m: e.dma_start(out=WP[m][:].rearrange("p k n -> p (k n)"), in_=wPb_d.ap()[m]),
                          reads=["@wbf"], writes=[("WP", m)])
                wck = [("WC", kc) for kc in range(8)]
                for t in range(S_S // 512):
                    slot = nxt("hTt", 2)
                    S.dma("sp", lambda e, slot=slot, t=t: e.dma_start(out=hTt[slot][:], in_=hTv[:, :, t * 512:(t + 1) * 512]),
                          reads=[f"@hT{seg}"], writes=[("hTt", slot, 0), ("hTt", slot, 1)])
                    hk = [("hTt", slot, 0), ("hTt", slot, 1)]
                    S.dma("sp", lambda e, t=t: e.dma_start(out=GAin[:], in_=gav[:, :, t * 512:(t + 1) * 512]),
                          reads=[f"@ga{seg}"], writes=["GAin"])
                    for sub in range(4):
                        b = bank2()
                        for half in range(2):
                            for kc in range(8):
                                S.op("pe", lambda e, b=b, half=half, kc=kc, sub=sub, slot=slot: e.matmul(
                                    ps[:, b + half, :], lhsT=hTt[slot][:, kc, sub * 128:(sub + 1) * 128],
                                    rhs=WC[:, kc, 1024 + half * 512:1024 + (half + 1) * 512], start=(kc == 0), stop=(kc == 7)),
                                    reads=wck + hk, writes=[("ps", b + half)])
                        pk = [("ps", b), ("ps", b + 1)]
                        c1, c2 = stcol(), stcol()
                        S.op("act", lambda e, b=b, c1=c1: e.activation(out=junk[:].rearrange("p (a b) -> p a b", a=2), in_=ps[:, b:b + 2, :],
                                                                       func=AF.Identity, accum_out=stat[:, c1:c1 + 1]),
                             reads=pk, writes=["junk", ("st", c1)])
                        S.op("act", lambda e, b=b, c2=c2: e.activation(out=junk[:].rearrange("p (a b) -> p a b", a=2), in_=ps[:, b:b + 2, :],
                                                                       func=AF.Square, accum_out=stat[:, c2:c2 + 1]),
                             reads=pk, writes=["junk", ("st", c2)])
                        c3, c4, c5, c6 = stcol(), stcol(), stcol(), stcol()
                        S.op("dve", lambda e, c1=c1, c3=c3: e.tensor_scalar(out=stat[:, c3:c3 + 1], in0=stat[:, c1:c1 + 1], scalar1=1.0 / D,
                                                                            scalar2=None, op0=ALU.mult), reads=[("st", c1)], writes=[("st", c3)])
                        S.op("dve", lambda e, c3=c3, c4=c4: e.scalar_tensor_tensor(out=stat[:, c4:c4 + 1], in0=stat[:, c3:c3 + 1], scalar=-1.0,
                                                                                   in1=stat[:, c3:c3 + 1], op0=ALU.mult, op1=ALU.mult),
                             reads=[("st", c3)], writes=[("st", c4)])
                        S.op("dve", lambda e, c2=c2, c4=c4, c5=c5: e.scalar_tensor_tensor(out=stat[:, c5:c5 + 1], in0=stat[:, c2:c2 + 1],
                                                                                         scalar=1.0 / D, in1=stat[:, c4:c4 + 1],
                                                                                         op0=ALU.mult, op1=ALU.add),
                             reads=[("st", c2), ("st", c4)], writes=[("st", c5)])
                        S.op("dve", lambda e, c5=c5: e.tensor_scalar(out=stat[:, c5:c5 + 1], in0=stat[:, c5:c5 + 1], scalar1=EPS,
                                                                     scalar2=None, op0=ALU.add),
                             reads=[("st", c5)], writes=[("st", c5)])
                        S.op("pool", lambda e, c5=c5, c6=c6: e.tensor_tensor(out=stat[:, c6:c6 + 1], in0=stat[:, c5:c5 + 1],
                                                                             in1=mhalf1, op=ALU.pow),
                             reads=[("st", c5)], writes=[("st", c6)])
                        c7 = stcol()
                        S.op("dve", lambda e, c3=c3, c6=c6, c7=c7: e.scalar_tensor_tensor(out=stat[:, c7:c7 + 1], in0=stat[:, c3:c3 + 1],
                                                                                         scalar=-1.0, in1=stat[:, c6:c6 + 1],
                                                                                         op0=ALU.mult, op1=ALU.mult),
                             reads=[("st", c3), ("st", c6)], writes=[("st", c7)])
                        S.op("act", lambda e, b=b, sub=sub, c6=c6, c7=c7: e.activation(
                            out=vm[:, sub, :].rearrange("p (a b) -> p a b", a=2), in_=ps[:, b:b + 2, :], func=AF.Identity,
                            bias=stat[:, c7:c7 + 1], scale=stat[:, c6:c6 + 1]),
                            reads=pk + [("st", c6), ("st", c7)], writes=[("vm", sub)])
                    for g in range(8):
                        bs_ = bank()
                        for sub in range(4):
                            S.op("pe", lambda e, bs_=bs_, sub=sub, g=g: e.matmul(
                                ps[:, bs_, sub * 128:(sub + 1) * 128], lhsT=vm[:, sub, g * 128:(g + 1) * 128], rhs=wsT_b[:, g, :],
                                start=True, stop=True), reads=[("vm", sub), "wsT_b"], writes=[("ps", bs_)])
                        bu = bank()
                        for kc in range(8):
                            S.op("pe", lambda e, bu=bu, kc=kc, g=g, slot=slot: e.matmul(
                                ps[:, bu, :], lhsT=WC[:, kc, g * 128:(g + 1) * 128], rhs=hTt[slot][:, kc, :], start=(kc == 0), stop=(kc == 7)),
                                reads=wck + hk, writes=[("ps", bu)])
                        bg = bank()
                        for kc in range(8):
                            S.op("pe", lambda e, bg=bg, kc=kc, g=g, slot=slot: e.matmul(
                                ps[:, bg, :], lhsT=WC[:, kc, 2048 + g * 128:2048 + (g + 1) * 128], rhs=hTt[slot][:, kc, :],
                                start=(kc == 0), stop=(kc == 7)), reads=wck + hk, writes=[("ps", bg)])
                        ta, tb, tc = tmp(), tmp(), tmp()
                        cga = Cg[:, g, :]
                        cgw = bass.AP(cga.tensor, cga.offset, [list(cga.ap[0]), [0, 4], [1, 128]])
                        S.op("dve", lambda e, bs_=bs_, ta=ta, g=g, cgw=cgw: e.scalar_tensor_tensor(
                            out=TT[:, ta, :].rearrange("p (a b) -> p a b", a=4), in0=ps[:, bs_, :].rearrange("p (a b) -> p a b", a=4),
                            scalar=lng[:, g:g + 1], in1=cgw, op0=ALU.mult, op1=ALU.add),
                            reads=[("ps", bs_), "Cg", "lng"], writes=[("T", ta)])
                        S.op("act", lambda e, bg=bg, tb=tb: e.activation(out=TT[:, tb, :], in_=ps[:, bg, :], func=AF.Exp, scale=-1.0),
                             reads=[("ps", bg)], writes=[("T", tb)])
                        S.op("dve", lambda e, tb=tb: e.tensor_scalar(out=TT[:, tb, :], in0=TT[:, tb, :], scalar1=1.0, scalar2=None, op0=ALU.add),
                             reads=[("T", tb)], writes=[("T", tb)])
                        S.op("dve", lambda e, tb=tb: e.reciprocal(out=TT[:, tb, :], in_=TT[:, tb, :]),
                             reads=[("T", tb)], writes=[("T", tb)])
                        S.op("dve", lambda e, bg=bg, tb=tb: e.tensor_tensor(out=TT[:, tb, :], in0=ps[:, bg, :], in1=TT[:, tb, :], op=ALU.mult),
                             reads=[("ps", bg), ("T", tb)], writes=[("T", tb)])
                        S.op("dve", lambda e, bu=bu, ta=ta, tc=tc: e.tensor_tensor(out=TT[:, tc, :], in0=ps[:, bu, :], in1=TT[:, ta, :], op=ALU.mult),
                             reads=[("ps", bu), ("T", ta)], writes=[("T", tc)])
                        S.op("dve", lambda e, tb=tb, tc=tc, g=g: e.tensor_tensor(out=gbT[:, g, :], in0=TT[:, tc, :], in1=TT[:, tb, :], op=ALU.mult),
                             reads=[("T", tb), ("T", tc)], writes=[("gbT", g)])
                    gbk = [("gbT", g) for g in range(8)]
                    vmk = [("vm", s_) for s_ in range(4)]
                    for n in range(8):
                        bma, bya, bmb, byb = bank(), bank(), bank(), bank()
                        for kc in range(8):
                            S.op("pe", lambda e, kc=kc, n=n, bma=bma, slot=slot: e.matmul(
                                ps[:, bma, :], lhsT=WC[:, kc, 3072 + n * 128:3072 + (n + 1) * 128], rhs=hTt[slot][:, kc, :],
                                start=(kc == 0), stop=(kc == 7)), reads=wck + hk, writes=[("ps", bma)])
                        for kc in range(8):
                            S.op("pe", lambda e, kc=kc, n=n, bya=bya: e.matmul(
                                ps[:, bya, :], lhsT=WP[0][:, kc, n * 128:(n + 1) * 128], rhs=GAin[:, kc, :],
                                start=(kc == 0), stop=(kc == 7)), reads=[("WP", 0), "GAin"], writes=[("ps", bya)])
                        for kc in range(8):
                            S.op("pe", lambda e, kc=kc, n=n, bmb=bmb, slot=slot: e.matmul(
                                ps[:, bmb, :], lhsT=WC[:, kc, 4096 + n * 128:4096 + (n + 1) * 128], rhs=hTt[slot][:, kc, :],
                                start=(kc == 0), stop=(kc == 7)), reads=wck + hk, writes=[("ps", bmb)])
                        for kc in range(8):
                            S.op("pe", lambda e, kc=kc, n=n, byb=byb: e.matmul(
                                ps[:, byb, :], lhsT=WP[1][:, kc, n * 128:(n + 1) * 128], rhs=gbT[:, kc, :],
                                start=(kc == 0), stop=(kc == 7)), reads=[("WP", 1)] + gbk, writes=[("ps", byb)])
                        ta, tb = tmp(), tmp()
                        for (bm, by, tx) in ((bma, bya, ta), (bmb, byb, tb)):
                            S.op("act", lambda e, bm=bm, tx=tx: e.activation(out=TT[:, tx, :], in_=ps[:, bm, :], func=AF.Exp, scale=-1.0),
                                 reads=[("ps", bm)], writes=[("T", tx)])
                            S.op("dve", lambda e, tx=tx: e.tensor_scalar(out=TT[:, tx, :], in0=TT[:, tx, :], scalar1=1.0, scalar2=None, op0=ALU.add),
                                 reads=[("T", tx)], writes=[("T", tx)])
                            S.op("dve", lambda e, tx=tx: e.reciprocal(out=TT[:, tx, :], in_=TT[:, tx, :]),
                                 reads=[("T", tx)], writes=[("T", tx)])
                            S.op("dve", lambda e, by=by, tx=tx: e.tensor_tensor(out=TT[:, tx, :], in0=ps[:, by, :], in1=TT[:, tx, :], op=ALU.mult),
                                 reads=[("ps", by), ("T", tx)], writes=[("T", tx)])
                        S.op("dve", lambda e, ta=ta, tb=tb, n=n: e.tensor_tensor(out=mT[:, n, :], in0=TT[:, ta, :], in1=TT[:, tb, :], op=ALU.add),
                             reads=[("T", ta), ("T", tb)], writes=[("mT", n)] + vmk)
                    mk_ = [("mT", n) for n in range(8)]
                    for sub in range(4):
                        xs_ = nxt("xin", 2)
                        r0 = t * 512 + sub * 128
                        S.dma("sp", lambda e, xs_=xs_, r0=r0: e.dma_start(out=xin[xs_][:], in_=x_ap[r0:r0 + 128, :]), writes=[("xin", xs_)])
                        b = bank2()
                        for half in range(2):
                            for kc in range(8):
                                S.op("pe", lambda e, b=b, half=half, kc=kc, sub=sub: e.matmul(
                                    ps[:, b + half, :], lhsT=mT[:, kc, sub * 128:(sub + 1) * 128], rhs=WP[2][:, kc, half * 512:(half + 1) * 512],
                                    start=(kc == 0), stop=(kc == 7)), reads=[("WP", 2)] + mk_ + vmk, writes=[("ps", b + half)])
                        pk = [("ps", b), ("ps", b + 1)]
                        c1, c2, c3 = stcol(), stcol(), stcol()
                        S.op("act", lambda e, b=b, c1=c1: e.activation(out=junk[:].rearrange("p (a b) -> p a b", a=2), in_=ps[:, b:b + 2, :],
                                                                       func=AF.Square, accum_out=stat[:, c1:c1 + 1]),
                             reads=pk, writes=["junk", ("st", c1)])
                        S.op("dve", lambda e, c1=c1, c2=c2: e.tensor_scalar(out=stat[:, c2:c2 + 1], in0=stat[:, c1:c1 + 1], scalar1=1.0 / D,
                                                                            scalar2=EPS, op0=ALU.mult, op1=ALU.add),
                             reads=[("st", c1)], writes=[("st", c2)])
                        S.op("pool", lambda e, c2=c2, c3=c3: e.tensor_tensor(out=stat[:, c3:c3 + 1], in0=stat[:, c2:c2 + 1],
                                                                             in1=mhalf1, op=ALU.pow),
                             reads=[("st", c2)], writes=[("st", c3)])
                        if cnt["T"] % 2:
                            cnt["T"] += 1
                        ta = tmp()
                        tmp()
                        S.op("dve", lambda e, b=b, c3=c3, ta=ta: e.scalar_tensor_tensor(
                            out=TT[:, ta:ta + 2, :], in0=ps[:, b:b + 2, :], scalar=stat[:, c3:c3 + 1],
                            in1=gpost_b[:].rearrange("p (a b) -> p a b", a=2), op0=ALU.mult, op1=ALU.mult),
                            reads=pk + [("st", c3), "gpost"], writes=[("T", ta), ("T", ta + 1)])
                        S.op("dve", lambda e, xs_=xs_, ta=ta: e.tensor_tensor(out=xin[xs_][:].rearrange("p (a b) -> p a b", a=2),
                                                                              in0=TT[:, ta:ta + 2, :],
                                                                              in1=xin[xs_][:].rearrange("p (a b) -> p a b", a=2), op=ALU.add),
                             reads=[("T", ta), ("T", ta + 1), ("xin", xs_)], writes=[("xin", xs_)])
                        S.dma("pool", lambda e, xs_=xs_, r0=r0: e.dma_start(out=y_ap[r0:r0 + 128, :], in_=xin[xs_][:]),
                              reads=[("xin", xs_)], writes=[("st_out", xs_)], group="out")
                S.flush()

        _orig_deps = S._deps

        def _deps(reads, writes):
            real = [r for r in reads if not (isinstance(r, str) and r.startswith("@"))]
            deps = _orig_deps(real, writes)
            for r in reads:
                if isinstance(r, str) and r.startswith("@"):
                    for key in S.groups[r[1:]]:
                        dd = S.dsem[key]
                        deps.append(Tok("dma", dd[0], dd[1]))
            return deps

        _orig_commit = S._commit

        def _commit(tok, reads, writes):
            real = [r for r in reads if not (isinstance(r, str) and r.startswith("@"))]
            _orig_commit(tok, real, writes)

        S._deps, S._commit = _deps, _commit

        setup()
        segs = [(0, xs_d.ap()[0], S_S, False, ys_d.ap()[0]), (1, xs_d.ap()[1], S_S, False, ys_d.ap()[1]),
                (2, xp_d.ap(), S_P, True, yp_d.ap())]
        for seg, x_ap, skv, prompt, y_ap in segs:
            phase_a(seg, x_ap, skv)
            units(seg, skv, prompt)
            phase_c(seg, x_ap, y_ap)
    return nc


def _bucket(rel):
    half, max_exact = 16, 8
    n = np.abs(rel)
    nf = np.maximum(n, 1).astype(np.float32)
    large = max_exact + (np.log(nf / np.float32(max_exact)) / np.float32(math.log(128 / max_exact))
                         * np.float32(half - max_exact)).astype(np.int32)
    large = np.minimum(large, half - 1)
    return np.where(rel > 0, half, 0) + np.where(n < max_exact, n, large)


_CACHE = {}


def kernel(x_prompt, x_sample, g_pre, w_in, lambda_q1, lambda_k1, lambda_q2, lambda_k2, subln_g,
           w_pa, ln_g, ln_b, w_s, b_s, w_pb, w_o, g_post, rel_bias):
    f = lambda a: np.ascontiguousarray(np.asarray(a, dtype=np.float32))
    x_prompt, x_sample, w_in = f(x_prompt), f(x_sample), f(w_in)[0]
    rep = lambda v, n=128: np.ascontiguousarray(np.broadcast_to(f(v).reshape(1, -1), (n, f(v).size)))
    wA = np.empty((NH, D, 512), np.float32)
    for h in range(NH):
        wA[h, :, 0:64] = w_in[:, h * 64:(h + 1) * 64]
        wA[h, :, 64:128] = w_in[:, 512 + h * 64:512 + (h + 1) * 64]
        wA[h, :, 128:192] = w_in[:, 1024 + h * 64:1024 + (h + 1) * 64]
        wA[h, :, 192:256] = w_in[:, 1536 + h * 64:1536 + (h + 1) * 64]
        wA[h, :, 256:384] = w_in[:, 2048 + h * 128:2048 + (h + 1) * 128]
        wA[h, :, 384:512] = w_in[:, 3072 + h * 128:3072 + (h + 1) * 128]
    wC = np.ascontiguousarray(w_in[:, 4096:9216])
    wP = np.stack([f(w_pa)[0], f(w_pb)[0], f(w_o)[0]])
    lamv = np.concatenate([rep(lambda_q1), rep(lambda_k1), rep(lambda_q2), rep(lambda_k2)], axis=1)
    subg = f(subln_g).reshape(128, 1)
    lng = np.ascontiguousarray(f(ln_g).reshape(8, 128).T)
    wsT = np.ascontiguousarray(f(w_s)[0].transpose(2, 0, 1).reshape(128, 8 * 128))
    bsrow = f(b_s).reshape(1, D)
    relb = f(rel_bias)
    cbf = np.concatenate([rep(relb[15]), rep(relb[31])], axis=1)
    j = np.arange(GL)
    oh = (np.arange(32)[:, None] == _bucket(639 - j)[None, :]).astype(np.float32)
    ident = np.eye(128, dtype=np.float32)
    common = dict(wA=wA, wC=wC, wP=wP, gpre_b=rep(g_pre), gpost_b=rep(g_post), lamv=lamv, subg=subg, lng=lng,
                  lnb_rows=rep(ln_b), wsT=wsT, bsrow=bsrow, relb=relb, cbf=cbf, oh=oh, ident=ident)
    in_maps = []
    for c in range(8):
        pb, pq = c // 4, c % 4
        xp = np.ascontiguousarray(np.roll(x_prompt[pb], -pq * S_S, axis=0))
        selb = [1.0 if pq == 3 else 0.0, 1.0 if pq >= 2 else 0.0, 1.0 if pq >= 1 else 0.0]
        selv = selb + [1.0 - s for s in selb] + [1.0 if pq <= 2 else 0.0, 1.0 if pq >= 1 else 0.0]
        sel = np.ascontiguousarray(np.broadcast_to(np.array(selv, np.float32)[None, :], (128, 8)))
        m = dict(common)
        m.update(xs=np.ascontiguousarray(x_sample[2 * c:2 * c + 2]), xp=xp, sel=sel)
        in_maps.append(m)
    if "nc" not in _CACHE:
        _CACHE["nc"] = build_program()
    res = run_bass_kernel_spmd(_CACHE["nc"], in_maps, core_ids=list(range(8)))
    y_prompt = np.empty((2, S_P, D), np.float32)
    y_sample = np.empty((16, S_S, D), np.float32)
    for c in range(8):
        r = res.results[c]
        pb, pq = c // 4, c % 4
        y_prompt[pb, pq * S_S:(pq + 1) * S_S] = r["yp"]
        y_sample[2 * c:2 * c + 2] = r["ys"]
    return (y_prompt, y_sample)
```

```python
import contextlib
import math
import numpy as np
import concourse.bass as bass
import concourse.mybir as mybir
from concourse.bass_utils import run_bass_kernel_spmd

F32 = mybir.dt.float32
BF16 = mybir.dt.bfloat16
AF = mybir.ActivationFunctionType
ALU = mybir.AluOpType

D = 1024
NH = 8
S_S = 4096
S_P = 16384
EPS = 1e-6
LAMBDA_INIT = 0.8 - 0.6 * math.exp(-0.3 * 0)
SCALE = 0.125
GL = 1280
EPOCH = 20000


class Tok:
    __slots__ = ("eng", "sem", "val", "group")

    def __init__(self, eng, sem, val, group=None):
        self.eng, self.sem, self.val, self.group = eng, sem, val, group


class Op:
    __slots__ = ("fn", "deps", "tok", "is_dma")

    def __init__(self, fn, deps, tok, is_dma):
        self.fn, self.deps, self.tok, self.is_dma = fn, deps, tok, is_dma


class Sched:
    COMPUTE = ("pe", "act", "dve", "pool")
    ALL = ("pe", "act", "dve", "pool", "sp")
    ATTR = {"pe": "tensor", "act": "scalar", "dve": "vector", "pool": "gpsimd", "sp": "sync"}

    def __init__(self, nc, stack):
        self.nc, self.stack = nc, stack
        self.ops = {e: [] for e in self.ALL}
        self.count = {e: 0 for e in self.COMPUTE}
        self.esems = {e: [] for e in self.COMPUTE}
        self.res = {}
        self.dsem = {}
        self.groups = {}
        self.nsem = 0
        self.waited = {e: {} for e in self.ALL}
        self.group_final = set()

    def _newsem(self, name):
        self.nsem += 1
        return self.stack.enter_context(self.nc.semaphore(f"s{self.nsem}_{name}"))

    def _deps(self, reads, writes):
        deps = []
        for r in reads:
            e = self.res.get(r)
            if e and e[0] is not None:
                deps.append(e[0])
        for w in writes:
            e = self.res.get(w)
            if e:
                if e[0] is not None:
                    deps.append(e[0])
                deps.extend(e[1])
        return deps

    def _commit(self, tok, reads, writes):
        for r in reads:
            e = self.res.setdefault(r, [None, []])
            e[1].append(tok)
            if len(e[1]) > 64:
                del e[1][:32]
        for w in writes:
            self.res[w] = [tok, []]

    def op(self, eng, fn, reads=(), writes=()):
        deps = self._deps(reads, writes)
        n = self.count[eng]
        ep = n // EPOCH
        if ep >= len(self.esems[eng]):
            self.esems[eng].append(self._newsem(f"{eng}{ep}"))
        tok = Tok(eng, self.esems[eng][ep], n - ep * EPOCH + 1)
        self.count[eng] = n + 1
        if eng == "pe":
            deps = [d for d in deps if d.eng != "pe"]
        self.ops[eng].append(Op(fn, deps, tok, False))
        self._commit(tok, reads, writes)
        return tok

    def dma(self, queue, fn, reads=(), writes=(), group=None):
        deps = self._deps(reads, writes)
        key = writes[0]
        d = self.dsem.get(key)
        if d is None:
            d = self.dsem[key] = [self._newsem("d"), 0]
        d[1] += 16
        tok = Tok("dma", d[0], d[1])
        if group is not None:
            self.groups.setdefault(group, set()).add(key)
        self.ops[queue].append(Op(fn, deps, tok, True))
        self._commit(tok, reads, writes)
        return tok

    def _resolve(self, d):
        return d.sem, d.val

    def flush(self, final_groups=()):
        nc = self.nc
        bar = []
        for e in self.COMPUTE:
            n = self.count[e]
            if n:
                ep = (n - 1) // EPOCH
                bar.append((self.esems[e][ep], n - ep * EPOCH))
        for d in self.dsem.values():
            bar.append((d[0], d[1]))

        def run(engname, e):
            waited = self.waited[engname]
            for op in self.ops[engname]:
                for d in op.deps:
                    sem, val = self._resolve(d)
                    k = id(sem)
                    if waited.get(k, 0) >= val:
                        continue
                    waited[k] = val
                    e.wait_ge(sem, val)
                ins = op.fn(e)
                ins.then_inc(op.tok.sem, 16 if op.is_dma else 1)
            for sem, val in bar:
                k = id(sem)
                if waited.get(k, 0) >= val:
                    continue
                waited[k] = val
                e.wait_ge(sem, val)
            self.ops[engname] = []

        with nc.Block() as block:
            for engname in self.ALL:
                def mk(engname=engname):
                    def f(e):
                        run(engname, e)
                    return f
                getattr(block, self.ATTR[engname])(mk())
        self.res = {}


def build_program():
    nc = bass.Bass("TRN2", target_bir_lowering=False)
    dt_in = lambda name, shape: nc.dram_tensor(name, shape, F32, kind="ExternalInput")
    xs_d = dt_in("xs", [2, S_S, D])
    xp_d = dt_in("xp", [S_P, D])
    wA_d = dt_in("wA", [NH, D, 512])
    wC_d = dt_in("wC", [D, 5120])
    wP_d = dt_in("wP", [3, D, D])
    gpre_d = dt_in("gpre_b", [128, D])
    gpost_d = dt_in("gpost_b", [128, D])
    lamv_d = dt_in("lamv", [128, 4 * 64])
    subg_d = dt_in("subg", [128, 1])
    lng_d = dt_in("lng", [128, 8])
    lnbrow_d = dt_in("lnb_rows", [128, D])
    wsT_d = dt_in("wsT", [128, 8 * 128])
    bsrow_d = dt_in("bsrow", [1, D])
    relb_d = dt_in("relb", [32, 8])
    cbf_d = dt_in("cbf", [128, 16])
    sel_d = dt_in("sel", [128, 8])
    oh_d = dt_in("oh", [32, GL])
    ident_d = dt_in("ident", [128, 128])
    ys_d = nc.dram_tensor("ys", [2, S_S, D], F32, kind="ExternalOutput")
    yp_d = nc.dram_tensor("yp", [S_S, D], F32, kind="ExternalOutput")

    scr = lambda name, shape, dt: nc.dram_tensor(name, shape, dt, kind="Internal")
    wAb_d = scr("wAb", [NH, 128, 8 * 512], BF16)
    wCb_d = scr("wCb", [128, 8 * 5120], BF16)
    wPb_d = scr("wPb", [3, 128, 8 * 1024], BF16)
    hT_d = [scr("hT0", [D, S_S], BF16), scr("hT1", [D, S_S], BF16), scr("hT2", [D, S_P], BF16)]
    ga_d = [scr(f"ga{i}", [D, S_S], BF16) for i in range(3)]
    gp_d = scr("gp", [NH * 128 * GL], F32)

    with contextlib.ExitStack() as st:
        S = Sched(nc, st)
        _nm = [0]

        def sb(stack, name, shape, dt):
            _nm[0] += 1
            return stack.enter_context(nc.sbuf_tensor(f"sb{_nm[0]}_{name}", shape, dt))
        ps = st.enter_context(nc.psum_tensor("ps", [128, 8, 512], F32))
        psb = ps[:, 0:8, :].bitcast(BF16)

        ident_f = sb(st, "ident_f", [128, 128], F32)
        ident_b = sb(st, "ident_b", [128, 128], BF16)
        ones_b = sb(st, "ones_b", [128, 128], BF16)
        onesdiv = sb(st, "onesdiv", [128, 128], F32)
        onesrow = sb(st, "onesrow", [1, 128], F32)
        gpre_b = sb(st, "gpre", [128, D], F32)
        gpost_b = sb(st, "gpost", [128, D], F32)
        small = sb(st, "small", [128, 64], F32)
        subgs = sb(st, "subgs", [128, 1], F32)
        lng = sb(st, "lng", [128, 8], F32)
        wsT_b = sb(st, "wsT_b", [128, 8, 128], BF16)
        Cg = sb(st, "Cg", [128, 8, 128], F32)
        cbf = sb(st, "cbf", [128, 2, 8], F32)
        cbf8 = sb(st, "cbf8", [128, 2, 8], F32)
        CBt = sb(st, "CBt", [128, 5, 8], F32)
        CBt8 = sb(st, "CBt8", [128, 5, 8], F32)
        sel = sb(st, "sel", [128, 8], F32)
        xin = [sb(st, f"xin{i}", [128, D], F32) for i in range(2)]
        junk = sb(st, "junk", [128, D], BF16)
        hTt = [sb(st, f"hTt{i}", [128, 8, 512], BF16) for i in range(2)]
        TT = sb(st, "TT", [128, 6, 512], F32)
        stat = sb(st, "stat", [128, 16], F32)
        neglam = small[:, 0:1]
        epst = sb(st, "epst", [128, 2], F32)

        cnt = {"xin": 0, "hTt": 0, "bank": 0, "T": 0, "st": 0}

        def nxt(k, n):
            v = cnt[k] % n
            cnt[k] += 1
            return v

        def bank():
            return nxt("bank", 8)

        def bank2():
            if cnt["bank"] % 2:
                cnt["bank"] += 1
            b = cnt["bank"] % 8
            cnt["bank"] += 2
            return b

        def tmp():
            return nxt("T", 6)

        def stcol():
            return nxt("st", 16)

        def setup():
            lp = lambda dst, src, key: S.dma("sp", lambda e: e.dma_start(out=dst, in_=src), writes=[key])
            lp(ident_f[:], ident_d.ap(), "ident_f")
            lp(gpre_b[:], gpre_d.ap(), "gpre")
            lp(gpost_b[:], gpost_d.ap(), "gpost")
            lp(subgs[:], subg_d.ap(), "subg_raw")
            lp(lng[:], lng_d.ap(), "lng")
            lp(cbf[:].rearrange("p a h -> p (a h)"), cbf_d.ap(), "cbf")
            lp(sel[:], sel_d.ap(), "sel")
            S.op("dve", lambda e: e.tensor_copy(out=ident_b[:], in_=ident_f[:]), reads=["ident_f"], writes=["ident_b"])
            S.op("dve", lambda e: e.memset(ones_b[:], 1.0), writes=["ones_b"])
            S.op("dve", lambda e: e.memset(onesdiv[:], 1.0 / 128.0), writes=["onesdiv"])
            S.op("dve", lambda e: e.memset(onesrow[:], 1.0), writes=["onesrow"])
            S.op("dve", lambda e: e.memset(epst[:, 0:1], EPS), writes=["eps0"])
            S.op("dve", lambda e: e.memset(epst[:, 1:2], 4.0 * EPS), writes=["eps1"])
            S.op("dve", lambda e: e.tensor_scalar(out=subgs[:], in0=subgs[:], scalar1=0.5 * (1.0 - LAMBDA_INIT), scalar2=None,
                                                  op0=ALU.mult), reads=["subg_raw"], writes=["subg_raw"])
            with contextlib.ExitStack() as ls:
                lamv = sb(ls, "lamv", [128, 4, 64], F32)
                lnbrows = sb(ls, "lnbrows", [128, D], F32)
                wsT_f = sb(ls, "wsT_f", [128, 8, 128], F32)
                bsrow = sb(ls, "bsrow", [1, D], F32)
                relb = sb(ls, "relb", [32, 8], F32)
                oh = sb(ls, "oh", [32, GL], F32)
                G = sb(ls, "G", [8, GL], F32)
                stg = [sb(ls, f"stg{i}", [128, 2048], F32) for i in range(2)]
                stb = [sb(ls, f"stb{i}", [128, 2048], BF16) for i in range(2)]
                lp(lamv[:].rearrange("p a j -> p (a j)"), lamv_d.ap(), "lamv")
                lp(lnbrows[:], lnbrow_d.ap(), "lnbrows")
                lp(wsT_f[:].rearrange("p g q -> p (g q)"), wsT_d.ap(), "wsT_f")
                lp(bsrow[:], bsrow_d.ap(), "bsrow")
                lp(relb[:], relb_d.ap(), "relb")
                lp(oh[:], oh_d.ap(), "oh")
                for j in range(2):
                    S.op("dve", lambda e, j=j: e.tensor_tensor(out=TT[:, j, 0:64], in0=lamv[:, 2 * j, :],
                                                               in1=lamv[:, 2 * j + 1, :], op=ALU.mult),
                         reads=["lamv"], writes=[("T", j)])
                    S.op("dve", lambda e, j=j: e.reduce_sum(out=small[:, 1 + j:2 + j], in_=TT[:, j, 0:64],
                                                            axis=mybir.AxisListType.X),
                         reads=[("T", j)], writes=[("sm", 1 + j)])
                    S.op("act", lambda e, j=j: e.activation(out=small[:, 3 + j:4 + j], in_=small[:, 1 + j:2 + j],
                                                            func=AF.Exp), reads=[("sm", 1 + j)], writes=[("sm", 3 + j)])
                S.op("dve", lambda e: e.tensor_tensor(out=small[:, 5:6], in0=small[:, 3:4], in1=small[:, 4:5],
                                                      op=ALU.subtract), reads=[("sm", 3), ("sm", 4)], writes=[("sm", 5)])
                S.op("dve", lambda e: e.tensor_scalar(out=small[:, 0:1], in0=small[:, 5:6], scalar1=LAMBDA_INIT,
                                                      scalar2=-1.0, op0=ALU.add, op1=ALU.mult),
                     reads=[("sm", 5)], writes=["neglam"])
                S.op("dve", lambda e: e.tensor_copy(out=wsT_b[:], in_=wsT_f[:]), reads=["wsT_f"], writes=["wsT_b"])
                for g in range(8):
                    bk, off = g // 4, (g % 4) * 128
                    S.op("pe", lambda e, g=g, bk=bk, off=off: e.matmul(
                        ps[:, bk, off:off + 128], lhsT=lnbrows[:, g * 128:(g + 1) * 128], rhs=wsT_f[:, g, :],
                        start=True, stop=False), reads=["lnbrows", "wsT_f"], writes=[("ps", bk)])
                    S.op("pe", lambda e, g=g, bk=bk, off=off: e.matmul(
                        ps[:, bk, off:off + 128], lhsT=onesrow[0:1, :], rhs=bsrow[0:1, g * 128:(g + 1) * 128],
                        start=False, stop=True), reads=["onesrow", "bsrow"], writes=[("ps", bk)])
                S.op("dve", lambda e: e.tensor_copy(out=Cg[:].rearrange("p (a b) q -> p a (b q)", a=2),
                                                    in_=ps[:, 0:2, :]), reads=[("ps", 0), ("ps", 1)], writes=["Cg"])
                S.op("dve", lambda e: e.tensor_scalar(out=cbf8[:], in0=cbf[:], scalar1=8.0, scalar2=None, op0=ALU.mult),
                     reads=["cbf"], writes=["cbf8"])
                S.op("dve", lambda e: e.tensor_copy(out=CBt[:, 0:2, :], in_=cbf[:]), reads=["cbf"], writes=["CBt01"])
                for i in range(3):
                    S.op("dve", lambda e, i=i: e.tensor_scalar(out=TT[:, 2, 0:8], in0=cbf[:, 0, :],
                                                               scalar1=sel[:, i:i + 1], scalar2=None, op0=ALU.mult),
                         reads=["cbf", "sel"], writes=[("T", 2)])
                    S.op("dve", lambda e, i=i: e.scalar_tensor_tensor(out=CBt[:, 2 + i, :], in0=cbf[:, 1, :],
                                                                      scalar=sel[:, 3 + i:4 + i], in1=TT[:, 2, 0:8],
                                                                      op0=ALU.mult, op1=ALU.add),
                         reads=["cbf", "sel", ("T", 2)], writes=[("CBt", i)])
                S.op("dve", lambda e: e.tensor_scalar(out=CBt8[:], in0=CBt[:], scalar1=8.0, scalar2=None, op0=ALU.mult),
                     reads=["CBt01", ("CBt", 0), ("CBt", 1), ("CBt", 2)], writes=["CBt8"])
                for c0 in range(0, GL, 512):
                    w = min(512, GL - c0)
                    bk = 2 + c0 // 512
                    S.op("pe", lambda e, c0=c0, w=w, bk=bk: e.matmul(ps[0:8, bk, 0:w], lhsT=relb[:, :], rhs=oh[:, c0:c0 + w],
                                                                     start=True, stop=True),
                         reads=["relb", "oh"], writes=[("ps", bk)])
                    S.op("dve", lambda e, c0=c0, w=w, bk=bk: e.tensor_scalar(out=G[:, c0:c0 + w], in0=ps[0:8, bk, 0:w],
                                                                             scalar1=8.0, scalar2=None, op0=ALU.mult),
                         reads=[("ps", bk)], writes=[("G", c0)])
                gsrc = bass.AP(G[:].tensor, G[:].offset, [list(G[:].ap[0]), [0, 128], [1, GL]])
                S.dma("pool", lambda e: e.dma_start(out=bass.AP(gp_d, 0, [[128 * GL, 8], [GL, 128], [1, GL]]), in_=gsrc),
                      reads=[("G", 0), ("G", 512), ("G", 1024)], writes=["st_gp"], group="gp")
                jobs = []
                for h in range(NH):
                    for k0 in (0, 4):
                        src = wA_d.ap()[h, k0 * 128:(k0 + 4) * 128, :].rearrange("(kc p) n -> p kc n", p=128)
                        dst = wAb_d.ap()[h].rearrange("p (kc n) -> p kc n", kc=8)[:, k0:k0 + 4, :]
                        jobs.append((src, dst, [4, 512]))
                for kc in range(8):
                    for c0, w in ((0, 2048), (2048, 2048), (4096, 1024)):
                        src = wC_d.ap()[kc * 128:(kc + 1) * 128, c0:c0 + w]
                        dst = wCb_d.ap().rearrange("p (kc n) -> p kc n", kc=8)[:, kc, c0:c0 + w]
                        jobs.append((src, dst, [w]))
                for m in range(3):
                    for k0 in (0, 2, 4, 6):
                        src = wP_d.ap()[m, k0 * 128:(k0 + 2) * 128, :].rearrange("(kc p) n -> p kc n", p=128)
                        dst = wPb_d.ap()[m].rearrange("p (kc n) -> p kc n", kc=8)[:, k0:k0 + 2, :]
                        jobs.append((src, dst, [2, 1024]))
                for i, (src, dst, shp) in enumerate(jobs):
                    sl = i % 2
                    n = int(np.prod(shp))
                    if len(shp) == 2:
                        vf = stg[sl][:, 0:n].rearrange("p (a b) -> p a b", a=shp[0])
                        vb = stb[sl][:, 0:n].rearrange("p (a b) -> p a b", a=shp[0])
                    else:
                        vf, vb = stg[sl][:, 0:n], stb[sl][:, 0:n]
                    S.dma("sp", lambda e, vf=vf, src=src: e.dma_start(out=vf, in_=src), writes=[("stg", sl)])
                    eng = ("dve", "pool", "act")[i % 3]
                    if eng == "act":
                        S.op("act", lambda e, sl=sl, n=n: e.activation(out=stb[sl][:, 0:n], in_=stg[sl][:, 0:n], func=AF.Copy),
                             reads=[("stg", sl)], writes=[("stb", sl)])
                    else:
                        S.op(eng, lambda e, sl=sl, n=n: e.tensor_copy(out=stb[sl][:, 0:n], in_=stg[sl][:, 0:n]),
                             reads=[("stg", sl)], writes=[("stb", sl)])
                    S.dma("pool", lambda e, vb=vb, dst=dst: e.dma_start(out=dst, in_=vb), reads=[("stb", sl)], writes=[("st_wbf", sl)], group="wbf")
                S.flush()

        def phase_a(seg, x_ap, ntok):
            with contextlib.ExitStack() as as_:
                hbf = sb(as_, "hbf", [128, D], BF16)
                phase_a_body(seg, x_ap, ntok, hbf)

        def phase_a_body(seg, x_ap, ntok, hbf):
            hTv = hT_d[seg].ap().rearrange("(kc p) t -> p kc t", p=128)
            for g in range(ntok // 512):
                slot = nxt("hTt", 2)
                for sub in range(4):
                    xs_ = nxt("xin", 2)
                    r0 = g * 512 + sub * 128
                    S.dma("sp", lambda e, xs_=xs_, r0=r0: e.dma_start(out=xin[xs_][:], in_=x_ap[r0:r0 + 128, :]),
                          writes=[("xin", xs_)])
                    c = stcol()
                    S.op("act", lambda e, xs_=xs_, c=c: e.activation(out=junk[:], in_=xin[xs_][:], func=AF.Square,
                                                                     accum_out=stat[:, c:c + 1]),
                         reads=[("xin", xs_)], writes=["junk", ("st", c)])
                    c2 = stcol()
                    S.op("act", lambda e, c=c, c2=c2: e.activation(out=stat[:, c2:c2 + 1], in_=stat[:, c:c + 1], func=AF.Ln,
                                                                   bias=epst[:, 0:1], scale=1.0 / D),
                         reads=[("st", c)], writes=[("st", c2)])
                    c3 = stcol()
                    S.op("act", lambda e, c2=c2, c3=c3: e.activation(out=stat[:, c3:c3 + 1], in_=stat[:, c2:c2 + 1], func=AF.Exp,
                                                                     scale=-0.5),
                         reads=[("st", c2)], writes=[("st", c3)])
                    S.op("dve", lambda e, xs_=xs_, c3=c3: e.scalar_tensor_tensor(
                        out=hbf[:], in0=xin[xs_][:], scalar=stat[:, c3:c3 + 1], in1=gpre_b[:], op0=ALU.mult, op1=ALU.mult),
                        reads=[("xin", xs_), ("st", c3), "gpre"], writes=["hbf"])
                    for kc in range(8):
                        S.op("pe", lambda e, kc=kc, sub=sub: e.transpose(
                            out=psb[:, kc, sub * 128:(sub + 1) * 128], in_=hbf[:, kc * 128:(kc + 1) * 128], identity=ident_b[:]),
                            reads=["hbf", "ident_b"], writes=[("ps", kc)])
                S.op("act", lambda e, slot=slot: e.activation(out=hTt[slot][:, 0:4, :], in_=psb[:, 0:4, 0:512], func=AF.Copy),
                     reads=[("ps", k) for k in range(4)], writes=[("hTt", slot, 0)])
                S.op("dve", lambda e, slot=slot: e.tensor_copy(out=hTt[slot][:, 4:8, :], in_=psb[:, 4:8, 0:512]),
                     reads=[("ps", k) for k in range(4, 8)], writes=[("hTt", slot, 1)])
                S.dma("pool", lambda e, slot=slot, g=g: e.dma_start(out=hTv[:, :, g * 512:(g + 1) * 512], in_=hTt[slot][:]),
                      reads=[("hTt", slot, 0), ("hTt", slot, 1)], writes=[("st_hT", slot)], group=f"hT{seg}")
            S.flush()

        def units(seg, skv, prompt):
            hTv = hT_d[seg].ap().rearrange("(kc p) t -> p kc t", p=128)
            gav = ga_d[seg].ap()
            nkb = skv // 128
            with contextlib.ExitStack() as us:
                KT = sb(us, "KT", [128, skv], BF16)
                V = sb(us, "V", [128, nkb, 128], BF16)
                QT = sb(us, "QT", [128, S_S], BF16)
                SG = sb(us, "SG", [128, S_S], BF16)
                Wh = sb(us, "Wh", [128, 8, 512], BF16)
                E = [sb(us, f"E{i}", [128, 2, 512], BF16) for i in range(3)]
                U0 = sb(us, "U0", [128, 1152], F32)
                BB = sb(us, "BB", [128, 2, 512], F32)
                o_all = sb(us, "o_all", [128, S_S], F32)
                GA = [sb(us, f"GA{i}", [128, 512], BF16) for i in range(2)]
                u0p = list(U0[:].ap[0])

                def uwin(w):
                    return bass.AP(U0[:].tensor, U0[:, w:w + 512].offset, [u0p, [0, 2], [1, 512]])

                def bwin(j):
                    a = BB[:, j, :]
                    return bass.AP(a.tensor, a.offset, [list(a.ap[0]), [0, 2], [1, 512]])

                for h in range(NH):
                    S.dma("sp", lambda e, h=h: e.dma_start(out=Wh[:].rearrange("p k n -> p (k n)"), in_=wAb_d.ap()[h]),
                          reads=["@wbf"], writes=["Wh"])
                    S.dma("sp", lambda e, h=h: e.dma_start(out=U0[:], in_=bass.AP(gp_d, h * 128 * GL + 127,
                                                                                  [[GL - 1, 128], [1, 1152]])),
                          reads=["@gp"], writes=["U0"])
                    if prompt:
                        S.op("dve", lambda e, h=h: e.tensor_scalar(out=BB[:, 0, :], in0=U0[:, 0:512], scalar1=cbf8[:, 1, h:h + 1],
                                                                   scalar2=sel[:, 6:7], op0=ALU.subtract, op1=ALU.mult),
                             reads=["U0", "cbf8", "sel"], writes=["BB0"])
                        S.op("dve", lambda e, h=h: e.tensor_scalar(out=BB[:, 0, :], in0=BB[:, 0, :], scalar1=CBt8[:, 2, h:h + 1],
                                                                   scalar2=None, op0=ALU.add),
                             reads=["BB0", "CBt8"], writes=["BB0"])
                        S.op("dve", lambda e, h=h: e.tensor_scalar(out=BB[:, 1, :], in0=U0[:, 640:1152], scalar1=cbf8[:, 0, h:h + 1],
                                                                   scalar2=sel[:, 7:8], op0=ALU.subtract, op1=ALU.mult),
                             reads=["U0", "cbf8", "sel"], writes=["BB1"])
                        S.op("dve", lambda e, h=h: e.tensor_scalar(out=BB[:, 1, :], in0=BB[:, 1, :], scalar1=CBt8[:, 4, h:h + 1],
                                                                   scalar2=None, op0=ALU.add),
                             reads=["BB1", "CBt8"], writes=["BB1"])
                    for t in range(skv // 512):
                        slot = nxt("hTt", 2)
                        S.dma("sp", lambda e, slot=slot, t=t: e.dma_start(out=hTt[slot][:], in_=hTv[:, :, t * 512:(t + 1) * 512]),
                              reads=[f"@hT{seg}"], writes=[("hTt", slot, 0), ("hTt", slot, 1)])
                        hk = [("hTt", slot, 0), ("hTt", slot, 1)]
                        b = bank()
                        for kc in range(8):
                            S.op("pe", lambda e, b=b, kc=kc, slot=slot: e.matmul(
                                ps[:, b, :], lhsT=Wh[:, kc, 128:256], rhs=hTt[slot][:, kc, :], start=(kc == 0), stop=(kc == 7)),
                                reads=["Wh"] + hk, writes=[("ps", b)])
                        S.op("dve", lambda e, b=b, t=t: e.tensor_copy(out=KT[:, t * 512:(t + 1) * 512], in_=ps[:, b, :]),
                             reads=[("ps", b)], writes=[("KT", t)])
                        b = bank()
                        for sub in range(4):
                            for kc in range(8):
                                S.op("pe", lambda e, b=b, kc=kc, sub=sub, slot=slot: e.matmul(
                                    ps[:, b, sub * 128:(sub + 1) * 128], lhsT=hTt[slot][:, kc, sub * 128:(sub + 1) * 128],
                                    rhs=Wh[:, kc, 256:384], start=(kc == 0), stop=(kc == 7)),
                                    reads=["Wh"] + hk, writes=[("ps", b)])
                        S.op("act", lambda e, b=b, t=t: e.activation(
                            out=V[:, 4 * t:4 * t + 4, :].rearrange("p a b -> p (a b)"), in_=ps[:, b, :], func=AF.Copy),
                            reads=[("ps", b)], writes=[("V", t)])
                        if t < 8:
                            b = bank()
                            for kc in range(8):
                                S.op("pe", lambda e, b=b, kc=kc, slot=slot: e.matmul(
                                    ps[:, b, :], lhsT=Wh[:, kc, 0:128], rhs=hTt[slot][:, kc, :], start=(kc == 0), stop=(kc == 7)),
                                    reads=["Wh"] + hk, writes=[("ps", b)])
                            S.op("dve", lambda e, b=b, t=t: e.tensor_copy(out=QT[:, t * 512:(t + 1) * 512], in_=ps[:, b, :]),
                                 reads=[("ps", b)], writes=[("QT", t)])
                            b = bank()
                            for kc in range(8):
                                S.op("pe", lambda e, b=b, kc=kc, slot=slot: e.matmul(
                                    ps[:, b, :], lhsT=Wh[:, kc, 384:512], rhs=hTt[slot][:, kc, :], start=(kc == 0), stop=(kc == 7)),
                                    reads=["Wh"] + hk, writes=[("ps", b)])
                            ti = tmp()
                            S.op("act", lambda e, b=b, ti=ti: e.activation(out=TT[:, ti, :], in_=ps[:, b, :], func=AF.Tanh, scale=0.5),
                                 reads=[("ps", b)], writes=[("T", ti)])
                            S.op("dve", lambda e, b=b, ti=ti, t=t: e.scalar_tensor_tensor(
                                out=SG[:, t * 512:(t + 1) * 512], in0=TT[:, ti, :], scalar=1.0, in1=ps[:, b, :],
                                op0=ALU.add, op1=ALU.mult), reads=[("ps", b), ("T", ti)], writes=[("SG", t)])
                    steps = [(qt, kb) for qt in range(8) for kb in range(nkb)]
                    nst = len(steps)

                    def emit_scores(i):
                        qt, kb = steps[i]
                        b0 = 2 * (i % 2)
                        rk = [("KT", kb // 4), ("QT", qt)]
                        S.op("pe", lambda e: e.matmul(ps[:, b0, :], lhsT=KT[0:64, kb * 128:(kb + 1) * 128],
                                                      rhs=QT[0:64, qt * 512:(qt + 1) * 512], start=True, stop=True),
                             reads=rk, writes=[("ps", b0)])
                        S.op("pe", lambda e: e.matmul(ps[:, b0 + 1, :], lhsT=KT[64:128, kb * 128:(kb + 1) * 128],
                                                      rhs=QT[64:128, qt * 512:(qt + 1) * 512], start=True, stop=True),
                             reads=rk, writes=[("ps", b0 + 1)])
                        chunk, kbl = kb // 32, kb % 32
                        dl = kbl - 4 * qt
                        win = None
                        if chunk == 0 and -1 <= dl <= 4:
                            win = uwin(512 - 128 * dl)
                            rd = ["U0"]
                        elif prompt and chunk == 1 and qt == 7 and kbl == 0:
                            win, rd = bwin(0), ["BB0"]
                        elif prompt and chunk == 3 and qt == 0 and kbl == 31:
                            win, rd = bwin(1), ["BB1"]
                        if win is not None:
                            S.op("dve", lambda e: e.tensor_tensor(out=ps[:, b0:b0 + 2, :], in0=ps[:, b0:b0 + 2, :], in1=win, op=ALU.add),
                                 reads=rd + [("ps", b0), ("ps", b0 + 1)], writes=[("ps", b0), ("ps", b0 + 1)])
                            bias = 0.0
                            rb = []
                        else:
                            if chunk == 0:
                                ci = 0 if dl < -1 else 1
                            else:
                                ci = 1 + chunk
                            bias = CBt[:, ci, h:h + 1]
                            rb = []
                        ei = i % 3
                        S.op("act", lambda e: e.activation(out=E[ei][:], in_=ps[:, b0:b0 + 2, :], func=AF.Exp, bias=bias, scale=SCALE),
                             reads=[("ps", b0), ("ps", b0 + 1)] + rb, writes=[("E", ei)])

                    def emit_pv(i):
                        qt, kb = steps[i]
                        ei = i % 3
                        first, last = kb == 0, kb == nkb - 1
                        for m in range(2):
                            S.op("pe", lambda e, m=m: e.matmul(ps[:, 4 + m, :], lhsT=V[:, kb, :], rhs=E[ei][:, m, :],
                                                               start=first, stop=last),
                                 reads=[("V", kb // 4), ("E", ei)], writes=[("ps", 4 + m)])
                        for m in range(2):
                            S.op("pe", lambda e, m=m: e.matmul(ps[:, 6 + m, :], lhsT=ones_b[:], rhs=E[ei][:, m, :],
                                                               start=first, stop=last),
                                 reads=[("E", ei)], writes=[("ps", 6 + m)])
                        if last:
                            t2, t3, t0, t1 = tmp(), tmp(), tmp(), tmp()
                            S.op("dve", lambda e: e.tensor_copy(out=TT[:, t2, :], in_=ps[:, 4, :]), reads=[("ps", 4)], writes=[("T", t2)])
                            S.op("dve", lambda e: e.tensor_copy(out=TT[:, t3, :], in_=ps[:, 5, :]), reads=[("ps", 5)], writes=[("T", t3)])
                            S.op("dve", lambda e: e.tensor_copy(out=TT[:, t0, :], in_=ps[:, 6, :]), reads=[("ps", 6)], writes=[("T", t0)])
                            S.op("dve", lambda e: e.tensor_copy(out=TT[:, t1, :], in_=ps[:, 7, :]), reads=[("ps", 7)], writes=[("T", t1)])
                            S.op("dve", lambda e: e.reciprocal(out=TT[:, t0, :], in_=TT[:, t0, :]), reads=[("T", t0)], writes=[("T", t0)])
                            S.op("dve", lambda e: e.reciprocal(out=TT[:, t1, :], in_=TT[:, t1, :]), reads=[("T", t1)], writes=[("T", t1)])
                            S.op("dve", lambda e: e.tensor_tensor(out=TT[:, t2, :], in0=TT[:, t2, :], in1=TT[:, t0, :], op=ALU.mult),
                                 reads=[("T", t2), ("T", t0)], writes=[("T", t2)])
                            S.op("dve", lambda e: e.tensor_tensor(out=TT[:, t3, :], in0=TT[:, t3, :], in1=TT[:, t1, :], op=ALU.mult),
                                 reads=[("T", t3), ("T", t1)], writes=[("T", t3)])
                            S.op("dve", lambda e: e.scalar_tensor_tensor(out=o_all[:, qt * 512:(qt + 1) * 512], in0=TT[:, t3, :],
                                                                         scalar=neglam, in1=TT[:, t2, :], op0=ALU.mult, op1=ALU.add),
                                 reads=[("T", t2), ("T", t3)], writes=[("o", qt)])

                    emit_scores(0)
                    for i in range(nst):
                        if i + 1 < nst:
                            emit_scores(i + 1)
                        emit_pv(i)
                    cnt["bank"] = 0
                    for qt in range(8):
                        osl = o_all[:, qt * 512:(qt + 1) * 512]
                        t0, t1, t2 = tmp(), tmp(), tmp()
                        S.op("dve", lambda e, osl=osl, t0=t0: e.tensor_tensor(out=TT[:, t0, :], in0=osl, in1=osl, op=ALU.mult),
                             reads=[("o", qt)], writes=[("T", t0)])
                        b = nxt("bank", 4)
                        S.op("pe", lambda e, b=b, t0=t0: e.matmul(ps[:, b, :], lhsT=onesdiv[:], rhs=TT[:, t0, :], start=True, stop=True),
                             reads=[("T", t0)], writes=[("ps", b)])
                        S.op("act", lambda e, b=b, t1=t1: e.activation(out=TT[:, t1, :], in_=ps[:, b, :], func=AF.Ln, bias=epst[:, 0:1]),
                             reads=[("ps", b)], writes=[("T", t1)])
                        S.op("act", lambda e, t1=t1: e.activation(out=TT[:, t1, :], in_=TT[:, t1, :], func=AF.Exp, scale=-0.5),
                             reads=[("T", t1)], writes=[("T", t1)])
                        S.op("dve", lambda e, osl=osl, t1=t1, t2=t2: e.scalar_tensor_tensor(
                            out=TT[:, t2, :], in0=osl, scalar=subgs[:, 0:1], in1=TT[:, t1, :], op0=ALU.mult, op1=ALU.mult),
                            reads=[("o", qt), ("T", t1)], writes=[("T", t2)])
                        gs = qt % 2
                        S.op("dve", lambda e, t2=t2, gs=gs, qt=qt: e.tensor_tensor(out=GA[gs][:], in0=TT[:, t2, :],
                                                                                   in1=SG[:, qt * 512:(qt + 1) * 512], op=ALU.mult),
                             reads=[("T", t2), ("SG", qt)], writes=[("GA", gs)])
                        S.dma("pool", lambda e, gs=gs, qt=qt, h=h: e.dma_start(
                            out=gav[h * 128:(h + 1) * 128, qt * 512:(qt + 1) * 512], in_=GA[gs][:]),
                            reads=[("GA", gs)], writes=[("st_ga", gs)], group=f"ga{seg}")
                    S.flush()

        def phase_c(seg, x_ap, y_ap):
            hTv = hT_d[seg].ap().rearrange("(kc p) t -> p kc t", p=128)
            gav = ga_d[seg].ap().rearrange("(kc p) t -> p kc t", p=128)
            with contextlib.ExitStack() as cs:
                WC = sb(cs, "WC", [128, 8, 5120], BF16)
                WP = [sb(cs, f"WP{i}", [128, 8, 1024], BF16) for i in range(3)]
                GAin = sb(cs, "GAin", [128, 8, 512], BF16)
                gbT = sb(cs, "gbT", [128, 8, 512], BF16)
                vm = sb(cs, "vm", [128, 4, 1024], BF16)
                mT = vm[:].rearrange("p a (b c) -> p (a b) c", b=2)
                for kc in range(8):
                    S.dma("sp", lambda e, kc=kc: e.dma_start(
                        out=WC[:, kc, :], in_=wCb_d.ap().rearrange("p (kc n) -> p kc n", kc=8)[:, kc, :]),
                        reads=["@wbf"], writes=[("WC", kc)])
                for m in range(3):
                    S.dma("sp", lambda e, m=m: e.dma_start(out=WP[m][:].rearrange("p k n -> p (k n)"), in_=wPb_d.ap()[m]),
                          reads=["@wbf"], writes=[("WP", m)])
                wck = [("WC", kc) for kc in range(8)]
                for t in range(S_S // 512):
                    slot = nxt("hTt", 2)
                    S.dma("sp", lambda e, slot=slot, t=t: e.dma_start(out=hTt[slot][:], in_=hTv[:, :, t * 512:(t + 1) * 512]),
                          reads=[f"@hT{seg}"], writes=[("hTt", slot, 0), ("hTt", slot, 1)])
                    hk = [("hTt", slot, 0), ("hTt", slot, 1)]
                    S.dma("sp", lambda e, t=t: e.dma_start(out=GAin[:], in_=gav[:, :, t * 512:(t + 1) * 512]),
                          reads=[f"@ga{seg}"], writes=["GAin"])
                    for sub in range(4):
                        b = bank2()
                        for half in range(2):
                            for kc in range(8):
                                S.op("pe", lambda e, b=b, half=half, kc=kc, sub=sub, slot=slot: e.matmul(
                                    ps[:, b + half, :], lhsT=hTt[slot][:, kc, sub * 128:(sub + 1) * 128],
                                    rhs=WC[:, kc, 1024 + half * 512:1024 + (half + 1) * 512], start=(kc == 0), stop=(kc == 7)),
                                    reads=wck + hk, writes=[("ps", b + half)])
                        pk = [("ps", b), ("ps", b + 1)]
                        c1, c2 = stcol(), stcol()
                        S.op("act", lambda e, b=b, c1=c1: e.activation(out=junk[:].rearrange("p (a b) -> p a b", a=2), in_=ps[:, b:b + 2, :],
                                                                       func=AF.Identity, accum_out=stat[:, c1:c1 + 1]),
                             reads=pk, writes=["junk", ("st", c1)])
                        S.op("act", lambda e, b=b, c2=c2: e.activation(out=junk[:].rearrange("p (a b) -> p a b", a=2), in_=ps[:, b:b + 2, :],
                                                                       func=AF.Square, accum_out=stat[:, c2:c2 + 1]),
                             reads=pk, writes=["junk", ("st", c2)])
                        c3, c4, c5, c6 = stcol(), stcol(), stcol(), stcol()
                        S.op("dve", lambda e, c1=c1, c3=c3: e.tensor_scalar(out=stat[:, c3:c3 + 1], in0=stat[:, c1:c1 + 1], scalar1=1.0 / D,
                                                                            scalar2=None, op0=ALU.mult), reads=[("st", c1)], writes=[("st", c3)])
                        S.op("dve", lambda e, c3=c3, c4=c4: e.scalar_tensor_tensor(out=stat[:, c4:c4 + 1], in0=stat[:, c3:c3 + 1], scalar=-1.0,
                                                                                   in1=stat[:, c3:c3 + 1], op0=ALU.mult, op1=ALU.mult),
                             reads=[("st", c3)], writes=[("st", c4)])
                        S.op("dve", lambda e, c2=c2, c4=c4, c5=c5: e.scalar_tensor_tensor(out=stat[:, c5:c5 + 1], in0=stat[:, c2:c2 + 1],
                                                                                         scalar=1.0 / D, in1=stat[:, c4:c4 + 1],
                                                                                         op0=ALU.mult, op1=ALU.add),
                             reads=[("st", c2), ("st", c4)], writes=[("st", c5)])
                        S.op("act", lambda e, c5=c5: e.activation(out=stat[:, c5:c5 + 1], in_=stat[:, c5:c5 + 1], func=AF.Ln,
                                                                  bias=epst[:, 0:1]),
                             reads=[("st", c5)], writes=[("st", c5)])
                        S.op("act", lambda e, c5=c5, c6=c6: e.activation(out=stat[:, c6:c6 + 1], in_=stat[:, c5:c5 + 1], func=AF.Exp,
                                                                         scale=-0.5),
                             reads=[("st", c5)], writes=[("st", c6)])
                        c7 = stcol()
                        S.op("dve", lambda e, c3=c3, c6=c6, c7=c7: e.scalar_tensor_tensor(out=stat[:, c7:c7 + 1], in0=stat[:, c3:c3 + 1],
                                                                                         scalar=-1.0, in1=stat[:, c6:c6 + 1],
                                                                                         op0=ALU.mult, op1=ALU.mult),
                             reads=[("st", c3), ("st", c6)], writes=[("st", c7)])
                        S.op("act", lambda e, b=b, sub=sub, c6=c6, c7=c7: e.activation(
                            out=vm[:, sub, :].rearrange("p (a b) -> p a b", a=2), in_=ps[:, b:b + 2, :], func=AF.Identity,
                            bias=stat[:, c7:c7 + 1], scale=stat[:, c6:c6 + 1]),
                            reads=pk + [("st", c6), ("st", c7)], writes=[("vm", sub)])
                    for g in range(8):
                        bs_ = bank()
                        for sub in range(4):
                            S.op("pe", lambda e, bs_=bs_, sub=sub, g=g: e.matmul(
                                ps[:, bs_, sub * 128:(sub + 1) * 128], lhsT=vm[:, sub, g * 128:(g + 1) * 128], rhs=wsT_b[:, g, :],
                                start=True, stop=True), reads=[("vm", sub), "wsT_b"], writes=[("ps", bs_)])
                        bu = bank()
                        for kc in range(8):
                            S.op("pe", lambda e, bu=bu, kc=kc, g=g, slot=slot: e.matmul(
                                ps[:, bu, :], lhsT=WC[:, kc, g * 128:(g + 1) * 128], rhs=hTt[slot][:, kc, :], start=(kc == 0), stop=(kc == 7)),
                                reads=wck + hk, writes=[("ps", bu)])
                        bg = bank()
                        for kc in range(8):
                            S.op("pe", lambda e, bg=bg, kc=kc, g=g, slot=slot: e.matmul(
                                ps[:, bg, :], lhsT=WC[:, kc, 2048 + g * 128:2048 + (g + 1) * 128], rhs=hTt[slot][:, kc, :],
                                start=(kc == 0), stop=(kc == 7)), reads=wck + hk, writes=[("ps", bg)])
                        ta, tb, tc = tmp(), tmp(), tmp()
                        cga = Cg[:, g, :]
                        cgw = bass.AP(cga.tensor, cga.offset, [list(cga.ap[0]), [0, 4], [1, 128]])
                        S.op("dve", lambda e, bs_=bs_, ta=ta, g=g, cgw=cgw: e.scalar_tensor_tensor(
                            out=TT[:, ta, :].rearrange("p (a b) -> p a b", a=4), in0=ps[:, bs_, :].rearrange("p (a b) -> p a b", a=4),
                            scalar=lng[:, g:g + 1], in1=cgw, op0=ALU.mult, op1=ALU.add),
                            reads=[("ps", bs_), "Cg", "lng"], writes=[("T", ta)])
                        S.op("act", lambda e, bg=bg, tb=tb: e.activation(out=TT[:, tb, :], in_=ps[:, bg, :], func=AF.Tanh, scale=0.5),
                             reads=[("ps", bg)], writes=[("T", tb)])
                        S.op("dve", lambda e, bg=bg, tb=tb: e.scalar_tensor_tensor(out=TT[:, tb, :], in0=TT[:, tb, :], scalar=1.0, in1=ps[:, bg, :],
                                                                                   op0=ALU.add, op1=ALU.mult),
                             reads=[("ps", bg), ("T", tb)], writes=[("T", tb)])
                        S.op("dve", lambda e, bu=bu, ta=ta, tc=tc: e.tensor_tensor(out=TT[:, tc, :], in0=ps[:, bu, :], in1=TT[:, ta, :], op=ALU.mult),
                             reads=[("ps", bu), ("T", ta)], writes=[("T", tc)])
                        S.op("dve", lambda e, tb=tb, tc=tc, g=g: e.scalar_tensor_tensor(out=gbT[:, g, :], in0=TT[:, tc, :], scalar=0.5, in1=TT[:, tb, :],
                                                                                        op0=ALU.mult, op1=ALU.mult),
                             reads=[("T", tb), ("T", tc)], writes=[("gbT", g)])
                    gbk = [("gbT", g) for g in range(8)]
                    vmk = [("vm", s_) for s_ in range(4)]
                    for n in range(8):
                        bma, bya, bmb, byb = bank(), bank(), bank(), bank()
                        for kc in range(8):
                            S.op("pe", lambda e, kc=kc, n=n, bma=bma, slot=slot: e.matmul(
                                ps[:, bma, :], lhsT=WC[:, kc, 3072 + n * 128:3072 + (n + 1) * 128], rhs=hTt[slot][:, kc, :],
                                start=(kc == 0), stop=(kc == 7)), reads=wck + hk, writes=[("ps", bma)])
                        for kc in range(8):
                            S.op("pe", lambda e, kc=kc, n=n, bya=bya: e.matmul(
                                ps[:, bya, :], lhsT=WP[0][:, kc, n * 128:(n + 1) * 128], rhs=GAin[:, kc, :],
                                start=(kc == 0), stop=(kc == 7)), reads=[("WP", 0), "GAin"], writes=[("ps", bya)])
                        for kc in range(8):
                            S.op("pe", lambda e, kc=kc, n=n, bmb=bmb, slot=slot: e.matmul(
                                ps[:, bmb, :], lhsT=WC[:, kc, 4096 + n * 128:4096 + (n + 1) * 128], rhs=hTt[slot][:, kc, :],
                                start=(kc == 0), stop=(kc == 7)), reads=wck + hk, writes=[("ps", bmb)])
                        for kc in range(8):
                            S.op("pe", lambda e, kc=kc, n=n, byb=byb: e.matmul(
                                ps[:, byb, :], lhsT=WP[1][:, kc, n * 128:(n + 1) * 128], rhs=gbT[:, kc, :],
                                start=(kc == 0), stop=(kc == 7)), reads=[("WP", 1)] + gbk, writes=[("ps", byb)])
                        ta, tb = tmp(), tmp()
                        for (bm, by, tx) in ((bma, bya, ta), (bmb, byb, tb)):
                            S.op("act", lambda e, bm=bm, tx=tx: e.activation(out=TT[:, tx, :], in_=ps[:, bm, :], func=AF.Tanh, scale=0.5),
                                 reads=[("ps", bm)], writes=[("T", tx)])
                            S.op("dve", lambda e, by=by, tx=tx: e.scalar_tensor_tensor(out=TT[:, tx, :], in0=TT[:, tx, :], scalar=1.0, in1=ps[:, by, :],
                                                                                       op0=ALU.add, op1=ALU.mult),
                                 reads=[("ps", by), ("T", tx)], writes=[("T", tx)])
                        S.op("dve", lambda e, ta=ta, tb=tb, n=n: e.tensor_tensor(out=mT[:, n, :], in0=TT[:, ta, :], in1=TT[:, tb, :], op=ALU.add),
                             reads=[("T", ta), ("T", tb)], writes=[("mT", n)] + vmk)
                    mk_ = [("mT", n) for n in range(8)]
                    for sub in range(4):
                        xs_ = nxt("xin", 2)
                        r0 = t * 512 + sub * 128
                        S.dma("sp", lambda e, xs_=xs_, r0=r0: e.dma_start(out=xin[xs_][:], in_=x_ap[r0:r0 + 128, :]), writes=[("xin", xs_)])
                        b = bank2()
                        for half in range(2):
                            for kc in range(8):
                                S.op("pe", lambda e, b=b, half=half, kc=kc, sub=sub: e.matmul(
                                    ps[:, b + half, :], lhsT=mT[:, kc, sub * 128:(sub + 1) * 128], rhs=WP[2][:, kc, half * 512:(half + 1) * 512],
                                    start=(kc == 0), stop=(kc == 7)), reads=[("WP", 2)] + mk_ + vmk, writes=[("ps", b + half)])
                        pk = [("ps", b), ("ps", b + 1)]
                        c1, c2, c3 = stcol(), stcol(), stcol()
                        S.op("act", lambda e, b=b, c1=c1: e.activation(out=junk[:].rearrange("p (a b) -> p a b", a=2), in_=ps[:, b:b + 2, :],
                                                                       func=AF.Square, accum_out=stat[:, c1:c1 + 1]),
                             reads=pk, writes=["junk", ("st", c1)])
                        S.op("act", lambda e, c1=c1, c2=c2: e.activation(out=stat[:, c2:c2 + 1], in_=stat[:, c1:c1 + 1], func=AF.Ln,
                                                                         bias=epst[:, 1:2], scale=1.0 / D),
                             reads=[("st", c1)], writes=[("st", c2)])
                        S.op("act", lambda e, c2=c2, c3=c3: e.activation(out=stat[:, c3:c3 + 1], in_=stat[:, c2:c2 + 1], func=AF.Exp,
                                                                         scale=-0.5),
                             reads=[("st", c2)], writes=[("st", c3)])
                        if cnt["T"] % 2:
                            cnt["T"] += 1
                        ta = tmp()
                        tmp()
                        S.op("dve", lambda e, b=b, c3=c3, ta=ta: e.scalar_tensor_tensor(
                            out=TT[:, ta:ta + 2, :], in0=ps[:, b:b + 2, :], scalar=stat[:, c3:c3 + 1],
                            in1=gpost_b[:].rearrange("p (a b) -> p a b", a=2), op0=ALU.mult, op1=ALU.mult),
                            reads=pk + [("st", c3), "gpost"], writes=[("T", ta), ("T", ta + 1)])
                        S.op("dve", lambda e, xs_=xs_, ta=ta: e.tensor_tensor(out=xin[xs_][:].rearrange("p (a b) -> p a b", a=2),
                                                                              in0=TT[:, ta:ta + 2, :],
                                                                              in1=xin[xs_][:].rearrange("p (a b) -> p a b", a=2), op=ALU.add),
                             reads=[("T", ta), ("T", ta + 1), ("xin", xs_)], writes=[("xin", xs_)])
                        S.dma("pool", lambda e, xs_=xs_, r0=r0: e.dma_start(out=y_ap[r0:r0 + 128, :], in_=xin[xs_][:]),
                              reads=[("xin", xs_)], writes=[("st_out", xs_)], group="out")
                S.flush()

        _orig_deps = S._deps

        def _deps(reads, writes):
            real = [r for r in reads if not (isinstance(r, str) and r.startswith("@"))]
            deps = _orig_deps(real, writes)
            for r in reads:
                if isinstance(r, str) and r.startswith("@"):
                    for key in S.groups[r[1:]]:
                        dd = S.dsem[key]
                        deps.append(Tok("dma", dd[0], dd[1]))
            return deps

        _orig_commit = S._commit

        def _commit(tok, reads, writes):
            real = [r for r in reads if not (isinstance(r, str) and r.startswith("@"))]
            _orig_commit(tok, real, writes)

        S._deps, S._commit = _deps, _commit

        setup()
        segs = [(0, xs_d.ap()[0], S_S, False, ys_d.ap()[0]), (1, xs_d.ap()[1], S_S, False, ys_d.ap()[1]),
                (2, xp_d.ap(), S_P, True, yp_d.ap())]
        for seg, x_ap, skv, prompt, y_ap in segs:
            phase_a(seg, x_ap, skv)
            units(seg, skv, prompt)
            phase_c(seg, x_ap, y_ap)
    return nc


def _bucket(rel):
    half, max_exact = 16, 8
    n = np.abs(rel)
    nf = np.maximum(n, 1).astype(np.float32)
    large = max_exact + (np.log(nf / np.float32(max_exact)) / np.float32(math.log(128 / max_exact))
                         * np.float32(half - max_exact)).astype(np.int32)
    large = np.minimum(large, half - 1)
    return np.where(rel > 0, half, 0) + np.where(n < max_exact, n, large)


_CACHE = {}


def kernel(x_prompt, x_sample, g_pre, w_in, lambda_q1, lambda_k1, lambda_q2, lambda_k2, subln_g,
           w_pa, ln_g, ln_b, w_s, b_s, w_pb, w_o, g_post, rel_bias):
    f = lambda a: np.ascontiguousarray(np.asarray(a, dtype=np.float32))
    x_prompt, x_sample, w_in = f(x_prompt), f(x_sample), f(w_in)[0]
    rep = lambda v, n=128: np.ascontiguousarray(np.broadcast_to(f(v).reshape(1, -1), (n, f(v).size)))
    wA = np.empty((NH, D, 512), np.float32)
    for h in range(NH):
        wA[h, :, 0:64] = w_in[:, h * 64:(h + 1) * 64]
        wA[h, :, 64:128] = w_in[:, 512 + h * 64:512 + (h + 1) * 64]
        wA[h, :, 128:192] = w_in[:, 1024 + h * 64:1024 + (h + 1) * 64]
        wA[h, :, 192:256] = w_in[:, 1536 + h * 64:1536 + (h + 1) * 64]
        wA[h, :, 256:384] = w_in[:, 2048 + h * 128:2048 + (h + 1) * 128]
        wA[h, :, 384:512] = w_in[:, 3072 + h * 128:3072 + (h + 1) * 128]
    wC = np.ascontiguousarray(w_in[:, 4096:9216])
    wP = np.stack([f(w_pa)[0], f(w_pb)[0], f(w_o)[0]])
    lamv = np.concatenate([rep(lambda_q1), rep(lambda_k1), rep(lambda_q2), rep(lambda_k2)], axis=1)
    subg = f(subln_g).reshape(128, 1)
    lng = np.ascontiguousarray(f(ln_g).reshape(8, 128).T)
    wsT = np.ascontiguousarray(f(w_s)[0].transpose(2, 0, 1).reshape(128, 8 * 128))
    bsrow = f(b_s).reshape(1, D)
    relb = f(rel_bias)
    cbf = np.concatenate([rep(relb[15]), rep(relb[31])], axis=1)
    j = np.arange(GL)
    oh = (np.arange(32)[:, None] == _bucket(639 - j)[None, :]).astype(np.float32)
    ident = np.eye(128, dtype=np.float32)
    common = dict(wA=wA, wC=wC, wP=wP, gpre_b=rep(g_pre), gpost_b=rep(g_post), lamv=lamv, subg=subg, lng=lng,
                  lnb_rows=rep(ln_b), wsT=wsT, bsrow=bsrow, relb=relb, cbf=cbf, oh=oh, ident=ident)
    in_maps = []
    for c in range(8):
        pb, pq = c // 4, c % 4
        xp = np.ascontiguousarray(np.roll(x_prompt[pb], -pq * S_S, axis=0))
        selb = [1.0 if pq == 3 else 0.0, 1.0 if pq >= 2 else 0.0, 1.0 if pq >= 1 else 0.0]
        selv = selb + [1.0 - s for s in selb] + [1.0 if pq <= 2 else 0.0, 1.0 if pq >= 1 else 0.0]
        sel = np.ascontiguousarray(np.broadcast_to(np.array(selv, np.float32)[None, :], (128, 8)))
        m = dict(common)
        m.update(xs=np.ascontiguousarray(x_sample[2 * c:2 * c + 2]), xp=xp, sel=sel)
        in_maps.append(m)
    if "nc" not in _CACHE:
        _CACHE["nc"] = build_program()
    res = run_bass_kernel_spmd(_CACHE["nc"], in_maps, core_ids=list(range(8)))
    y_prompt = np.empty((2, S_P, D), np.float32)
    y_sample = np.empty((16, S_S, D), np.float32)
    for c in range(8):
        r = res.results[c]
        pb, pq = c // 4, c % 4
        y_prompt[pb, pq * S_S:(pq + 1) * S_S] = r["yp"]
        y_sample[2 * c:2 * c + 2] = r["ys"]
    return (y_prompt, y_sample)
```

```python
import contextlib
import math
import numpy as np
import concourse.bass as bass
import concourse.mybir as mybir
from concourse.bass_utils import run_bass_kernel_spmd

F32 = mybir.dt.float32
BF16 = mybir.dt.bfloat16
AF = mybir.ActivationFunctionType
ALU = mybir.AluOpType

D = 1024
NH = 8
S_S = 4096
S_P = 16384
EPS = 1e-6
LAMBDA_INIT = 0.8 - 0.6 * math.exp(-0.3 * 0)
SCALE = 0.125
GL = 1280
EPOCH = 20000


class Tok:
    __slots__ = ("eng", "sem", "val", "group")

    def __init__(self, eng, sem, val, group=None):
        self.eng, self.sem, self.val, self.group = eng, sem, val, group


class Op:
    __slots__ = ("fn", "deps", "tok", "is_dma")

    def __init__(self, fn, deps, tok, is_dma):
        self.fn, self.deps, self.tok, self.is_dma = fn, deps, tok, is_dma


class Sched:
    COMPUTE = ("pe", "act", "dve", "pool")
    ALL = ("pe", "act", "dve", "pool", "sp")
    ATTR = {"pe": "tensor", "act": "scalar", "dve": "vector", "pool": "gpsimd", "sp": "sync"}

    def __init__(self, nc, stack):
        self.nc, self.stack = nc, stack
        self.ops = {e: [] for e in self.ALL}
        self.count = {e: 0 for e in self.COMPUTE}
        self.esems = {e: [] for e in self.COMPUTE}
        self.res = {}
        self.dsem = {}
        self.groups = {}
        self.nsem = 0
        self.waited = {e: {} for e in self.ALL}
        self.group_final = set()

    def _newsem(self, name):
        self.nsem += 1
        return self.stack.enter_context(self.nc.semaphore(f"s{self.nsem}_{name}"))

    def _deps(self, reads, writes):
        deps = []
        for r in reads:
            e = self.res.get(r)
            if e and e[0] is not None:
                deps.append(e[0])
        for w in writes:
            e = self.res.get(w)
            if e:
                if e[0] is not None:
                    deps.append(e[0])
                deps.extend(e[1].values())
        return deps

    def _commit(self, tok, reads, writes):
        for r in reads:
            e = self.res.setdefault(r, [None, {}])
            e[1][tok.eng if tok.eng != "dma" else id(tok.sem)] = tok
        for w in writes:
            self.res[w] = [tok, {}]

    def op(self, eng, fn, reads=(), writes=()):
        deps = self._deps(reads, writes)
        n = self.count[eng]
        ep = n // EPOCH
        if ep >= len(self.esems[eng]):
            self.esems[eng].append(self._newsem(f"{eng}{ep}"))
        tok = Tok(eng, self.esems[eng][ep], n - ep * EPOCH + 1)
        self.count[eng] = n + 1
        if eng == "pe":
            deps = [d for d in deps if d.eng != "pe"]
        self.ops[eng].append(Op(fn, deps, tok, False))
        self._commit(tok, reads, writes)
        return tok

    def dma(self, queue, fn, reads=(), writes=(), group=None):
        deps = self._deps(reads, writes)
        key = writes[0]
        d = self.dsem.get(key)
        if d is None:
            d = self.dsem[key] = [self._newsem("d"), 0]
        d[1] += 16
        tok = Tok("dma", d[0], d[1])
        if group is not None:
            self.groups.setdefault(group, set()).add(key)
        self.ops[queue].append(Op(fn, deps, tok, True))
        self._commit(tok, reads, writes)
        return tok

    def _resolve(self, d):
        return d.sem, d.val

    def flush(self, final_groups=()):
        nc = self.nc
        bar = []
        for e in self.COMPUTE:
            n = self.count[e]
            if n:
                ep = (n - 1) // EPOCH
                bar.append((self.esems[e][ep], n - ep * EPOCH))
        for d in self.dsem.values():
            bar.append((d[0], d[1]))

        def run(engname, e):
            waited = self.waited[engname]
            for op in self.ops[engname]:
                for d in op.deps:
                    sem, val = self._resolve(d)
                    k = id(sem)
                    if waited.get(k, 0) >= val:
                        continue
                    waited[k] = val
                    e.wait_ge(sem, val)
                ins = op.fn(e)
                ins.then_inc(op.tok.sem, 16 if op.is_dma else 1)
            for sem, val in bar:
                k = id(sem)
                if waited.get(k, 0) >= val:
                    continue
                waited[k] = val
                e.wait_ge(sem, val)
            self.ops[engname] = []

        with nc.Block() as block:
            for engname in self.ALL:
                def mk(engname=engname):
                    def f(e):
                        run(engname, e)
                    return f
                getattr(block, self.ATTR[engname])(mk())
        self.res = {}


def build_program():
    nc = bass.Bass("TRN2", target_bir_lowering=False)
    dt_in = lambda name, shape: nc.dram_tensor(name, shape, F32, kind="ExternalInput")
    xs_d = dt_in("xs", [2, S_S, D])
    xp_d = dt_in("xp", [S_P, D])
    wA_d = dt_in("wA", [NH, D, 512])
    wC_d = dt_in("wC", [D, 5120])
    wP_d = dt_in("wP", [3, D, D])
    gpre_d = dt_in("gpre_b", [128, D])
    gpost_d = dt_in("gpost_b", [128, D])
    lamv_d = dt_in("lamv", [128, 4 * 64])
    subg_d = dt_in("subg", [128, 1])
    lng_d = dt_in("lng", [128, 8])
    lnbrow_d = dt_in("lnb_rows", [128, D])
    wsT_d = dt_in("wsT", [128, 8 * 128])
    bsrow_d = dt_in("bsrow", [1, D])
    relb_d = dt_in("relb", [32, 8])
    cbf_d = dt_in("cbf", [128, 16])
    sel_d = dt_in("sel", [128, 8])
    oh_d = dt_in("oh", [32, GL])
    ident_d = dt_in("ident", [128, 128])
    ys_d = nc.dram_tensor("ys", [2, S_S, D], F32, kind="ExternalOutput")
    yp_d = nc.dram_tensor("yp", [S_S, D], F32, kind="ExternalOutput")

    scr = lambda name, shape, dt: nc.dram_tensor(name, shape, dt, kind="Internal")
    wAb_d = scr("wAb", [NH, 128, 8 * 512], BF16)
    wCb_d = scr("wCb", [128, 8 * 5120], BF16)
    wPb_d = scr("wPb", [3, 128, 8 * 1024], BF16)
    hT_d = [scr("hT0", [D, S_S], BF16), scr("hT1", [D, S_S], BF16), scr("hT2", [D, S_P], BF16)]
    ga_d = [scr(f"ga{i}", [D, S_S], BF16) for i in range(3)]
    gp_d = scr("gp", [NH * 128 * GL], F32)

    with contextlib.ExitStack() as st:
        S = Sched(nc, st)
        _nm = [0]

        def sb(stack, name, shape, dt):
            _nm[0] += 1
            return stack.enter_context(nc.sbuf_tensor(f"sb{_nm[0]}_{name}", shape, dt))
        ps = st.enter_context(nc.psum_tensor("ps", [128, 8, 512], F32))
        psb = ps[:, 0:8, :].bitcast(BF16)

        ident_b = sb(st, "ident_b", [128, 128], BF16)
        ones_b = sb(st, "ones_b", [128, 128], BF16)
        onesdiv = sb(st, "onesdiv", [128, 128], F32)
        gpre_b = sb(st, "gpre", [128, D], F32)
        gpost_b = sb(st, "gpost", [128, D], F32)
        small = sb(st, "small", [128, 64], F32)
        subgs = sb(st, "subgs", [128, 1], F32)
        lng = sb(st, "lng", [128, 8], F32)
        wsT_b = sb(st, "wsT_b", [128, 8, 128], BF16)
        Cg = sb(st, "Cg", [128, 8, 128], F32)
        cbf = sb(st, "cbf", [128, 2, 8], F32)
        cbf8 = sb(st, "cbf8", [128, 2, 8], F32)
        CBt = sb(st, "CBt", [128, 5, 8], F32)
        CBt8 = sb(st, "CBt8", [128, 5, 8], F32)
        sel = sb(st, "sel", [128, 8], F32)
        xin = [sb(st, f"xin{i}", [128, D], F32) for i in range(2)]
        junk = sb(st, "junk", [128, D], BF16)
        hTt = [sb(st, f"hTt{i}", [128, 8, 512], BF16) for i in range(2)]
        TT = sb(st, "TT", [128, 6, 512], F32)
        stat = sb(st, "stat", [128, 16], F32)
        neglam = small[:, 0:1]
        epst = sb(st, "epst", [128, 2], F32)
        sel2 = sb(st, "sel2", [128, 2, 128], F32)

        cnt = {"xin": 0, "hTt": 0, "bank": 0, "T": 0, "st": 0, "GA": 0}

        def nxt(k, n):
            v = cnt[k] % n
            cnt[k] += 1
            return v

        def bank():
            return nxt("bank", 8)

        def bank2():
            if cnt["bank"] % 2:
                cnt["bank"] += 1
            b = cnt["bank"] % 8
            cnt["bank"] += 2
            return b

        def tmp():
            return nxt("T", 6)

        def stcol():
            return nxt("st", 16)

        def setup():
            lp = lambda dst, src, key: S.dma("sp", lambda e: e.dma_start(out=dst, in_=src), writes=[key])
            lp(gpre_b[:], gpre_d.ap(), "gpre")
            lp(gpost_b[:], gpost_d.ap(), "gpost")
            lp(subgs[:], subg_d.ap(), "subg_raw")
            lp(lng[:], lng_d.ap(), "lng")
            lp(cbf[:].rearrange("p a h -> p (a h)"), cbf_d.ap(), "cbf")
            lp(sel[:], sel_d.ap(), "sel")
            S.op("dve", lambda e: e.memset(ones_b[:], 1.0), writes=["ones_b"])
            S.op("dve", lambda e: e.memset(onesdiv[:], 1.0 / 128.0), writes=["onesdiv"])
            S.op("dve", lambda e: e.memset(sel2[:], 0.0), writes=["sel2"])
            for m_, rows_ in ((0, (0, 64)), (1, (32, 96))):
                for r_ in rows_:
                    S.op("dve", lambda e, m_=m_, r_=r_: e.memset(sel2[r_:r_ + 1, m_, :], 1.0), reads=["sel2"], writes=["sel2"])
            S.op("dve", lambda e: e.memset(epst[:, 0:1], EPS), writes=["eps0"])
            S.op("dve", lambda e: e.memset(epst[:, 1:2], 4.0 * EPS), writes=["eps1"])
            S.op("dve", lambda e: e.tensor_scalar(out=subgs[:], in0=subgs[:], scalar1=0.5 * (1.0 - LAMBDA_INIT), scalar2=None,
                                                  op0=ALU.mult), reads=["subg_raw"], writes=["subg_raw"])
            with contextlib.ExitStack() as ls:
                lamv = sb(ls, "lamv", [128, 4, 64], F32)
                ident_f = sb(ls, "ident_f", [128, 128], F32)
                onesrow = sb(ls, "onesrow", [1, 128], F32)
                lp(ident_f[:], ident_d.ap(), "ident_f")
                S.op("dve", lambda e: e.tensor_copy(out=ident_b[:], in_=ident_f[:]), reads=["ident_f"], writes=["ident_b"])
                S.op("dve", lambda e: e.memset(onesrow[:], 1.0), writes=["onesrow"])
                lnbrows = sb(ls, "lnbrows", [128, D], F32)
                wsT_f = sb(ls, "wsT_f", [128, 8, 128], F32)
                bsrow = sb(ls, "bsrow", [1, D], F32)
                relb = sb(ls, "relb", [32, 8], F32)
                oh = sb(ls, "oh", [32, GL], F32)
                G = sb(ls, "G", [8, GL], F32)
                stg = [sb(ls, f"stg{i}", [128, 2048], F32) for i in range(2)]
                stb = [sb(ls, f"stb{i}", [128, 2048], BF16) for i in range(2)]
                lp(lamv[:].rearrange("p a j -> p (a j)"), lamv_d.ap(), "lamv")
                lp(lnbrows[:], lnbrow_d.ap(), "lnbrows")
                lp(wsT_f[:].rearrange("p g q -> p (g q)"), wsT_d.ap(), "wsT_f")
                lp(bsrow[:], bsrow_d.ap(), "bsrow")
                lp(relb[:], relb_d.ap(), "relb")
                lp(oh[:], oh_d.ap(), "oh")
                for j in range(2):
                    S.op("dve", lambda e, j=j: e.tensor_tensor(out=TT[:, j, 0:64], in0=lamv[:, 2 * j, :],
                                                               in1=lamv[:, 2 * j + 1, :], op=ALU.mult),
                         reads=["lamv"], writes=[("T", j)])
                    S.op("dve", lambda e, j=j: e.reduce_sum(out=small[:, 1 + j:2 + j], in_=TT[:, j, 0:64],
                                                            axis=mybir.AxisListType.X),
                         reads=[("T", j)], writes=[("sm", 1 + j)])
                    S.op("act", lambda e, j=j: e.activation(out=small[:, 3 + j:4 + j], in_=small[:, 1 + j:2 + j],
                                                            func=AF.Exp), reads=[("sm", 1 + j)], writes=[("sm", 3 + j)])
                S.op("dve", lambda e: e.tensor_tensor(out=small[:, 5:6], in0=small[:, 3:4], in1=small[:, 4:5],
                                                      op=ALU.subtract), reads=[("sm", 3), ("sm", 4)], writes=[("sm", 5)])
                S.op("dve", lambda e: e.tensor_scalar(out=small[:, 0:1], in0=small[:, 5:6], scalar1=LAMBDA_INIT,
                                                      scalar2=-1.0, op0=ALU.add, op1=ALU.mult),
                     reads=[("sm", 5)], writes=["neglam"])
                S.op("dve", lambda e: e.tensor_copy(out=wsT_b[:], in_=wsT_f[:]), reads=["wsT_f"], writes=["wsT_b"])
                for g in range(8):
                    bk, off = g // 4, (g % 4) * 128
                    S.op("pe", lambda e, g=g, bk=bk, off=off: e.matmul(
                        ps[:, bk, off:off + 128], lhsT=lnbrows[:, g * 128:(g + 1) * 128], rhs=wsT_f[:, g, :],
                        start=True, stop=False), reads=["lnbrows", "wsT_f"], writes=[("ps", bk)])
                    S.op("pe", lambda e, g=g, bk=bk, off=off: e.matmul(
                        ps[:, bk, off:off + 128], lhsT=onesrow[0:1, :], rhs=bsrow[0:1, g * 128:(g + 1) * 128],
                        start=False, stop=True), reads=["onesrow", "bsrow"], writes=[("ps", bk)])
                S.op("dve", lambda e: e.tensor_copy(out=Cg[:].rearrange("p (a b) q -> p a (b q)", a=2),
                                                    in_=ps[:, 0:2, :]), reads=[("ps", 0), ("ps", 1)], writes=["Cg"])
                S.op("dve", lambda e: e.tensor_scalar(out=cbf8[:], in0=cbf[:], scalar1=8.0, scalar2=None, op0=ALU.mult),
                     reads=["cbf"], writes=["cbf8"])
                S.op("dve", lambda e: e.tensor_copy(out=CBt[:, 0:2, :], in_=cbf[:]), reads=["cbf"], writes=["CBt01"])
                for i in range(3):
                    S.op("dve", lambda e, i=i: e.tensor_scalar(out=TT[:, 2, 0:8], in0=cbf[:, 0, :],
                                                               scalar1=sel[:, i:i + 1], scalar2=None, op0=ALU.mult),
                         reads=["cbf", "sel"], writes=[("T", 2)])
                    S.op("dve", lambda e, i=i: e.scalar_tensor_tensor(out=CBt[:, 2 + i, :], in0=cbf[:, 1, :],
                                                                      scalar=sel[:, 3 + i:4 + i], in1=TT[:, 2, 0:8],
                                                                      op0=ALU.mult, op1=ALU.add),
                         reads=["cbf", "sel", ("T", 2)], writes=[("CBt", i)])
                S.op("dve", lambda e: e.tensor_scalar(out=CBt8[:], in0=CBt[:], scalar1=8.0, scalar2=None, op0=ALU.mult),
                     reads=["CBt01", ("CBt", 0), ("CBt", 1), ("CBt", 2)], writes=["CBt8"])
                for c0 in range(0, GL, 512):
                    w = min(512, GL - c0)
                    bk = 2 + c0 // 512
                    S.op("pe", lambda e, c0=c0, w=w, bk=bk: e.matmul(ps[0:8, bk, 0:w], lhsT=relb[:, :], rhs=oh[:, c0:c0 + w],
                                                                     start=True, stop=True),
                         reads=["relb", "oh"], writes=[("ps", bk)])
                    S.op("dve", lambda e, c0=c0, w=w, bk=bk: e.tensor_scalar(out=G[:, c0:c0 + w], in0=ps[0:8, bk, 0:w],
                                                                             scalar1=8.0, scalar2=None, op0=ALU.mult),
                         reads=[("ps", bk)], writes=[("G", c0)])
                gsrc = bass.AP(G[:].tensor, G[:].offset, [list(G[:].ap[0]), [0, 128], [1, GL]])
                S.dma("pool", lambda e: e.dma_start(out=bass.AP(gp_d, 0, [[128 * GL, 8], [GL, 128], [1, GL]]), in_=gsrc),
                      reads=[("G", 0), ("G", 512), ("G", 1024)], writes=["st_gp"], group="gp")
                jobs = []
                for h in range(NH):
                    for k0 in (0, 4):
                        src = wA_d.ap()[h, k0 * 128:(k0 + 4) * 128, :].rearrange("(kc p) n -> p kc n", p=128)
                        dst = wAb_d.ap()[h].rearrange("p (kc n) -> p kc n", kc=8)[:, k0:k0 + 4, :]
                        jobs.append((src, dst, [4, 512]))
                for kc in range(8):
                    for c0, w in ((0, 2048), (2048, 2048), (4096, 1024)):
                        src = wC_d.ap()[kc * 128:(kc + 1) * 128, c0:c0 + w]
                        dst = wCb_d.ap().rearrange("p (kc n) -> p kc n", kc=8)[:, kc, c0:c0 + w]
                        jobs.append((src, dst, [w]))
                for m in range(3):
                    for k0 in (0, 2, 4, 6):
                        src = wP_d.ap()[m, k0 * 128:(k0 + 2) * 128, :].rearrange("(kc p) n -> p kc n", p=128)
                        dst = wPb_d.ap()[m].rearrange("p (kc n) -> p kc n", kc=8)[:, k0:k0 + 2, :]
                        jobs.append((src, dst, [2, 1024]))
                for i, (src, dst, shp) in enumerate(jobs):
                    sl = i % 2
                    n = int(np.prod(shp))
                    if len(shp) == 2:
                        vf = stg[sl][:, 0:n].rearrange("p (a b) -> p a b", a=shp[0])
                        vb = stb[sl][:, 0:n].rearrange("p (a b) -> p a b", a=shp[0])
                    else:
                        vf, vb = stg[sl][:, 0:n], stb[sl][:, 0:n]
                    S.dma("sp", lambda e, vf=vf, src=src: e.dma_start(out=vf, in_=src), writes=[("stg", sl)])
                    eng = ("dve", "pool", "act")[i % 3]
                    if eng == "act":
                        S.op("act", lambda e, sl=sl, n=n: e.activation(out=stb[sl][:, 0:n], in_=stg[sl][:, 0:n], func=AF.Copy),
                             reads=[("stg", sl)], writes=[("stb", sl)])
                    else:
                        S.op(eng, lambda e, sl=sl, n=n: e.tensor_copy(out=stb[sl][:, 0:n], in_=stg[sl][:, 0:n]),
                             reads=[("stg", sl)], writes=[("stb", sl)])
                    S.dma("pool", lambda e, vb=vb, dst=dst: e.dma_start(out=dst, in_=vb), reads=[("stb", sl)], writes=[("st_wbf", sl)], group="wbf")
                S.flush()

        def phase_a(seg, x_ap, ntok):
            with contextlib.ExitStack() as as_:
                hbf = sb(as_, "hbf", [128, D], BF16)
                phase_a_body(seg, x_ap, ntok, hbf)

        def phase_a_body(seg, x_ap, ntok, hbf):
            hTv = hT_d[seg].ap().rearrange("(kc p) t -> p kc t", p=128)
            for g in range(ntok // 512):
                slot = nxt("hTt", 2)
                for sub in range(4):
                    xs_ = nxt("xin", 2)
                    r0 = g * 512 + sub * 128
                    S.dma("sp", lambda e, xs_=xs_, r0=r0: e.dma_start(out=xin[xs_][:], in_=x_ap[r0:r0 + 128, :]),
                          writes=[("xin", xs_)])
                    c = stcol()
                    S.op("act", lambda e, xs_=xs_, c=c: e.activation(out=junk[:], in_=xin[xs_][:], func=AF.Square,
                                                                     accum_out=stat[:, c:c + 1]),
                         reads=[("xin", xs_)], writes=["junk", ("st", c)])
                    c2 = stcol()
                    S.op("act", lambda e, c=c, c2=c2: e.activation(out=stat[:, c2:c2 + 1], in_=stat[:, c:c + 1], func=AF.Ln,
                                                                   bias=epst[:, 0:1], scale=1.0 / D),
                         reads=[("st", c)], writes=[("st", c2)])
                    c3 = stcol()
                    S.op("act", lambda e, c2=c2, c3=c3: e.activation(out=stat[:, c3:c3 + 1], in_=stat[:, c2:c2 + 1], func=AF.Exp,
                                                                     scale=-0.5),
                         reads=[("st", c2)], writes=[("st", c3)])
                    S.op("dve", lambda e, xs_=xs_, c3=c3: e.scalar_tensor_tensor(
                        out=hbf[:], in0=xin[xs_][:], scalar=stat[:, c3:c3 + 1], in1=gpre_b[:], op0=ALU.mult, op1=ALU.mult),
                        reads=[("xin", xs_), ("st", c3), "gpre"], writes=["hbf"])
                    for kc in range(8):
                        S.op("pe", lambda e, kc=kc, sub=sub: e.transpose(
                            out=psb[:, kc, sub * 128:(sub + 1) * 128], in_=hbf[:, kc * 128:(kc + 1) * 128], identity=ident_b[:]),
                            reads=["hbf", "ident_b"], writes=[("ps", kc)])
                S.op("act", lambda e, slot=slot: e.activation(out=hTt[slot][:, 0:4, :], in_=psb[:, 0:4, 0:512], func=AF.Copy),
                     reads=[("ps", k) for k in range(4)], writes=[("hTt", slot, 0)])
                S.op("dve", lambda e, slot=slot: e.tensor_copy(out=hTt[slot][:, 4:8, :], in_=psb[:, 4:8, 0:512]),
                     reads=[("ps", k) for k in range(4, 8)], writes=[("hTt", slot, 1)])
                S.dma("pool", lambda e, slot=slot, g=g: e.dma_start(out=hTv[:, :, g * 512:(g + 1) * 512], in_=hTt[slot][:]),
                      reads=[("hTt", slot, 0), ("hTt", slot, 1)], writes=[("st_hT", slot)], group=f"hT{seg}")
            S.flush()

        def units(seg, skv, prompt):
            hTv = hT_d[seg].ap().rearrange("(kc p) t -> p kc t", p=128)
            gav = ga_d[seg].ap()
            nkb = skv // 128
            with contextlib.ExitStack() as us:
                KT = sb(us, "KT", [128, skv], BF16)
                V = sb(us, "V", [128, nkb, 128], BF16)
                QT = sb(us, "QT", [128, S_S], BF16)
                SGb = [sb(us, f"SG{i}", [128, S_S], BF16) for i in range(2)]
                Wh = sb(us, "Wh", [128, 8, 512], BF16)
                E = [sb(us, f"E{i}", [128, 2, 512], BF16) for i in range(4)]
                U0 = sb(us, "U0", [128, 1152], F32)
                BB = sb(us, "BB", [128, 2, 512], F32)
                o_all = sb(us, "o_all", [128, S_S], F32)
                GA = [sb(us, f"GA{i}", [128, 512], BF16) for i in range(2)]
                u0p = list(U0[:].ap[0])

                def uwin(w):
                    return bass.AP(U0[:].tensor, U0[:, w:w + 512].offset, [u0p, [0, 2], [1, 512]])

                def bwin(j):
                    a = BB[:, j, :]
                    return bass.AP(a.tensor, a.offset, [list(a.ap[0]), [0, 2], [1, 512]])

                def stage2(h, qt):
                    SG = SGb[h % 2]
                    osl = o_all[:, qt * 512:(qt + 1) * 512]
                    t0, t1, t2 = tmp(), tmp(), tmp()
                    S.op("dve", lambda e: e.tensor_tensor(out=TT[:, t0, :], in0=osl, in1=osl, op=ALU.mult),
                         reads=[("o", qt)], writes=[("T", t0)])
                    b = bank()
                    S.op("pe", lambda e: e.matmul(ps[:, b, :], lhsT=onesdiv[:], rhs=TT[:, t0, :], start=True, stop=True),
                         reads=[("T", t0)], writes=[("ps", b)])
                    S.op("act", lambda e: e.activation(out=TT[:, t1, :], in_=ps[:, b, :], func=AF.Ln, bias=epst[:, 0:1]),
                         reads=[("ps", b)], writes=[("T", t1)])
                    S.op("act", lambda e: e.activation(out=TT[:, t1, :], in_=TT[:, t1, :], func=AF.Exp, scale=-0.5),
                         reads=[("T", t1)], writes=[("T", t1)])
                    S.op("dve", lambda e: e.scalar_tensor_tensor(
                        out=TT[:, t2, :], in0=osl, scalar=subgs[:, 0:1], in1=TT[:, t1, :], op0=ALU.mult, op1=ALU.mult),
                        reads=[("o", qt), ("T", t1)], writes=[("T", t2)])
                    gs = nxt("GA", 2)
                    S.op("dve", lambda e: e.tensor_tensor(out=GA[gs][:], in0=TT[:, t2, :],
                                                          in1=SG[:, qt * 512:(qt + 1) * 512], op=ALU.mult),
                         reads=[("T", t2), ("SG", h % 2, qt)], writes=[("GA", gs)])
                    S.dma("pool", lambda e: e.dma_start(
                        out=gav[h * 128:(h + 1) * 128, qt * 512:(qt + 1) * 512], in_=GA[gs][:]),
                        reads=[("GA", gs)], writes=[("st_ga", gs)], group=f"ga{seg}")

                for h in range(NH):
                    SG = SGb[h % 2]
                    S.dma("sp", lambda e, h=h: e.dma_start(out=Wh[:].rearrange("p k n -> p (k n)"), in_=wAb_d.ap()[h]),
                          reads=["@wbf"], writes=["Wh"])
                    S.dma("sp", lambda e, h=h: e.dma_start(out=U0[:], in_=bass.AP(gp_d, h * 128 * GL + 127,
                                                                                  [[GL - 1, 128], [1, 1152]])),
                          reads=["@gp"], writes=["U0"])
                    if prompt:
                        S.op("dve", lambda e, h=h: e.tensor_scalar(out=BB[:, 0, :], in0=U0[:, 0:512], scalar1=cbf8[:, 1, h:h + 1],
                                                                   scalar2=sel[:, 6:7], op0=ALU.subtract, op1=ALU.mult),
                             reads=["U0"], writes=["BB0"])
                        S.op("dve", lambda e, h=h: e.tensor_scalar(out=BB[:, 0, :], in0=BB[:, 0, :], scalar1=CBt8[:, 2, h:h + 1],
                                                                   scalar2=None, op0=ALU.add),
                             reads=["BB0"], writes=["BB0"])
                        S.op("dve", lambda e, h=h: e.tensor_scalar(out=BB[:, 1, :], in0=U0[:, 640:1152], scalar1=cbf8[:, 0, h:h + 1],
                                                                   scalar2=sel[:, 7:8], op0=ALU.subtract, op1=ALU.mult),
                             reads=["U0"], writes=["BB1"])
                        S.op("dve", lambda e, h=h: e.tensor_scalar(out=BB[:, 1, :], in0=BB[:, 1, :], scalar1=CBt8[:, 4, h:h + 1],
                                                                   scalar2=None, op0=ALU.add),
                             reads=["BB1"], writes=["BB1"])
                    for t in range(skv // 512):
                        slot = nxt("hTt", 2)
                        S.dma("sp", lambda e, slot=slot, t=t: e.dma_start(out=hTt[slot][:], in_=hTv[:, :, t * 512:(t + 1) * 512]),
                              reads=[f"@hT{seg}"], writes=[("hTt", slot, 0), ("hTt", slot, 1)])
                        hk = [("hTt", slot, 0), ("hTt", slot, 1)]
                        b = bank()
                        for kc in range(8):
                            S.op("pe", lambda e, b=b, kc=kc, slot=slot: e.matmul(
                                ps[:, b, :], lhsT=Wh[:, kc, 128:256], rhs=hTt[slot][:, kc, :], start=(kc == 0), stop=(kc == 7)),
                                reads=["Wh"] + hk, writes=[("ps", b)])
                        S.op("dve", lambda e, b=b, t=t: e.tensor_copy(out=KT[:, t * 512:(t + 1) * 512], in_=ps[:, b, :]),
                             reads=[("ps", b)], writes=[("KT", t)])
                        b = bank()
                        for sub in range(4):
                            for kc in range(8):
                                S.op("pe", lambda e, b=b, kc=kc, sub=sub, slot=slot: e.matmul(
                                    ps[:, b, sub * 128:(sub + 1) * 128], lhsT=hTt[slot][:, kc, sub * 128:(sub + 1) * 128],
                                    rhs=Wh[:, kc, 256:384], start=(kc == 0), stop=(kc == 7)),
                                    reads=["Wh"] + hk, writes=[("ps", b)])
                        S.op("act", lambda e, b=b, t=t: e.activation(
                            out=V[:, 4 * t:4 * t + 4, :].rearrange("p a b -> p (a b)"), in_=ps[:, b, :], func=AF.Copy),
                            reads=[("ps", b)], writes=[("V", t)])
                        if t < 8:
                            b = bank()
                            for kc in range(8):
                                S.op("pe", lambda e, b=b, kc=kc, slot=slot: e.matmul(
                                    ps[:, b, :], lhsT=Wh[:, kc, 0:128], rhs=hTt[slot][:, kc, :], start=(kc == 0), stop=(kc == 7)),
                                    reads=["Wh"] + hk, writes=[("ps", b)])
                            S.op("dve", lambda e, b=b, t=t: e.tensor_copy(out=QT[:, t * 512:(t + 1) * 512], in_=ps[:, b, :]),
                                 reads=[("ps", b)], writes=[("QT", t)])
                            b = bank()
                            for kc in range(8):
                                S.op("pe", lambda e, b=b, kc=kc, slot=slot: e.matmul(
                                    ps[:, b, :], lhsT=Wh[:, kc, 384:512], rhs=hTt[slot][:, kc, :], start=(kc == 0), stop=(kc == 7)),
                                    reads=["Wh"] + hk, writes=[("ps", b)])
                            ti = tmp()
                            S.op("act", lambda e, b=b, ti=ti: e.activation(out=TT[:, ti, :], in_=ps[:, b, :], func=AF.Tanh, scale=0.5),
                                 reads=[("ps", b)], writes=[("T", ti)])
                            S.op("dve", lambda e, b=b, ti=ti, t=t, SG=SG: e.scalar_tensor_tensor(
                                out=SG[:, t * 512:(t + 1) * 512], in0=TT[:, ti, :], scalar=1.0, in1=ps[:, b, :],
                                op0=ALU.add, op1=ALU.mult), reads=[("ps", b), ("T", ti)], writes=[("SG", h % 2, t)])
                            if h > 0:
                                stage2(h - 1, t)
                    steps = [(qt, kb) for qt in range(8) for kb in range(nkb)]
                    nst = len(steps)
                    pending = []

                    def emit_scores(i):
                        qt, kb = steps[i]
                        b0 = 2 * (i % 2)
                        rk = [("KT", kb // 4), ("QT", qt)]
                        S.op("pe", lambda e: e.matmul(ps[:, b0, :], lhsT=KT[0:64, kb * 128:(kb + 1) * 128],
                                                      rhs=QT[0:64, qt * 512:(qt + 1) * 512], start=True, stop=True),
                             reads=rk, writes=[("ps", b0)])
                        S.op("pe", lambda e: e.matmul(ps[:, b0 + 1, :], lhsT=KT[64:128, kb * 128:(kb + 1) * 128],
                                                      rhs=QT[64:128, qt * 512:(qt + 1) * 512], start=True, stop=True),
                             reads=rk, writes=[("ps", b0 + 1)])
                        chunk, kbl = kb // 32, kb % 32
                        dl = kbl - 4 * qt
                        win = None
                        if chunk == 0 and -1 <= dl <= 4:
                            win, rd = uwin(512 - 128 * dl), ["U0"]
                        elif prompt and chunk == 1 and qt == 7 and kbl == 0:
                            win, rd = bwin(0), ["BB0"]
                        elif prompt and chunk == 3 and qt == 0 and kbl == 31:
                            win, rd = bwin(1), ["BB1"]
                        if win is not None:
                            S.op("dve", lambda e: e.tensor_tensor(out=ps[:, b0:b0 + 2, :], in0=ps[:, b0:b0 + 2, :], in1=win, op=ALU.add),
                                 reads=rd + [("ps", b0), ("ps", b0 + 1)], writes=[("ps", b0), ("ps", b0 + 1)])
                            bias = 0.0
                        else:
                            if chunk == 0:
                                ci = 0 if dl < -1 else 1
                            else:
                                ci = 1 + chunk
                            bias = CBt[:, ci, h:h + 1]
                        ei = i % 4
                        S.op("act", lambda e: e.activation(out=E[ei][:], in_=ps[:, b0:b0 + 2, :], func=AF.Exp, bias=bias, scale=SCALE),
                             reads=[("ps", b0), ("ps", b0 + 1)], writes=[("E", ei)])

                    def emit_pv(i):
                        qt, kb = steps[i]
                        ei = i % 4
                        first, last = kb == 0, kb == nkb - 1
                        for m in range(2):
                            S.op("pe", lambda e, m=m: e.matmul(ps[:, 4 + m, :], lhsT=V[:, kb, :], rhs=E[ei][:, m, :],
                                                               start=first, stop=last),
                                 reads=[("V", kb // 4), ("E", ei)], writes=[("ps", 4 + m)])
                        if kb % 2 == 1:
                            for j, (ii, m) in enumerate(((i - 1, 0), (i - 1, 1), (i, 0), (i, 1))):
                                ej = ii % 4
                                S.op("pe", lambda e, j=j, ej=ej, m=m: e.matmul(
                                    ps[32 * j:32 * j + 32, 6, :], lhsT=ones_b[:, 0:32], rhs=E[ej][:, m, :],
                                    start=(kb == 1), stop=last, tile_position=(0, 32 * j)),
                                    reads=[("E", ej)], writes=[("ps", 6)])
                        if last:
                            ta, t2, t3, r1, r2 = tmp(), tmp(), tmp(), tmp(), tmp()
                            S.op("dve", lambda e: e.tensor_copy(out=TT[:, ta, :], in_=ps[:, 6, :]), reads=[("ps", 6)], writes=[("T", ta)])
                            S.op("dve", lambda e: e.tensor_copy(out=TT[:, t2, :], in_=ps[:, 4, :]), reads=[("ps", 4)], writes=[("T", t2)])
                            S.op("dve", lambda e: e.tensor_copy(out=TT[:, t3, :], in_=ps[:, 5, :]), reads=[("ps", 5)], writes=[("T", t3)])
                            S.op("pe", lambda e: e.matmul(ps[:, 7, :], lhsT=sel2[:, 0, :], rhs=TT[:, ta, :], start=True, stop=True),
                                 reads=[("T", ta)], writes=[("ps", 7)])
                            S.op("dve", lambda e: e.reciprocal(out=TT[:, r1, :], in_=ps[:, 7, :]), reads=[("ps", 7)], writes=[("T", r1)])

                            def rest():
                                S.op("pe", lambda e: e.matmul(ps[:, 7, :], lhsT=sel2[:, 1, :], rhs=TT[:, ta, :], start=True, stop=True),
                                     reads=[("T", ta)], writes=[("ps", 7)])
                                S.op("dve", lambda e: e.reciprocal(out=TT[:, r2, :], in_=ps[:, 7, :]), reads=[("ps", 7)], writes=[("T", r2)])
                                S.op("dve", lambda e: e.tensor_tensor(out=TT[:, t2, :], in0=TT[:, t2, :], in1=TT[:, r1, :], op=ALU.mult),
                                     reads=[("T", t2), ("T", r1)], writes=[("T", t2)])
                                S.op("dve", lambda e: e.tensor_tensor(out=TT[:, t3, :], in0=TT[:, t3, :], in1=TT[:, r2, :], op=ALU.mult),
                                     reads=[("T", t3), ("T", r2)], writes=[("T", t3)])
                                S.op("dve", lambda e: e.scalar_tensor_tensor(out=o_all[:, qt * 512:(qt + 1) * 512], in0=TT[:, t3, :],
                                                                             scalar=neglam, in1=TT[:, t2, :], op0=ALU.mult, op1=ALU.add),
                                     reads=[("T", t2), ("T", t3)], writes=[("o", qt)])
                            pending.append((i + 3, rest))

                    emit_scores(0)
                    for i in range(nst):
                        if i + 1 < nst:
                            emit_scores(i + 1)
                        emit_pv(i)
                        while pending and pending[0][0] <= i:
                            pending.pop(0)[1]()
                    while pending:
                        pending.pop(0)[1]()
                for qt in range(8):
                    stage2(NH - 1, qt)
                S.flush()

        def phase_c(seg, x_ap, y_ap):
            hTv = hT_d[seg].ap().rearrange("(kc p) t -> p kc t", p=128)
            gav = ga_d[seg].ap().rearrange("(kc p) t -> p kc t", p=128)
            with contextlib.ExitStack() as cs:
                WC = sb(cs, "WC", [128, 8, 5120], BF16)
                WP = [sb(cs, f"WP{i}", [128, 8, 1024], BF16) for i in range(3)]
                GAin = sb(cs, "GAin", [128, 8, 512], BF16)
                gbT = sb(cs, "gbT", [128, 8, 512], BF16)
                vm = sb(cs, "vm", [128, 4, 1024], BF16)
                mT = vm[:].rearrange("p a (b c) -> p (a b) c", b=2)
                for kc in range(8):
                    S.dma("sp", lambda e, kc=kc: e.dma_start(
                        out=WC[:, kc, :], in_=wCb_d.ap().rearrange("p (kc n) -> p kc n", kc=8)[:, kc, :]),
                        reads=["@wbf"], writes=[("WC", kc)])
                for m in range(3):
                    S.dma("sp", lambda e, m=m: e.dma_start(out=WP[m][:].rearrange("p k n -> p (k n)"), in_=wPb_d.ap()[m]),
                          reads=["@wbf"], writes=[("WP", m)])
                wck = [("WC", kc) for kc in range(8)]
                for t in range(S_S // 512):
                    slot = nxt("hTt", 2)
                    S.dma("sp", lambda e, slot=slot, t=t: e.dma_start(out=hTt[slot][:], in_=hTv[:, :, t * 512:(t + 1) * 512]),
                          reads=[f"@hT{seg}"], writes=[("hTt", slot, 0), ("hTt", slot, 1)])
                    hk = [("hTt", slot, 0), ("hTt", slot, 1)]
                    S.dma("sp", lambda e, t=t: e.dma_start(out=GAin[:], in_=gav[:, :, t * 512:(t + 1) * 512]),
                          reads=[f"@ga{seg}"], writes=["GAin"])
                    for sub in range(4):
                        b = bank2()
                        for half in range(2):
                            for kc in range(8):
                                S.op("pe", lambda e, b=b, half=half, kc=kc, sub=sub, slot=slot: e.matmul(
                                    ps[:, b + half, :], lhsT=hTt[slot][:, kc, sub * 128:(sub + 1) * 128],
                                    rhs=WC[:, kc, 1024 + half * 512:1024 + (half + 1) * 512], start=(kc == 0), stop=(kc == 7)),
                                    reads=wck + hk, writes=[("ps", b + half)])
                        pk = [("ps", b), ("ps", b + 1)]
                        c1, c2 = stcol(), stcol()
                        S.op("act", lambda e, b=b, c1=c1: e.activation(out=junk[:].rearrange("p (a b) -> p a b", a=2), in_=ps[:, b:b + 2, :],
                                                                       func=AF.Identity, accum_out=stat[:, c1:c1 + 1]),
                             reads=pk, writes=["junk", ("st", c1)])
                        S.op("act", lambda e, b=b, c2=c2: e.activation(out=junk[:].rearrange("p (a b) -> p a b", a=2), in_=ps[:, b:b + 2, :],
                                                                       func=AF.Square, accum_out=stat[:, c2:c2 + 1]),
                             reads=pk, writes=["junk", ("st", c2)])
                        c3, c4, c5, c6 = stcol(), stcol(), stcol(), stcol()
                        S.op("dve", lambda e, c1=c1, c3=c3: e.tensor_scalar(out=stat[:, c3:c3 + 1], in0=stat[:, c1:c1 + 1], scalar1=1.0 / D,
                                                                            scalar2=None, op0=ALU.mult), reads=[("st", c1)], writes=[("st", c3)])
                        S.op("dve", lambda e, c3=c3, c4=c4: e.scalar_tensor_tensor(out=stat[:, c4:c4 + 1], in0=stat[:, c3:c3 + 1], scalar=-1.0,
                                                                                   in1=stat[:, c3:c3 + 1], op0=ALU.mult, op1=ALU.mult),
                             reads=[("st", c3)], writes=[("st", c4)])
                        S.op("dve", lambda e, c2=c2, c4=c4, c5=c5: e.scalar_tensor_tensor(out=stat[:, c5:c5 + 1], in0=stat[:, c2:c2 + 1],
                                                                                         scalar=1.0 / D, in1=stat[:, c4:c4 + 1],
                                                                                         op0=ALU.mult, op1=ALU.add),
                             reads=[("st", c2), ("st", c4)], writes=[("st", c5)])
                        S.op("act", lambda e, c5=c5: e.activation(out=stat[:, c5:c5 + 1], in_=stat[:, c5:c5 + 1], func=AF.Ln,
                                                                  bias=epst[:, 0:1]),
                             reads=[("st", c5)], writes=[("st", c5)])
                        S.op("act", lambda e, c5=c5, c6=c6: e.activation(out=stat[:, c6:c6 + 1], in_=stat[:, c5:c5 + 1], func=AF.Exp,
                                                                         scale=-0.5),
                             reads=[("st", c5)], writes=[("st", c6)])
                        c7 = stcol()
                        S.op("dve", lambda e, c3=c3, c6=c6, c7=c7: e.scalar_tensor_tensor(out=stat[:, c7:c7 + 1], in0=stat[:, c3:c3 + 1],
                                                                                         scalar=-1.0, in1=stat[:, c6:c6 + 1],
                                                                                         op0=ALU.mult, op1=ALU.mult),
                             reads=[("st", c3), ("st", c6)], writes=[("st", c7)])
                        S.op("act", lambda e, b=b, sub=sub, c6=c6, c7=c7: e.activation(
                            out=vm[:, sub, :].rearrange("p (a b) -> p a b", a=2), in_=ps[:, b:b + 2, :], func=AF.Identity,
                            bias=stat[:, c7:c7 + 1], scale=stat[:, c6:c6 + 1]),
                            reads=pk + [("st", c6), ("st", c7)], writes=[("vm", sub)])
                    for g in range(8):
                        bs_ = bank()
                        for sub in range(4):
                            S.op("pe", lambda e, bs_=bs_, sub=sub, g=g: e.matmul(
                                ps[:, bs_, sub * 128:(sub + 1) * 128], lhsT=vm[:, sub, g * 128:(g + 1) * 128], rhs=wsT_b[:, g, :],
                                start=True, stop=True), reads=[("vm", sub), "wsT_b"], writes=[("ps", bs_)])
                        bu = bank()
                        for kc in range(8):
                            S.op("pe", lambda e, bu=bu, kc=kc, g=g, slot=slot: e.matmul(
                                ps[:, bu, :], lhsT=WC[:, kc, g * 128:(g + 1) * 128], rhs=hTt[slot][:, kc, :], start=(kc == 0), stop=(kc == 7)),
                                reads=wck + hk, writes=[("ps", bu)])
                        bg = bank()
                        for kc in range(8):
                            S.op("pe", lambda e, bg=bg, kc=kc, g=g, slot=slot: e.matmul(
                                ps[:, bg, :], lhsT=WC[:, kc, 2048 + g * 128:2048 + (g + 1) * 128], rhs=hTt[slot][:, kc, :],
                                start=(kc == 0), stop=(kc == 7)), reads=wck + hk, writes=[("ps", bg)])
                        ta, tb, tc = tmp(), tmp(), tmp()
                        cga = Cg[:, g, :]
                        cgw = bass.AP(cga.tensor, cga.offset, [list(cga.ap[0]), [0, 4], [1, 128]])
                        S.op("dve", lambda e, bs_=bs_, ta=ta, g=g, cgw=cgw: e.scalar_tensor_tensor(
                            out=TT[:, ta, :].rearrange("p (a b) -> p a b", a=4), in0=ps[:, bs_, :].rearrange("p (a b) -> p a b", a=4),
                            scalar=lng[:, g:g + 1], in1=cgw, op0=ALU.mult, op1=ALU.add),
                            reads=[("ps", bs_), "Cg", "lng"], writes=[("T", ta)])
                        S.op("act", lambda e, bg=bg, tb=tb: e.activation(out=TT[:, tb, :], in_=ps[:, bg, :], func=AF.Tanh, scale=0.5),
                             reads=[("ps", bg)], writes=[("T", tb)])
                        S.op("dve", lambda e, bg=bg, tb=tb: e.scalar_tensor_tensor(out=TT[:, tb, :], in0=TT[:, tb, :], scalar=1.0, in1=ps[:, bg, :],
                                                                                   op0=ALU.add, op1=ALU.mult),
                             reads=[("ps", bg), ("T", tb)], writes=[("T", tb)])
                        S.op("dve", lambda e, bu=bu, ta=ta, tc=tc: e.tensor_tensor(out=TT[:, tc, :], in0=ps[:, bu, :], in1=TT[:, ta, :], op=ALU.mult),
                             reads=[("ps", bu), ("T", ta)], writes=[("T", tc)])
                        S.op("dve", lambda e, tb=tb, tc=tc, g=g: e.scalar_tensor_tensor(out=gbT[:, g, :], in0=TT[:, tc, :], scalar=0.5, in1=TT[:, tb, :],
                                                                                        op0=ALU.mult, op1=ALU.mult),
                             reads=[("T", tb), ("T", tc)], writes=[("gbT", g)])
                    gbk = [("gbT", g) for g in range(8)]
                    vmk = [("vm", s_) for s_ in range(4)]
                    for n in range(8):
                        bma, bya, bmb, byb = bank(), bank(), bank(), bank()
                        for kc in range(8):
                            S.op("pe", lambda e, kc=kc, n=n, bma=bma, slot=slot: e.matmul(
                                ps[:, bma, :], lhsT=WC[:, kc, 3072 + n * 128:3072 + (n + 1) * 128], rhs=hTt[slot][:, kc, :],
                                start=(kc == 0), stop=(kc == 7)), reads=wck + hk, writes=[("ps", bma)])
                        for kc in range(8):
                            S.op("pe", lambda e, kc=kc, n=n, bya=bya: e.matmul(
                                ps[:, bya, :], lhsT=WP[0][:, kc, n * 128:(n + 1) * 128], rhs=GAin[:, kc, :],
                                start=(kc == 0), stop=(kc == 7)), reads=[("WP", 0), "GAin"], writes=[("ps", bya)])
                        for kc in range(8):
                            S.op("pe", lambda e, kc=kc, n=n, bmb=bmb, slot=slot: e.matmul(
                                ps[:, bmb, :], lhsT=WC[:, kc, 4096 + n * 128:4096 + (n + 1) * 128], rhs=hTt[slot][:, kc, :],
                                start=(kc == 0), stop=(kc == 7)), reads=wck + hk, writes=[("ps", bmb)])
                        for kc in range(8):
                            S.op("pe", lambda e, kc=kc, n=n, byb=byb: e.matmul(
                                ps[:, byb, :], lhsT=WP[1][:, kc, n * 128:(n + 1) * 128], rhs=gbT[:, kc, :],
                                start=(kc == 0), stop=(kc == 7)), reads=[("WP", 1)] + gbk, writes=[("ps", byb)])
                        ta, tb = tmp(), tmp()
                        for (bm, by, tx) in ((bma, bya, ta), (bmb, byb, tb)):
                            S.op("act", lambda e, bm=bm, tx=tx: e.activation(out=TT[:, tx, :], in_=ps[:, bm, :], func=AF.Tanh, scale=0.5),
                                 reads=[("ps", bm)], writes=[("T", tx)])
                            S.op("dve", lambda e, by=by, tx=tx: e.scalar_tensor_tensor(out=TT[:, tx, :], in0=TT[:, tx, :], scalar=1.0, in1=ps[:, by, :],
                                                                                       op0=ALU.add, op1=ALU.mult),
                                 reads=[("ps", by), ("T", tx)], writes=[("T", tx)])
                        S.op("dve", lambda e, ta=ta, tb=tb, n=n: e.tensor_tensor(out=mT[:, n, :], in0=TT[:, ta, :], in1=TT[:, tb, :], op=ALU.add),
                             reads=[("T", ta), ("T", tb)], writes=[("mT", n)] + vmk)
                    mk_ = [("mT", n) for n in range(8)]
                    for sub in range(4):
                        xs_ = nxt("xin", 2)
                        r0 = t * 512 + sub * 128
                        S.dma("sp", lambda e, xs_=xs_, r0=r0: e.dma_start(out=xin[xs_][:], in_=x_ap[r0:r0 + 128, :]), writes=[("xin", xs_)])
                        b = bank2()
                        for half in range(2):
                            for kc in range(8):
                                S.op("pe", lambda e, b=b, half=half, kc=kc, sub=sub: e.matmul(
                                    ps[:, b + half, :], lhsT=mT[:, kc, sub * 128:(sub + 1) * 128], rhs=WP[2][:, kc, half * 512:(half + 1) * 512],
                                    start=(kc == 0), stop=(kc == 7)), reads=[("WP", 2)] + mk_ + vmk, writes=[("ps", b + half)])
                        pk = [("ps", b), ("ps", b + 1)]
                        c1, c2, c3 = stcol(), stcol(), stcol()
                        S.op("act", lambda e, b=b, c1=c1: e.activation(out=junk[:].rearrange("p (a b) -> p a b", a=2), in_=ps[:, b:b + 2, :],
                                                                       func=AF.Square, accum_out=stat[:, c1:c1 + 1]),
                             reads=pk, writes=["junk", ("st", c1)])
                        S.op("act", lambda e, c1=c1, c2=c2: e.activation(out=stat[:, c2:c2 + 1], in_=stat[:, c1:c1 + 1], func=AF.Ln,
                                                                         bias=epst[:, 1:2], scale=1.0 / D),
                             reads=[("st", c1)], writes=[("st", c2)])
                        S.op("act", lambda e, c2=c2, c3=c3: e.activation(out=stat[:, c3:c3 + 1], in_=stat[:, c2:c2 + 1], func=AF.Exp,
                                                                         scale=-0.5),
                             reads=[("st", c2)], writes=[("st", c3)])
                        if cnt["T"] % 2:
                            cnt["T"] += 1
                        ta = tmp()
                        tmp()
                        S.op("dve", lambda e, b=b, c3=c3, ta=ta: e.scalar_tensor_tensor(
                            out=TT[:, ta:ta + 2, :], in0=ps[:, b:b + 2, :], scalar=stat[:, c3:c3 + 1],
                            in1=gpost_b[:].rearrange("p (a b) -> p a b", a=2), op0=ALU.mult, op1=ALU.mult),
                            reads=pk + [("st", c3), "gpost"], writes=[("T", ta), ("T", ta + 1)])
                        S.op("dve", lambda e, xs_=xs_, ta=ta: e.tensor_tensor(out=xin[xs_][:].rearrange("p (a b) -> p a b", a=2),
                                                                              in0=TT[:, ta:ta + 2, :],
                                                                              in1=xin[xs_][:].rearrange("p (a b) -> p a b", a=2), op=ALU.add),
                             reads=[("T", ta), ("T", ta + 1), ("xin", xs_)], writes=[("xin", xs_)])
                        S.dma("pool", lambda e, xs_=xs_, r0=r0: e.dma_start(out=y_ap[r0:r0 + 128, :], in_=xin[xs_][:]),
                              reads=[("xin", xs_)], writes=[("st_out", xs_)], group="out")
                S.flush()

        _orig_deps = S._deps

        def _deps(reads, writes):
            real = [r for r in reads if not (isinstance(r, str) and r.startswith("@"))]
            deps = _orig_deps(real, writes)
            for r in reads:
                if isinstance(r, str) and r.startswith("@"):
                    for key in S.groups[r[1:]]:
                        dd = S.dsem[key]
                        deps.append(Tok("dma", dd[0], dd[1]))
            return deps

        _orig_commit = S._commit

        def _commit(tok, reads, writes):
            real = [r for r in reads if not (isinstance(r, str) and r.startswith("@"))]
            _orig_commit(tok, real, writes)

        S._deps, S._commit = _deps, _commit

        setup()
        segs = [(0, xs_d.ap()[0], S_S, False, ys_d.ap()[0]), (1, xs_d.ap()[1], S_S, False, ys_d.ap()[1]),
                (2, xp_d.ap(), S_P, True, yp_d.ap())]
        for seg, x_ap, skv, prompt, y_ap in segs:
            phase_a(seg, x_ap, skv)
            units(seg, skv, prompt)
            phase_c(seg, x_ap, y_ap)
    return nc


def _bucket(rel):
    half, max_exact = 16, 8
    n = np.abs(rel)
    nf = np.maximum(n, 1).astype(np.float32)
    large = max_exact + (np.log(nf / np.float32(max_exact)) / np.float32(math.log(128 / max_exact))
                         * np.float32(half - max_exact)).astype(np.int32)
    large = np.minimum(large, half - 1)
    return np.where(rel > 0, half, 0) + np.where(n < max_exact, n, large)


_CACHE = {}


def kernel(x_prompt, x_sample, g_pre, w_in, lambda_q1, lambda_k1, lambda_q2, lambda_k2, subln_g,
           w_pa, ln_g, ln_b, w_s, b_s, w_pb, w_o, g_post, rel_bias):
    f = lambda a: np.ascontiguousarray(np.asarray(a, dtype=np.float32))
    x_prompt, x_sample, w_in = f(x_prompt), f(x_sample), f(w_in)[0]
    rep = lambda v, n=128: np.ascontiguousarray(np.broadcast_to(f(v).reshape(1, -1), (n, f(v).size)))
    wA = np.empty((NH, D, 512), np.float32)
    for h in range(NH):
        wA[h, :, 0:64] = w_in[:, h * 64:(h + 1) * 64]
        wA[h, :, 64:128] = w_in[:, 512 + h * 64:512 + (h + 1) * 64]
        wA[h, :, 128:192] = w_in[:, 1024 + h * 64:1024 + (h + 1) * 64]
        wA[h, :, 192:256] = w_in[:, 1536 + h * 64:1536 + (h + 1) * 64]
        wA[h, :, 256:384] = w_in[:, 2048 + h * 128:2048 + (h + 1) * 128]
        wA[h, :, 384:512] = w_in[:, 3072 + h * 128:3072 + (h + 1) * 128]
    wC = np.ascontiguousarray(w_in[:, 4096:9216])
    wP = np.stack([f(w_pa)[0], f(w_pb)[0], f(w_o)[0]])
    lamv = np.concatenate([rep(lambda_q1), rep(lambda_k1), rep(lambda_q2), rep(lambda_k2)], axis=1)
    subg = f(subln_g).reshape(128, 1)
    lng = np.ascontiguousarray(f(ln_g).reshape(8, 128).T)
    wsT = np.ascontiguousarray(f(w_s)[0].transpose(2, 0, 1).reshape(128, 8 * 128))
    bsrow = f(b_s).reshape(1, D)
    relb = f(rel_bias)
    cbf = np.concatenate([rep(relb[15]), rep(relb[31])], axis=1)
    j = np.arange(GL)
    oh = (np.arange(32)[:, None] == _bucket(639 - j)[None, :]).astype(np.float32)
    ident = np.eye(128, dtype=np.float32)
    common = dict(wA=wA, wC=wC, wP=wP, gpre_b=rep(g_pre), gpost_b=rep(g_post), lamv=lamv, subg=subg, lng=lng,
                  lnb_rows=rep(ln_b), wsT=wsT, bsrow=bsrow, relb=relb, cbf=cbf, oh=oh, ident=ident)
    in_maps = []
    for c in range(8):
        pb, pq = c // 4, c % 4
        xp = np.ascontiguousarray(np.roll(x_prompt[pb], -pq * S_S, axis=0))
        selb = [1.0 if pq == 3 else 0.0, 1.0 if pq >= 2 else 0.0, 1.0 if pq >= 1 else 0.0]
        selv = selb + [1.0 - s for s in selb] + [1.0 if pq <= 2 else 0.0, 1.0 if pq >= 1 else 0.0]
        sel = np.ascontiguousarray(np.broadcast_to(np.array(selv, np.float32)[None, :], (128, 8)))
        m = dict(common)
        m.update(xs=np.ascontiguousarray(x_sample[2 * c:2 * c + 2]), xp=xp, sel=sel)
        in_maps.append(m)
    if "nc" not in _CACHE:
        _CACHE["nc"] = build_program()
    res = run_bass_kernel_spmd(_CACHE["nc"], in_maps, core_ids=list(range(8)))
    y_prompt = np.empty((2, S_P, D), np.float32)
    y_sample = np.empty((16, S_S, D), np.float32)
    for c in range(8):
        r = res.results[c]
        pb, pq = c // 4, c % 4
        y_prompt[pb, pq * S_S:(pq + 1) * S_S] = r["yp"]
        y_sample[2 * c:2 * c + 2] = r["ys"]
    return (y_prompt, y_sample)
```

```python
import contextlib
import math
import numpy as np
import concourse.bass as bass
import concourse.mybir as mybir
from concourse.bass_utils import run_bass_kernel_spmd

F32 = mybir.dt.float32
BF16 = mybir.dt.bfloat16
AF = mybir.ActivationFunctionType
ALU = mybir.AluOpType

D = 1024
NH = 8
S_S = 4096
S_P = 16384
EPS = 1e-6
LAMBDA_INIT = 0.8 - 0.6 * math.exp(-0.3 * 0)
SCALE = 0.125
GL = 1280
EPOCH = 20000


class Tok:
    __slots__ = ("eng", "sem", "val", "group")

    def __init__(self, eng, sem, val, group=None):
        self.eng, self.sem, self.val, self.group = eng, sem, val, group


class Op:
    __slots__ = ("fn", "deps", "tok", "is_dma")

    def __init__(self, fn, deps, tok, is_dma):
        self.fn, self.deps, self.tok, self.is_dma = fn, deps, tok, is_dma


class Sched:
    COMPUTE = ("pe", "act", "dve", "pool")
    ALL = ("pe", "act", "dve", "pool", "sp")
    ATTR = {"pe": "tensor", "act": "scalar", "dve": "vector", "pool": "gpsimd", "sp": "sync"}

    def __init__(self, nc, stack):
        self.nc, self.stack = nc, stack
        self.ops = {e: [] for e in self.ALL}
        self.count = {e: 0 for e in self.COMPUTE}
        self.esems = {e: [] for e in self.COMPUTE}
        self.res = {}
        self.dsem = {}
        self.groups = {}
        self.nsem = 0
        self.waited = {e: {} for e in self.ALL}
        self.group_final = set()

    def _newsem(self, name):
        self.nsem += 1
        return self.stack.enter_context(self.nc.semaphore(f"s{self.nsem}_{name}"))

    def _deps(self, reads, writes):
        deps = []
        for r in reads:
            e = self.res.get(r)
            if e and e[0] is not None:
                deps.append(e[0])
        for w in writes:
            e = self.res.get(w)
            if e:
                if e[0] is not None:
                    deps.append(e[0])
                deps.extend(e[1].values())
        return deps

    def _commit(self, tok, reads, writes):
        for r in reads:
            e = self.res.setdefault(r, [None, {}])
            e[1][tok.eng if tok.eng != "dma" else id(tok.sem)] = tok
        for w in writes:
            self.res[w] = [tok, {}]

    def op(self, eng, fn, reads=(), writes=()):
        deps = self._deps(reads, writes)
        n = self.count[eng]
        ep = n // EPOCH
        if ep >= len(self.esems[eng]):
            self.esems[eng].append(self._newsem(f"{eng}{ep}"))
        tok = Tok(eng, self.esems[eng][ep], n - ep * EPOCH + 1)
        self.count[eng] = n + 1
        if eng == "pe":
            deps = [d for d in deps if d.eng != "pe"]
        self.ops[eng].append(Op(fn, deps, tok, False))
        self._commit(tok, reads, writes)
        return tok

    def dma(self, queue, fn, reads=(), writes=(), group=None):
        deps = self._deps(reads, writes)
        key = writes[0]
        d = self.dsem.get(key)
        if d is None:
            d = self.dsem[key] = [self._newsem("d"), 0]
        d[1] += 16
        tok = Tok("dma", d[0], d[1])
        if group is not None:
            self.groups.setdefault(group, set()).add(key)
        self.ops[queue].append(Op(fn, deps, tok, True))
        self._commit(tok, reads, writes)
        return tok

    def _resolve(self, d):
        return d.sem, d.val

    def flush(self, final_groups=()):
        nc = self.nc
        bar = []
        for e in self.COMPUTE:
            n = self.count[e]
            if n:
                ep = (n - 1) // EPOCH
                bar.append((self.esems[e][ep], n - ep * EPOCH))
        for d in self.dsem.values():
            bar.append((d[0], d[1]))

        def run(engname, e):
            waited = self.waited[engname]
            for op in self.ops[engname]:
                for d in op.deps:
                    sem, val = self._resolve(d)
                    k = id(sem)
                    if waited.get(k, 0) >= val:
                        continue
                    waited[k] = val
                    e.wait_ge(sem, val)
                ins = op.fn(e)
                ins.then_inc(op.tok.sem, 16 if op.is_dma else 1)
            for sem, val in bar:
                k = id(sem)
                if waited.get(k, 0) >= val:
                    continue
                waited[k] = val
                e.wait_ge(sem, val)
            self.ops[engname] = []

        with nc.Block() as block:
            for engname in self.ALL:
                def mk(engname=engname):
                    def f(e):
                        run(engname, e)
                    return f
                getattr(block, self.ATTR[engname])(mk())
        self.res = {}


def build_program():
    nc = bass.Bass("TRN2", target_bir_lowering=False)
    dt_in = lambda name, shape: nc.dram_tensor(name, shape, F32, kind="ExternalInput")
    xs_d = dt_in("xs", [2, S_S, D])
    xp_d = dt_in("xp", [S_P, D])
    wA_d = dt_in("wA", [NH, D, 512])
    wC_d = dt_in("wC", [D, 5120])
    wP_d = dt_in("wP", [3, D, D])
    gpre_d = dt_in("gpre_b", [128, D])
    gpost_d = dt_in("gpost_b", [128, D])
    lamv_d = dt_in("lamv", [128, 4 * 64])
    subg_d = dt_in("subg", [128, 1])
    lng_d = dt_in("lng", [128, 8])
    lnbrow_d = dt_in("lnb_rows", [128, D])
    wsT_d = dt_in("wsT", [128, 8 * 128])
    bsrow_d = dt_in("bsrow", [1, D])
    relb_d = dt_in("relb", [32, 8])
    cbf_d = dt_in("cbf", [128, 16])
    sel_d = dt_in("sel", [128, 8])
    oh_d = dt_in("oh", [32, GL])
    ident_d = dt_in("ident", [128, 128])
    ys_d = nc.dram_tensor("ys", [2, S_S, D], F32, kind="ExternalOutput")
    yp_d = nc.dram_tensor("yp", [S_S, D], F32, kind="ExternalOutput")

    scr = lambda name, shape, dt: nc.dram_tensor(name, shape, dt, kind="Internal")
    wAb_d = scr("wAb", [NH, 128, 8 * 512], BF16)
    wCb_d = scr("wCb", [128, 8 * 5120], BF16)
    wPb_d = scr("wPb", [3, 128, 8 * 1024], BF16)
    hT_d = [scr("hT0", [S_S // 512, 128, 4096], BF16), scr("hT1", [S_S // 512, 128, 4096], BF16),
            scr("hT2", [S_P // 512, 128, 4096], BF16)]
    ga_d = [scr(f"ga{i}", [S_S // 512, 128, 4096], BF16) for i in range(3)]
    gp_d = scr("gp", [NH * 128 * GL], F32)

    with contextlib.ExitStack() as st:
        S = Sched(nc, st)
        _nm = [0]

        def sb(stack, name, shape, dt):
            _nm[0] += 1
            return stack.enter_context(nc.sbuf_tensor(f"sb{_nm[0]}_{name}", shape, dt))
        ps = st.enter_context(nc.psum_tensor("ps", [128, 8, 512], F32))
        psb = ps[:, 0:8, :].bitcast(BF16)

        ident_b = sb(st, "ident_b", [128, 128], BF16)
        ones_b = sb(st, "ones_b", [128, 128], BF16)
        onesdiv = sb(st, "onesdiv", [128, 128], F32)
        gpre_b = sb(st, "gpre", [128, D], F32)
        gpost_b = sb(st, "gpost", [128, D], F32)
        small = sb(st, "small", [128, 64], F32)
        subgs = sb(st, "subgs", [128, 1], F32)
        lng = sb(st, "lng", [128, 8], F32)
        wsT_b = sb(st, "wsT_b", [128, 8, 128], BF16)
        Cg = sb(st, "Cg", [128, 8, 128], F32)
        cbf = sb(st, "cbf", [128, 2, 8], F32)
        cbf8 = sb(st, "cbf8", [128, 2, 8], F32)
        CBt = sb(st, "CBt", [128, 5, 8], F32)
        CBt8 = sb(st, "CBt8", [128, 5, 8], F32)
        sel = sb(st, "sel", [128, 8], F32)
        xin = [sb(st, f"xin{i}", [128, D], F32) for i in range(2)]
        junk = sb(st, "junk", [128, D], BF16)
        hTt = [sb(st, f"hTt{i}", [128, 8, 512], BF16) for i in range(2)]
        TT = sb(st, "TT", [128, 6, 512], F32)
        stat = sb(st, "stat", [128, 16], F32)
        neglam = small[:, 0:1]
        epst = sb(st, "epst", [128, 2], F32)
        sel2 = sb(st, "sel2", [128, 2, 128], F32)

        cnt = {"xin": 0, "hTt": 0, "bank": 0, "T": 0, "st": 0, "GA": 0}

        def nxt(k, n):
            v = cnt[k] % n
            cnt[k] += 1
            return v

        def bank():
            return nxt("bank", 8)

        def bank2():
            if cnt["bank"] % 2:
                cnt["bank"] += 1
            b = cnt["bank"] % 8
            cnt["bank"] += 2
            return b

        def tmp():
            return nxt("T", 6)

        def stcol():
            return nxt("st", 16)

        def setup():
            lp = lambda dst, src, key: S.dma("sp", lambda e: e.dma_start(out=dst, in_=src), writes=[key])
            lp(gpre_b[:], gpre_d.ap(), "gpre")
            lp(gpost_b[:], gpost_d.ap(), "gpost")
            lp(subgs[:], subg_d.ap(), "subg_raw")
            lp(lng[:], lng_d.ap(), "lng")
            lp(cbf[:].rearrange("p a h -> p (a h)"), cbf_d.ap(), "cbf")
            lp(sel[:], sel_d.ap(), "sel")
            S.op("dve", lambda e: e.memset(ones_b[:], 1.0), writes=["ones_b"])
            S.op("dve", lambda e: e.memset(onesdiv[:], 1.0 / 128.0), writes=["onesdiv"])
            S.op("dve", lambda e: e.memset(sel2[:], 0.0), writes=["sel2"])
            for m_, rows_ in ((0, (0, 64)), (1, (32, 96))):
                for r_ in rows_:
                    S.op("dve", lambda e, m_=m_, r_=r_: e.memset(sel2[r_:r_ + 1, m_, :], 1.0), reads=["sel2"], writes=["sel2"])
            S.op("dve", lambda e: e.memset(epst[:, 0:1], EPS), writes=["eps0"])
            S.op("dve", lambda e: e.memset(epst[:, 1:2], 4.0 * EPS), writes=["eps1"])
            S.op("dve", lambda e: e.tensor_scalar(out=subgs[:], in0=subgs[:], scalar1=0.5 * (1.0 - LAMBDA_INIT), scalar2=None,
                                                  op0=ALU.mult), reads=["subg_raw"], writes=["subg_raw"])
            with contextlib.ExitStack() as ls:
                lamv = sb(ls, "lamv", [128, 4, 64], F32)
                ident_f = sb(ls, "ident_f", [128, 128], F32)
                onesrow = sb(ls, "onesrow", [1, 128], F32)
                lp(ident_f[:], ident_d.ap(), "ident_f")
                S.op("dve", lambda e: e.tensor_copy(out=ident_b[:], in_=ident_f[:]), reads=["ident_f"], writes=["ident_b"])
                S.op("dve", lambda e: e.memset(onesrow[:], 1.0), writes=["onesrow"])
                lnbrows = sb(ls, "lnbrows", [128, D], F32)
                wsT_f = sb(ls, "wsT_f", [128, 8, 128], F32)
                bsrow = sb(ls, "bsrow", [1, D], F32)
                relb = sb(ls, "relb", [32, 8], F32)
                oh = sb(ls, "oh", [32, GL], F32)
                G = sb(ls, "G", [8, GL], F32)
                stg = [sb(ls, f"stg{i}", [128, 2048], F32) for i in range(2)]
                stb = [sb(ls, f"stb{i}", [128, 2048], BF16) for i in range(2)]
                lp(lamv[:].rearrange("p a j -> p (a j)"), lamv_d.ap(), "lamv")
                lp(lnbrows[:], lnbrow_d.ap(), "lnbrows")
                lp(wsT_f[:].rearrange("p g q -> p (g q)"), wsT_d.ap(), "wsT_f")
                lp(bsrow[:], bsrow_d.ap(), "bsrow")
                lp(relb[:], relb_d.ap(), "relb")
                lp(oh[:], oh_d.ap(), "oh")
                for j in range(2):
                    S.op("dve", lambda e, j=j: e.tensor_tensor(out=TT[:, j, 0:64], in0=lamv[:, 2 * j, :],
                                                               in1=lamv[:, 2 * j + 1, :], op=ALU.mult),
                         reads=["lamv"], writes=[("T", j)])
                    S.op("dve", lambda e, j=j: e.reduce_sum(out=small[:, 1 + j:2 + j], in_=TT[:, j, 0:64],
                                                            axis=mybir.AxisListType.X),
                         reads=[("T", j)], writes=[("sm", 1 + j)])
                    S.op("act", lambda e, j=j: e.activation(out=small[:, 3 + j:4 + j], in_=small[:, 1 + j:2 + j],
                                                            func=AF.Exp), reads=[("sm", 1 + j)], writes=[("sm", 3 + j)])
                S.op("dve", lambda e: e.tensor_tensor(out=small[:, 5:6], in0=small[:, 3:4], in1=small[:, 4:5],
                                                      op=ALU.subtract), reads=[("sm", 3), ("sm", 4)], writes=[("sm", 5)])
                S.op("dve", lambda e: e.tensor_scalar(out=small[:, 0:1], in0=small[:, 5:6], scalar1=LAMBDA_INIT,
                                                      scalar2=-1.0, op0=ALU.add, op1=ALU.mult),
                     reads=[("sm", 5)], writes=["neglam"])
                S.op("dve", lambda e: e.tensor_copy(out=wsT_b[:], in_=wsT_f[:]), reads=["wsT_f"], writes=["wsT_b"])
                for g in range(8):
                    bk, off = g // 4, (g % 4) * 128
                    S.op("pe", lambda e, g=g, bk=bk, off=off: e.matmul(
                        ps[:, bk, off:off + 128], lhsT=lnbrows[:, g * 128:(g + 1) * 128], rhs=wsT_f[:, g, :],
                        start=True, stop=False), reads=["lnbrows", "wsT_f"], writes=[("ps", bk)])
                    S.op("pe", lambda e, g=g, bk=bk, off=off: e.matmul(
                        ps[:, bk, off:off + 128], lhsT=onesrow[0:1, :], rhs=bsrow[0:1, g * 128:(g + 1) * 128],
                        start=False, stop=True), reads=["onesrow", "bsrow"], writes=[("ps", bk)])
                S.op("dve", lambda e: e.tensor_copy(out=Cg[:].rearrange("p (a b) q -> p a (b q)", a=2),
                                                    in_=ps[:, 0:2, :]), reads=[("ps", 0), ("ps", 1)], writes=["Cg"])
                S.op("dve", lambda e: e.tensor_scalar(out=cbf8[:], in0=cbf[:], scalar1=8.0, scalar2=None, op0=ALU.mult),
                     reads=["cbf"], writes=["cbf8"])
                S.op("dve", lambda e: e.tensor_copy(out=CBt[:, 0:2, :], in_=cbf[:]), reads=["cbf"], writes=["CBt01"])
                for i in range(3):
                    S.op("dve", lambda e, i=i: e.tensor_scalar(out=TT[:, 2, 0:8], in0=cbf[:, 0, :],
                                                               scalar1=sel[:, i:i + 1], scalar2=None, op0=ALU.mult),
                         reads=["cbf", "sel"], writes=[("T", 2)])
                    S.op("dve", lambda e, i=i: e.scalar_tensor_tensor(out=CBt[:, 2 + i, :], in0=cbf[:, 1, :],
                                                                      scalar=sel[:, 3 + i:4 + i], in1=TT[:, 2, 0:8],
                                                                      op0=ALU.mult, op1=ALU.add),
                         reads=["cbf", "sel", ("T", 2)], writes=[("CBt", i)])
                S.op("dve", lambda e: e.tensor_scalar(out=CBt8[:], in0=CBt[:], scalar1=8.0, scalar2=None, op0=ALU.mult),
                     reads=["CBt01", ("CBt", 0), ("CBt", 1), ("CBt", 2)], writes=["CBt8"])
                for c0 in range(0, GL, 512):
                    w = min(512, GL - c0)
                    bk = 2 + c0 // 512
                    S.op("pe", lambda e, c0=c0, w=w, bk=bk: e.matmul(ps[0:8, bk, 0:w], lhsT=relb[:, :], rhs=oh[:, c0:c0 + w],
                                                                     start=True, stop=True),
                         reads=["relb", "oh"], writes=[("ps", bk)])
                    S.op("dve", lambda e, c0=c0, w=w, bk=bk: e.tensor_scalar(out=G[:, c0:c0 + w], in0=ps[0:8, bk, 0:w],
                                                                             scalar1=8.0, scalar2=None, op0=ALU.mult),
                         reads=[("ps", bk)], writes=[("G", c0)])
                gsrc = bass.AP(G[:].tensor, G[:].offset, [list(G[:].ap[0]), [0, 128], [1, GL]])
                S.dma("pool", lambda e: e.dma_start(out=bass.AP(gp_d, 0, [[128 * GL, 8], [GL, 128], [1, GL]]), in_=gsrc),
                      reads=[("G", 0), ("G", 512), ("G", 1024)], writes=["st_gp"], group="gp")
                jobs = []
                for h in range(NH):
                    for k0 in (0, 4):
                        src = wA_d.ap()[h, k0 * 128:(k0 + 4) * 128, :].rearrange("(kc p) n -> p kc n", p=128)
                        dst = wAb_d.ap()[h].rearrange("p (kc n) -> p kc n", kc=8)[:, k0:k0 + 4, :]
                        jobs.append((src, dst, [4, 512]))
                for kc in range(8):
                    for c0, w in ((0, 2048), (2048, 2048), (4096, 1024)):
                        src = wC_d.ap()[kc * 128:(kc + 1) * 128, c0:c0 + w]
                        dst = wCb_d.ap().rearrange("p (kc n) -> p kc n", kc=8)[:, kc, c0:c0 + w]
                        jobs.append((src, dst, [w]))
                for m in range(3):
                    for k0 in (0, 2, 4, 6):
                        src = wP_d.ap()[m, k0 * 128:(k0 + 2) * 128, :].rearrange("(kc p) n -> p kc n", p=128)
                        dst = wPb_d.ap()[m].rearrange("p (kc n) -> p kc n", kc=8)[:, k0:k0 + 2, :]
                        jobs.append((src, dst, [2, 1024]))
                for i, (src, dst, shp) in enumerate(jobs):
                    sl = i % 2
                    n = int(np.prod(shp))
                    if len(shp) == 2:
                        vf = stg[sl][:, 0:n].rearrange("p (a b) -> p a b", a=shp[0])
                        vb = stb[sl][:, 0:n].rearrange("p (a b) -> p a b", a=shp[0])
                    else:
                        vf, vb = stg[sl][:, 0:n], stb[sl][:, 0:n]
                    S.dma("sp", lambda e, vf=vf, src=src: e.dma_start(out=vf, in_=src), writes=[("stg", sl)])
                    eng = ("dve", "pool", "act")[i % 3]
                    if eng == "act":
                        S.op("act", lambda e, sl=sl, n=n: e.activation(out=stb[sl][:, 0:n], in_=stg[sl][:, 0:n], func=AF.Copy),
                             reads=[("stg", sl)], writes=[("stb", sl)])
                    else:
                        S.op(eng, lambda e, sl=sl, n=n: e.tensor_copy(out=stb[sl][:, 0:n], in_=stg[sl][:, 0:n]),
                             reads=[("stg", sl)], writes=[("stb", sl)])
                    S.dma("pool", lambda e, vb=vb, dst=dst: e.dma_start(out=dst, in_=vb), reads=[("stb", sl)], writes=[("st_wbf", sl)], group="wbf")
                S.flush()

        def phase_a(seg, x_ap, ntok):
            with contextlib.ExitStack() as as_:
                hbf = sb(as_, "hbf", [128, D], BF16)
                phase_a_body(seg, x_ap, ntok, hbf)

        def phase_a_body(seg, x_ap, ntok, hbf):
            hTv = hT_d[seg].ap()
            for g in range(ntok // 512):
                slot = nxt("hTt", 2)
                for sub in range(4):
                    xs_ = nxt("xin", 2)
                    r0 = g * 512 + sub * 128
                    S.dma("sp", lambda e, xs_=xs_, r0=r0: e.dma_start(out=xin[xs_][:], in_=x_ap[r0:r0 + 128, :]),
                          writes=[("xin", xs_)])
                    c = stcol()
                    S.op("act", lambda e, xs_=xs_, c=c: e.activation(out=junk[:], in_=xin[xs_][:], func=AF.Square,
                                                                     accum_out=stat[:, c:c + 1]),
                         reads=[("xin", xs_)], writes=["junk", ("st", c)])
                    c2 = stcol()
                    S.op("act", lambda e, c=c, c2=c2: e.activation(out=stat[:, c2:c2 + 1], in_=stat[:, c:c + 1], func=AF.Ln,
                                                                   bias=epst[:, 0:1], scale=1.0 / D),
                         reads=[("st", c)], writes=[("st", c2)])
                    c3 = stcol()
                    S.op("act", lambda e, c2=c2, c3=c3: e.activation(out=stat[:, c3:c3 + 1], in_=stat[:, c2:c2 + 1], func=AF.Exp,
                                                                     scale=-0.5),
                         reads=[("st", c2)], writes=[("st", c3)])
                    S.op("dve", lambda e, xs_=xs_, c3=c3: e.scalar_tensor_tensor(
                        out=hbf[:], in0=xin[xs_][:], scalar=stat[:, c3:c3 + 1], in1=gpre_b[:], op0=ALU.mult, op1=ALU.mult),
                        reads=[("xin", xs_), ("st", c3), "gpre"], writes=["hbf"])
                    for kc in range(8):
                        S.op("pe", lambda e, kc=kc, sub=sub: e.transpose(
                            out=psb[:, kc, sub * 128:(sub + 1) * 128], in_=hbf[:, kc * 128:(kc + 1) * 128], identity=ident_b[:]),
                            reads=["hbf", "ident_b"], writes=[("ps", kc)])
                S.op("act", lambda e, slot=slot: e.activation(out=hTt[slot][:, 0:4, :], in_=psb[:, 0:4, 0:512], func=AF.Copy),
                     reads=[("ps", k) for k in range(4)], writes=[("hTt", slot, 0)])
                S.op("dve", lambda e, slot=slot: e.tensor_copy(out=hTt[slot][:, 4:8, :], in_=psb[:, 4:8, 0:512]),
                     reads=[("ps", k) for k in range(4, 8)], writes=[("hTt", slot, 1)])
                S.dma("pool", lambda e, slot=slot, g=g: e.dma_start(out=hTv[g], in_=hTt[slot][:].rearrange("p k n -> p (k n)")),
                      reads=[("hTt", slot, 0), ("hTt", slot, 1)], writes=[("st_hT", slot)], group=f"hT{seg}")
            S.flush()

        def units(seg, skv, prompt):
            hTv = hT_d[seg].ap()
            gav = ga_d[seg].ap()
            nkb = skv // 128
            with contextlib.ExitStack() as us:
                KT = sb(us, "KT", [128, skv], BF16)
                V = sb(us, "V", [128, nkb, 128], BF16)
                QT = sb(us, "QT", [128, S_S], BF16)
                SGb = [sb(us, f"SG{i}", [128, S_S], BF16) for i in range(2)]
                Wh = sb(us, "Wh", [128, 8, 512], BF16)
                E = [sb(us, f"E{i}", [128, 2, 512], BF16) for i in range(4)]
                U0 = sb(us, "U0", [128, 1152], F32)
                BB = sb(us, "BB", [128, 2, 512], F32)
                o_all = sb(us, "o_all", [128, S_S], F32)
                GA = [sb(us, f"GA{i}", [128, 512], BF16) for i in range(2)]
                u0p = list(U0[:].ap[0])

                def uwin(w):
                    return bass.AP(U0[:].tensor, U0[:, w:w + 512].offset, [u0p, [0, 2], [1, 512]])

                def bwin(j):
                    a = BB[:, j, :]
                    return bass.AP(a.tensor, a.offset, [list(a.ap[0]), [0, 2], [1, 512]])

                def stage2(h, qt):
                    SG = SGb[h % 2]
                    osl = o_all[:, qt * 512:(qt + 1) * 512]
                    t0, t1, t2 = tmp(), tmp(), tmp()
                    S.op("dve", lambda e: e.tensor_tensor(out=TT[:, t0, :], in0=osl, in1=osl, op=ALU.mult),
                         reads=[("o", qt)], writes=[("T", t0)])
                    b = bank()
                    S.op("pe", lambda e: e.matmul(ps[:, b, :], lhsT=onesdiv[:], rhs=TT[:, t0, :], start=True, stop=True),
                         reads=[("T", t0)], writes=[("ps", b)])
                    S.op("act", lambda e: e.activation(out=TT[:, t1, :], in_=ps[:, b, :], func=AF.Ln, bias=epst[:, 0:1]),
                         reads=[("ps", b)], writes=[("T", t1)])
                    S.op("act", lambda e: e.activation(out=TT[:, t1, :], in_=TT[:, t1, :], func=AF.Exp, scale=-0.5),
                         reads=[("T", t1)], writes=[("T", t1)])
                    S.op("dve", lambda e: e.scalar_tensor_tensor(
                        out=TT[:, t2, :], in0=osl, scalar=subgs[:, 0:1], in1=TT[:, t1, :], op0=ALU.mult, op1=ALU.mult),
                        reads=[("o", qt), ("T", t1)], writes=[("T", t2)])
                    gs = nxt("GA", 2)
                    S.op("dve", lambda e: e.tensor_tensor(out=GA[gs][:], in0=TT[:, t2, :],
                                                          in1=SG[:, qt * 512:(qt + 1) * 512], op=ALU.mult),
                         reads=[("T", t2), ("SG", h % 2, qt)], writes=[("GA", gs)])
                    S.dma("pool", lambda e: e.dma_start(
                        out=gav[qt][:, h * 512:(h + 1) * 512], in_=GA[gs][:]),
                        reads=[("GA", gs)], writes=[("st_ga", gs)], group=f"ga{seg}")

                for h in range(NH):
                    SG = SGb[h % 2]
                    S.dma("sp", lambda e, h=h: e.dma_start(out=Wh[:].rearrange("p k n -> p (k n)"), in_=wAb_d.ap()[h]),
                          reads=["@wbf"], writes=["Wh"])
                    S.dma("sp", lambda e, h=h: e.dma_start(out=U0[:], in_=bass.AP(gp_d, h * 128 * GL + 127,
                                                                                  [[GL - 1, 128], [1, 1152]])),
                          reads=["@gp"], writes=["U0"])
                    if prompt:
                        S.op("dve", lambda e, h=h: e.tensor_scalar(out=BB[:, 0, :], in0=U0[:, 0:512], scalar1=cbf8[:, 1, h:h + 1],
                                                                   scalar2=sel[:, 6:7], op0=ALU.subtract, op1=ALU.mult),
                             reads=["U0"], writes=["BB0"])
                        S.op("dve", lambda e, h=h: e.tensor_scalar(out=BB[:, 0, :], in0=BB[:, 0, :], scalar1=CBt8[:, 2, h:h + 1],
                                                                   scalar2=None, op0=ALU.add),
                             reads=["BB0"], writes=["BB0"])
                        S.op("dve", lambda e, h=h: e.tensor_scalar(out=BB[:, 1, :], in0=U0[:, 640:1152], scalar1=cbf8[:, 0, h:h + 1],
                                                                   scalar2=sel[:, 7:8], op0=ALU.subtract, op1=ALU.mult),
                             reads=["U0"], writes=["BB1"])
                        S.op("dve", lambda e, h=h: e.tensor_scalar(out=BB[:, 1, :], in0=BB[:, 1, :], scalar1=CBt8[:, 4, h:h + 1],
                                                                   scalar2=None, op0=ALU.add),
                             reads=["BB1"], writes=["BB1"])
                    for t in range(skv // 512):
                        slot = nxt("hTt", 2)
                        S.dma("sp", lambda e, slot=slot, t=t: e.dma_start(out=hTt[slot][:].rearrange("p k n -> p (k n)"), in_=hTv[t]),
                              reads=[f"@hT{seg}"], writes=[("hTt", slot, 0), ("hTt", slot, 1)])
                        hk = [("hTt", slot, 0), ("hTt", slot, 1)]
                        b = bank()
                        for kc in range(8):
                            S.op("pe", lambda e, b=b, kc=kc, slot=slot: e.matmul(
                                ps[:, b, :], lhsT=Wh[:, kc, 128:256], rhs=hTt[slot][:, kc, :], start=(kc == 0), stop=(kc == 7)),
                                reads=["Wh"] + hk, writes=[("ps", b)])
                        S.op("dve", lambda e, b=b, t=t: e.tensor_copy(out=KT[:, t * 512:(t + 1) * 512], in_=ps[:, b, :]),
                             reads=[("ps", b)], writes=[("KT", t)])
                        b = bank()
                        for sub in range(4):
                            for kc in range(8):
                                S.op("pe", lambda e, b=b, kc=kc, sub=sub, slot=slot: e.matmul(
                                    ps[:, b, sub * 128:(sub + 1) * 128], lhsT=hTt[slot][:, kc, sub * 128:(sub + 1) * 128],
                                    rhs=Wh[:, kc, 256:384], start=(kc == 0), stop=(kc == 7)),
                                    reads=["Wh"] + hk, writes=[("ps", b)])
                        S.op("act", lambda e, b=b, t=t: e.activation(
                            out=V[:, 4 * t:4 * t + 4, :].rearrange("p a b -> p (a b)"), in_=ps[:, b, :], func=AF.Copy),
                            reads=[("ps", b)], writes=[("V", t)])
                        if t < 8:
                            b = bank()
                            for kc in range(8):
                                S.op("pe", lambda e, b=b, kc=kc, slot=slot: e.matmul(
                                    ps[:, b, :], lhsT=Wh[:, kc, 0:128], rhs=hTt[slot][:, kc, :], start=(kc == 0), stop=(kc == 7)),
                                    reads=["Wh"] + hk, writes=[("ps", b)])
                            S.op("dve", lambda e, b=b, t=t: e.tensor_copy(out=QT[:, t * 512:(t + 1) * 512], in_=ps[:, b, :]),
                                 reads=[("ps", b)], writes=[("QT", t)])
                            b = bank()
                            for kc in range(8):
                                S.op("pe", lambda e, b=b, kc=kc, slot=slot: e.matmul(
                                    ps[:, b, :], lhsT=Wh[:, kc, 384:512], rhs=hTt[slot][:, kc, :], start=(kc == 0), stop=(kc == 7)),
                                    reads=["Wh"] + hk, writes=[("ps", b)])
                            ti = tmp()
                            S.op("act", lambda e, b=b, ti=ti: e.activation(out=TT[:, ti, :], in_=ps[:, b, :], func=AF.Tanh, scale=0.5),
                                 reads=[("ps", b)], writes=[("T", ti)])
                            S.op("dve", lambda e, b=b, ti=ti, t=t, SG=SG: e.scalar_tensor_tensor(
                                out=SG[:, t * 512:(t + 1) * 512], in0=TT[:, ti, :], scalar=1.0, in1=ps[:, b, :],
                                op0=ALU.add, op1=ALU.mult), reads=[("ps", b), ("T", ti)], writes=[("SG", h % 2, t)])
                            if h > 0:
                                stage2(h - 1, t)
                    steps = [(qt, kb) for qt in range(8) for kb in range(nkb)]
                    nst = len(steps)
                    pending = []

                    def emit_scores(i):
                        qt, kb = steps[i]
                        b0 = 2 * (i % 2)
                        rk = [("KT", kb // 4), ("QT", qt)]
                        S.op("pe", lambda e: e.matmul(ps[:, b0, :], lhsT=KT[0:64, kb * 128:(kb + 1) * 128],
                                                      rhs=QT[0:64, qt * 512:(qt + 1) * 512], start=True, stop=True),
                             reads=rk, writes=[("ps", b0)])
                        S.op("pe", lambda e: e.matmul(ps[:, b0 + 1, :], lhsT=KT[64:128, kb * 128:(kb + 1) * 128],
                                                      rhs=QT[64:128, qt * 512:(qt + 1) * 512], start=True, stop=True),
                             reads=rk, writes=[("ps", b0 + 1)])
                        chunk, kbl = kb // 32, kb % 32
                        dl = kbl - 4 * qt
                        win = None
                        if chunk == 0 and -1 <= dl <= 4:
                            win, rd = uwin(512 - 128 * dl), ["U0"]
                        elif prompt and chunk == 1 and qt == 7 and kbl == 0:
                            win, rd = bwin(0), ["BB0"]
                        elif prompt and chunk == 3 and qt == 0 and kbl == 31:
                            win, rd = bwin(1), ["BB1"]
                        if win is not None:
                            S.op("dve", lambda e: e.tensor_tensor(out=ps[:, b0:b0 + 2, :], in0=ps[:, b0:b0 + 2, :], in1=win, op=ALU.add),
                                 reads=rd + [("ps", b0), ("ps", b0 + 1)], writes=[("ps", b0), ("ps", b0 + 1)])
                            bias = 0.0
                        else:
                            if chunk == 0:
                                ci = 0 if dl < -1 else 1
                            else:
                                ci = 1 + chunk
                            bias = CBt[:, ci, h:h + 1]
                        ei = i % 4
                        S.op("act", lambda e: e.activation(out=E[ei][:], in_=ps[:, b0:b0 + 2, :], func=AF.Exp, bias=bias, scale=SCALE),
                             reads=[("ps", b0), ("ps", b0 + 1)], writes=[("E", ei)])

                    def emit_pv(i):
                        qt, kb = steps[i]
                        ei = i % 4
                        first, last = kb == 0, kb == nkb - 1
                        for m in range(2):
                            S.op("pe", lambda e, m=m: e.matmul(ps[:, 4 + m, :], lhsT=V[:, kb, :], rhs=E[ei][:, m, :],
                                                               start=first, stop=last),
                                 reads=[("V", kb // 4), ("E", ei)], writes=[("ps", 4 + m)])
                        for m in range(2):
                            S.op("pe", lambda e, m=m: e.matmul(ps[:, 6 + m, :], lhsT=ones_b[:], rhs=E[ei][:, m, :],
                                                               start=first, stop=last),
                                 reads=[("E", ei)], writes=[("ps", 6 + m)])
                        if last:
                            t2, t3, t0, t1 = tmp(), tmp(), tmp(), tmp()
                            S.op("act", lambda e: e.activation(out=TT[:, t2, :], in_=ps[:, 4, :], func=AF.Copy), reads=[("ps", 4)], writes=[("T", t2)])
                            S.op("act", lambda e: e.activation(out=TT[:, t3, :], in_=ps[:, 5, :], func=AF.Copy), reads=[("ps", 5)], writes=[("T", t3)])
                            S.op("dve", lambda e: e.tensor_copy(out=TT[:, t0, :], in_=ps[:, 6, :]), reads=[("ps", 6)], writes=[("T", t0)])
                            S.op("dve", lambda e: e.tensor_copy(out=TT[:, t1, :], in_=ps[:, 7, :]), reads=[("ps", 7)], writes=[("T", t1)])
                            S.op("dve", lambda e: e.reciprocal(out=TT[:, t0, :], in_=TT[:, t0, :]), reads=[("T", t0)], writes=[("T", t0)])
                            S.op("dve", lambda e: e.reciprocal(out=TT[:, t1, :], in_=TT[:, t1, :]), reads=[("T", t1)], writes=[("T", t1)])
                            S.op("dve", lambda e: e.tensor_tensor(out=TT[:, t2, :], in0=TT[:, t2, :], in1=TT[:, t0, :], op=ALU.mult),
                                 reads=[("T", t2), ("T", t0)], writes=[("T", t2)])
                            S.op("dve", lambda e: e.tensor_tensor(out=TT[:, t3, :], in0=TT[:, t3, :], in1=TT[:, t1, :], op=ALU.mult),
                                 reads=[("T", t3), ("T", t1)], writes=[("T", t3)])
                            S.op("dve", lambda e: e.scalar_tensor_tensor(out=o_all[:, qt * 512:(qt + 1) * 512], in0=TT[:, t3, :],
                                                                         scalar=neglam, in1=TT[:, t2, :], op0=ALU.mult, op1=ALU.add),
                                 reads=[("T", t2), ("T", t3)], writes=[("o", qt)])

                    emit_scores(0)
                    for i in range(nst):
                        if i + 1 < nst:
                            emit_scores(i + 1)
                        emit_pv(i)
                        while pending and pending[0][0] <= i:
                            pending.pop(0)[1]()
                    while pending:
                        pending.pop(0)[1]()
                for qt in range(8):
                    stage2(NH - 1, qt)
                S.flush()

        def phase_c(seg, x_ap, y_ap):
            hTv = hT_d[seg].ap()
            gav = ga_d[seg].ap()
            with contextlib.ExitStack() as cs:
                WC = sb(cs, "WC", [128, 8, 5120], BF16)
                WP = [sb(cs, f"WP{i}", [128, 8, 1024], BF16) for i in range(3)]
                GAin = sb(cs, "GAin", [128, 8, 512], BF16)
                gbT = sb(cs, "gbT", [128, 8, 512], BF16)
                vm = sb(cs, "vm", [128, 4, 1024], BF16)
                mT = vm[:].rearrange("p a (b c) -> p (a b) c", b=2)
                for kc in range(8):
                    S.dma("sp", lambda e, kc=kc: e.dma_start(
                        out=WC[:, kc, :], in_=wCb_d.ap().rearrange("p (kc n) -> p kc n", kc=8)[:, kc, :]),
                        reads=["@wbf"], writes=[("WC", kc)])
                for m in range(3):
                    S.dma("sp", lambda e, m=m: e.dma_start(out=WP[m][:].rearrange("p k n -> p (k n)"), in_=wPb_d.ap()[m]),
                          reads=["@wbf"], writes=[("WP", m)])
                wck = [("WC", kc) for kc in range(8)]
                for t in range(S_S // 512):
                    slot = nxt("hTt", 2)
                    S.dma("sp", lambda e, slot=slot, t=t: e.dma_start(out=hTt[slot][:].rearrange("p k n -> p (k n)"), in_=hTv[t]),
                          reads=[f"@hT{seg}"], writes=[("hTt", slot, 0), ("hTt", slot, 1)])
                    hk = [("hTt", slot, 0), ("hTt", slot, 1)]
                    S.dma("sp", lambda e, t=t: e.dma_start(out=GAin[:].rearrange("p k n -> p (k n)"), in_=gav[t]),
                          reads=[f"@ga{seg}"], writes=["GAin"])
                    for sub in range(4):
                        b = bank2()
                        for half in range(2):
                            for kc in range(8):
                                S.op("pe", lambda e, b=b, half=half, kc=kc, sub=sub, slot=slot: e.matmul(
                                    ps[:, b + half, :], lhsT=hTt[slot][:, kc, sub * 128:(sub + 1) * 128],
                                    rhs=WC[:, kc, 1024 + half * 512:1024 + (half + 1) * 512], start=(kc == 0), stop=(kc == 7)),
                                    reads=wck + hk, writes=[("ps", b + half)])
                        pk = [("ps", b), ("ps", b + 1)]
                        c1, c2 = stcol(), stcol()
                        S.op("act", lambda e, b=b, c1=c1: e.activation(out=junk[:].rearrange("p (a b) -> p a b", a=2), in_=ps[:, b:b + 2, :],
                                                                       func=AF.Identity, accum_out=stat[:, c1:c1 + 1]),
                             reads=pk, writes=["junk", ("st", c1)])
                        S.op("act", lambda e, b=b, c2=c2: e.activation(out=junk[:].rearrange("p (a b) -> p a b", a=2), in_=ps[:, b:b + 2, :],
                                                                       func=AF.Square, accum_out=stat[:, c2:c2 + 1]),
                             reads=pk, writes=["junk", ("st", c2)])
                        c3, c4, c5, c6 = stcol(), stcol(), stcol(), stcol()
                        S.op("dve", lambda e, c1=c1, c3=c3: e.tensor_scalar(out=stat[:, c3:c3 + 1], in0=stat[:, c1:c1 + 1], scalar1=1.0 / D,
                                                                            scalar2=None, op0=ALU.mult), reads=[("st", c1)], writes=[("st", c3)])
                        S.op("dve", lambda e, c3=c3, c4=c4: e.scalar_tensor_tensor(out=stat[:, c4:c4 + 1], in0=stat[:, c3:c3 + 1], scalar=-1.0,
                                                                                   in1=stat[:, c3:c3 + 1], op0=ALU.mult, op1=ALU.mult),
                             reads=[("st", c3)], writes=[("st", c4)])
                        S.op("dve", lambda e, c2=c2, c4=c4, c5=c5: e.scalar_tensor_tensor(out=stat[:, c5:c5 + 1], in0=stat[:, c2:c2 + 1],
                                                                                         scalar=1.0 / D, in1=stat[:, c4:c4 + 1],
                                                                                         op0=ALU.mult, op1=ALU.add),
                             reads=[("st", c2), ("st", c4)], writes=[("st", c5)])
                        S.op("act", lambda e, c5=c5: e.activation(out=stat[:, c5:c5 + 1], in_=stat[:, c5:c5 + 1], func=AF.Ln,
                                                                  bias=epst[:, 0:1]),
                             reads=[("st", c5)], writes=[("st", c5)])
                        S.op("act", lambda e, c5=c5, c6=c6: e.activation(out=stat[:, c6:c6 + 1], in_=stat[:, c5:c5 + 1], func=AF.Exp,
                                                                         scale=-0.5),
                             reads=[("st", c5)], writes=[("st", c6)])
                        c7 = stcol()
                        S.op("dve", lambda e, c3=c3, c6=c6, c7=c7: e.scalar_tensor_tensor(out=stat[:, c7:c7 + 1], in0=stat[:, c3:c3 + 1],
                                                                                         scalar=-1.0, in1=stat[:, c6:c6 + 1],
                                                                                         op0=ALU.mult, op1=ALU.mult),
                             reads=[("st", c3), ("st", c6)], writes=[("st", c7)])
                        S.op("act", lambda e, b=b, sub=sub, c6=c6, c7=c7: e.activation(
                            out=vm[:, sub, :].rearrange("p (a b) -> p a b", a=2), in_=ps[:, b:b + 2, :], func=AF.Identity,
                            bias=stat[:, c7:c7 + 1], scale=stat[:, c6:c6 + 1]),
                            reads=pk + [("st", c6), ("st", c7)], writes=[("vm", sub)])
                    for g in range(8):
                        bs_ = bank()
                        for sub in range(4):
                            S.op("pe", lambda e, bs_=bs_, sub=sub, g=g: e.matmul(
                                ps[:, bs_, sub * 128:(sub + 1) * 128], lhsT=vm[:, sub, g * 128:(g + 1) * 128], rhs=wsT_b[:, g, :],
                                start=True, stop=True), reads=[("vm", sub), "wsT_b"], writes=[("ps", bs_)])
                        bu = bank()
                        for kc in range(8):
                            S.op("pe", lambda e, bu=bu, kc=kc, g=g, slot=slot: e.matmul(
                                ps[:, bu, :], lhsT=WC[:, kc, g * 128:(g + 1) * 128], rhs=hTt[slot][:, kc, :], start=(kc == 0), stop=(kc == 7)),
                                reads=wck + hk, writes=[("ps", bu)])
                        bg = bank()
                        for kc in range(8):
                            S.op("pe", lambda e, bg=bg, kc=kc, g=g, slot=slot: e.matmul(
                                ps[:, bg, :], lhsT=WC[:, kc, 2048 + g * 128:2048 + (g + 1) * 128], rhs=hTt[slot][:, kc, :],
                                start=(kc == 0), stop=(kc == 7)), reads=wck + hk, writes=[("ps", bg)])
                        ta, tb, tc = tmp(), tmp(), tmp()
                        cga = Cg[:, g, :]
                        cgw = bass.AP(cga.tensor, cga.offset, [list(cga.ap[0]), [0, 4], [1, 128]])
                        S.op("dve", lambda e, bs_=bs_, ta=ta, g=g, cgw=cgw: e.scalar_tensor_tensor(
                            out=TT[:, ta, :].rearrange("p (a b) -> p a b", a=4), in0=ps[:, bs_, :].rearrange("p (a b) -> p a b", a=4),
                            scalar=lng[:, g:g + 1], in1=cgw, op0=ALU.mult, op1=ALU.add),
                            reads=[("ps", bs_), "Cg", "lng"], writes=[("T", ta)])
                        S.op("act", lambda e, bg=bg, tb=tb: e.activation(out=TT[:, tb, :], in_=ps[:, bg, :], func=AF.Tanh, scale=0.5),
                             reads=[("ps", bg)], writes=[("T", tb)])
                        S.op("dve", lambda e, bg=bg, tb=tb: e.scalar_tensor_tensor(out=TT[:, tb, :], in0=TT[:, tb, :], scalar=1.0, in1=ps[:, bg, :],
                                                                                   op0=ALU.add, op1=ALU.mult),
                             reads=[("ps", bg), ("T", tb)], writes=[("T", tb)])
                        S.op("dve", lambda e, bu=bu, ta=ta, tc=tc: e.tensor_tensor(out=TT[:, tc, :], in0=ps[:, bu, :], in1=TT[:, ta, :], op=ALU.mult),
                             reads=[("ps", bu), ("T", ta)], writes=[("T", tc)])
                        S.op("dve", lambda e, tb=tb, tc=tc, g=g: e.scalar_tensor_tensor(out=gbT[:, g, :], in0=TT[:, tc, :], scalar=0.5, in1=TT[:, tb, :],
                                                                                        op0=ALU.mult, op1=ALU.mult),
                             reads=[("T", tb), ("T", tc)], writes=[("gbT", g)])
                    gbk = [("gbT", g) for g in range(8)]
                    vmk = [("vm", s_) for s_ in range(4)]
                    for n in range(8):
                        bma, bya, bmb, byb = bank(), bank(), bank(), bank()
                        for kc in range(8):
                            S.op("pe", lambda e, kc=kc, n=n, bma=bma, slot=slot: e.matmul(
                                ps[:, bma, :], lhsT=WC[:, kc, 3072 + n * 128:3072 + (n + 1) * 128], rhs=hTt[slot][:, kc, :],
                                start=(kc == 0), stop=(kc == 7)), reads=wck + hk, writes=[("ps", bma)])
                        for kc in range(8):
                            S.op("pe", lambda e, kc=kc, n=n, bya=bya: e.matmul(
                                ps[:, bya, :], lhsT=WP[0][:, kc, n * 128:(n + 1) * 128], rhs=GAin[:, kc, :],
                                start=(kc == 0), stop=(kc == 7)), reads=[("WP", 0), "GAin"], writes=[("ps", bya)])
                        for kc in range(8):
                            S.op("pe", lambda e, kc=kc, n=n, bmb=bmb, slot=slot: e.matmul(
                                ps[:, bmb, :], lhsT=WC[:, kc, 4096 + n * 128:4096 + (n + 1) * 128], rhs=hTt[slot][:, kc, :],
                                start=(kc == 0), stop=(kc == 7)), reads=wck + hk, writes=[("ps", bmb)])
                        for kc in range(8):
                            S.op("pe", lambda e, kc=kc, n=n, byb=byb: e.matmul(
                                ps[:, byb, :], lhsT=WP[1][:, kc, n * 128:(n + 1) * 128], rhs=gbT[:, kc, :],
                                start=(kc == 0), stop=(kc == 7)), reads=[("WP", 1)] + gbk, writes=[("ps", byb)])
                        ta, tb = tmp(), tmp()
                        for (bm, by, tx) in ((bma, bya, ta), (bmb, byb, tb)):
                            S.op("act", lambda e, bm=bm, tx=tx: e.activation(out=TT[:, tx, :], in_=ps[:, bm, :], func=AF.Tanh, scale=0.5),
                                 reads=[("ps", bm)], writes=[("T", tx)])
                            S.op("dve", lambda e, by=by, tx=tx: e.scalar_tensor_tensor(out=TT[:, tx, :], in0=TT[:, tx, :], scalar=1.0, in1=ps[:, by, :],
                                                                                       op0=ALU.add, op1=ALU.mult),
                                 reads=[("ps", by), ("T", tx)], writes=[("T", tx)])
                        S.op("dve", lambda e, ta=ta, tb=tb, n=n: e.tensor_tensor(out=mT[:, n, :], in0=TT[:, ta, :], in1=TT[:, tb, :], op=ALU.add),
                             reads=[("T", ta), ("T", tb)], writes=[("mT", n)] + vmk)
                    mk_ = [("mT", n) for n in range(8)]
                    for sub in range(4):
                        xs_ = nxt("xin", 2)
                        r0 = t * 512 + sub * 128
                        S.dma("sp", lambda e, xs_=xs_, r0=r0: e.dma_start(out=xin[xs_][:], in_=x_ap[r0:r0 + 128, :]), writes=[("xin", xs_)])
                        b = bank2()
                        for half in range(2):
                            for kc in range(8):
                                S.op("pe", lambda e, b=b, half=half, kc=kc, sub=sub: e.matmul(
                                    ps[:, b + half, :], lhsT=mT[:, kc, sub * 128:(sub + 1) * 128], rhs=WP[2][:, kc, half * 512:(half + 1) * 512],
                                    start=(kc == 0), stop=(kc == 7)), reads=[("WP", 2)] + mk_ + vmk, writes=[("ps", b + half)])
                        pk = [("ps", b), ("ps", b + 1)]
                        c1, c2, c3 = stcol(), stcol(), stcol()
                        S.op("act", lambda e, b=b, c1=c1: e.activation(out=junk[:].rearrange("p (a b) -> p a b", a=2), in_=ps[:, b:b + 2, :],
                                                                       func=AF.Square, accum_out=stat[:, c1:c1 + 1]),
                             reads=pk, writes=["junk", ("st", c1)])
                        S.op("act", lambda e, c1=c1, c2=c2: e.activation(out=stat[:, c2:c2 + 1], in_=stat[:, c1:c1 + 1], func=AF.Ln,
                                                                         bias=epst[:, 1:2], scale=1.0 / D),
                             reads=[("st", c1)], writes=[("st", c2)])
                        S.op("act", lambda e, c2=c2, c3=c3: e.activation(out=stat[:, c3:c3 + 1], in_=stat[:, c2:c2 + 1], func=AF.Exp,
                                                                         scale=-0.5),
                             reads=[("st", c2)], writes=[("st", c3)])
                        if cnt["T"] % 2:
                            cnt["T"] += 1
                        ta = tmp()
                        tmp()
                        S.op("dve", lambda e, b=b, c3=c3, ta=ta: e.scalar_tensor_tensor(
                            out=TT[:, ta:ta + 2, :], in0=ps[:, b:b + 2, :], scalar=stat[:, c3:c3 + 1],
                            in1=gpost_b[:].rearrange("p (a b) -> p a b", a=2), op0=ALU.mult, op1=ALU.mult),
                            reads=pk + [("st", c3), "gpost"], writes=[("T", ta), ("T", ta + 1)])
                        S.op("dve", lambda e, xs_=xs_, ta=ta: e.tensor_tensor(out=xin[xs_][:].rearrange("p (a b) -> p a b", a=2),
                                                                              in0=TT[:, ta:ta + 2, :],
                                                                              in1=xin[xs_][:].rearrange("p (a b) -> p a b", a=2), op=ALU.add),
                             reads=[("T", ta), ("T", ta + 1), ("xin", xs_)], writes=[("xin", xs_)])
                        S.dma("pool", lambda e, xs_=xs_, r0=r0: e.dma_start(out=y_ap[r0:r0 + 128, :], in_=xin[xs_][:]),
                              reads=[("xin", xs_)], writes=[("st_out", xs_)], group="out")
                S.flush()

        _orig_deps = S._deps

        def _deps(reads, writes):
            real = [r for r in reads if not (isinstance(r, str) and r.startswith("@"))]
            deps = _orig_deps(real, writes)
            for r in reads:
                if isinstance(r, str) and r.startswith("@"):
                    for key in S.groups[r[1:]]:
                        dd = S.dsem[key]
                        deps.append(Tok("dma", dd[0], dd[1]))
            return deps

        _orig_commit = S._commit

        def _commit(tok, reads, writes):
            real = [r for r in reads if not (isinstance(r, str) and r.startswith("@"))]
            _orig_commit(tok, real, writes)

        S._deps, S._commit = _deps, _commit

        setup()
        segs = [(0, xs_d.ap()[0], S_S, False, ys_d.ap()[0]), (1, xs_d.ap()[1], S_S, False, ys_d.ap()[1]),
                (2, xp_d.ap(), S_P, True, yp_d.ap())]
        for seg, x_ap, skv, prompt, y_ap in segs:
            phase_a(seg, x_ap, skv)
            units(seg, skv, prompt)
            phase_c(seg, x_ap, y_ap)
    return nc


def _bucket(rel):
    half, max_exact = 16, 8
    n = np.abs(rel)
    nf = np.maximum(n, 1).astype(np.float32)
    large = max_exact + (np.log(nf / np.float32(max_exact)) / np.float32(math.log(128 / max_exact))
                         * np.float32(half - max_exact)).astype(np.int32)
    large = np.minimum(large, half - 1)
    return np.where(rel > 0, half, 0) + np.where(n < max_exact, n, large)


_CACHE = {}


def kernel(x_prompt, x_sample, g_pre, w_in, lambda_q1, lambda_k1, lambda_q2, lambda_k2, subln_g,
           w_pa, ln_g, ln_b, w_s, b_s, w_pb, w_o, g_post, rel_bias):
    f = lambda a: np.ascontiguousarray(np.asarray(a, dtype=np.float32))
    x_prompt, x_sample, w_in = f(x_prompt), f(x_sample), f(w_in)[0]
    rep = lambda v, n=128: np.ascontiguousarray(np.broadcast_to(f(v).reshape(1, -1), (n, f(v).size)))
    wA = np.empty((NH, D, 512), np.float32)
    for h in range(NH):
        wA[h, :, 0:64] = w_in[:, h * 64:(h + 1) * 64]
        wA[h, :, 64:128] = w_in[:, 512 + h * 64:512 + (h + 1) * 64]
        wA[h, :, 128:192] = w_in[:, 1024 + h * 64:1024 + (h + 1) * 64]
        wA[h, :, 192:256] = w_in[:, 1536 + h * 64:1536 + (h + 1) * 64]
        wA[h, :, 256:384] = w_in[:, 2048 + h * 128:2048 + (h + 1) * 128]
        wA[h, :, 384:512] = w_in[:, 3072 + h * 128:3072 + (h + 1) * 128]
    wC = np.ascontiguousarray(w_in[:, 4096:9216])
    wP = np.stack([f(w_pa)[0], f(w_pb)[0], f(w_o)[0]])
    lamv = np.concatenate([rep(lambda_q1), rep(lambda_k1), rep(lambda_q2), rep(lambda_k2)], axis=1)
    subg = f(subln_g).reshape(128, 1)
    lng = np.ascontiguousarray(f(ln_g).reshape(8, 128).T)
    wsT = np.ascontiguousarray(f(w_s)[0].transpose(2, 0, 1).reshape(128, 8 * 128))
    bsrow = f(b_s).reshape(1, D)
    relb = f(rel_bias)
    cbf = np.concatenate([rep(relb[15]), rep(relb[31])], axis=1)
    j = np.arange(GL)
    oh = (np.arange(32)[:, None] == _bucket(639 - j)[None, :]).astype(np.float32)
    ident = np.eye(128, dtype=np.float32)
    common = dict(wA=wA, wC=wC, wP=wP, gpre_b=rep(g_pre), gpost_b=rep(g_post), lamv=lamv, subg=subg, lng=lng,
                  lnb_rows=rep(ln_b), wsT=wsT, bsrow=bsrow, relb=relb, cbf=cbf, oh=oh, ident=ident)
    in_maps = []
    for c in range(8):
        pb, pq = c // 4, c % 4
        xp = np.ascontiguousarray(np.roll(x_prompt[pb], -pq * S_S, axis=0))
        selb = [1.0 if pq == 3 else 0.0, 1.0 if pq >= 2 else 0.0, 1.0 if pq >= 1 else 0.0]
        selv = selb + [1.0 - s for s in selb] + [1.0 if pq <= 2 else 0.0, 1.0 if pq >= 1 else 0.0]
        sel = np.ascontiguousarray(np.broadcast_to(np.array(selv, np.float32)[None, :], (128, 8)))
        m = dict(common)
        m.update(xs=np.ascontiguousarray(x_sample[2 * c:2 * c + 2]), xp=xp, sel=sel)
        in_maps.append(m)
    if "nc" not in _CACHE:
        _CACHE["nc"] = build_program()
    res = run_bass_kernel_spmd(_CACHE["nc"], in_maps, core_ids=list(range(8)))
    y_prompt = np.empty((2, S_P, D), np.float32)
    y_sample = np.empty((16, S_S, D), np.float32)
    for c in range(8):
        r = res.results[c]
        pb, pq = c // 4, c % 4
        y_prompt[pb, pq * S_S:(pq + 1) * S_S] = r["yp"]
        y_sample[2 * c:2 * c + 2] = r["ys"]
    return (y_prompt, y_sample)
```
